# Optimizing a Trainium2 kernel written in Bass

```python
import math
import jax, jax.numpy as jnp
from jax import lax
import numpy as np

D_MODEL = 2048
BATCH = 8
SEQ = 2048
DEPTH = 1

CTX_LEN = 256
GRID_W = 64
MIX_W = D_MODEL
HY_W = MIX_W // 2
HY_HEADS = 8
S5_W = MIX_W - HY_W
S5_GROUP = 16
S5_GROUPS = S5_W // S5_GROUP
S5_STATE = 64
HY_ORDER = 2
HY_BANDS = 16
HY_EMB = 1 + 2 * HY_BANDS
HY_FFN = 64
HY_DECAY_TARGET = 1e-2
HY_DECAY_FAST = 0.3
HY_DECAY_SLOW = 1.5
SHORT_K = 3
D_FF = -(-8 * D_MODEL // (3 * 256)) * 256
EPS = 1e-6

kernel_name = 'hymba_style_hyena_s5_dit_block'


def rmsnorm(x, g):
    xf = x.astype(jnp.float32)
    y = xf * lax.rsqrt(jnp.mean(xf * xf, axis=-1, keepdims=True) + EPS)
    return (y * g.astype(jnp.float32)).astype(x.dtype)


def modulate(h, shift, scale):
    return h * (1.0 + scale) + shift


def short_conv(u, w, b):
    n = u.shape[-2]
    up = jnp.pad(u, [(0, 0)] * (u.ndim - 2) + [(1, 1), (0, 0)])
    return up[..., :n, :] * w[0] + up[..., 1:n + 1, :] * w[1] + up[..., 2:, :] * w[2] + b


def hyena_filters(n, w1, b1, w2, b2, w3, sin_freq, decay):
    pos = jnp.arange(n, dtype=jnp.float32)
    t = pos[:, None] / n
    bands = jnp.linspace(1e-4, HY_BANDS - 1, HY_BANDS, dtype=jnp.float32)
    ang = 2.0 * math.pi * pos[:, None] * bands[None, :] / n
    z = jnp.concatenate([t, jnp.cos(ang), -jnp.sin(ang)], axis=-1)
    h = jnp.sin(sin_freq[0] * (z @ w1 + b1))
    h = jnp.sin(sin_freq[1] * (h @ w2 + b2))
    h = (h @ w3).astype(jnp.float32).reshape(n, HY_ORDER, 2, HY_W)
    h = h * jnp.exp(-t[:, :, None, None] * jnp.abs(decay.astype(jnp.float32)))
    fwd, bwd = h[:, :, 0], h[:, :, 1]
    two = jnp.concatenate([fwd, jnp.zeros((1, HY_ORDER, HY_W), jnp.float32), bwd[1:][::-1]], axis=0)
    return two / (jnp.sum(jnp.abs(two), axis=0, keepdims=True) + EPS)


def fftconv(u, h):
    n = u.shape[1]
    uf = jnp.fft.rfft(u.astype(jnp.float32), n=2 * n, axis=1)
    hf = jnp.fft.rfft(h, n=2 * n, axis=0)
    y = jnp.fft.irfft(uf * hf[None], n=2 * n, axis=1)[:, :n]
    return y.astype(u.dtype)


def hyena_mixer(hy, filt, conv_w, conv_b, bias, rows):
    bsz, n, ch = hy.shape
    if rows is None:
        hs = short_conv(hy, conv_w, conv_b)
    else:
        hs = short_conv(hy.reshape(bsz, rows, GRID_W, ch), conv_w, conv_b).reshape(bsz, n, ch)
    v, x1, x2 = jnp.split(hs, 3, axis=-1)
    z = v
    for o, gate in enumerate((x1, x2)):
        z = gate * (fftconv(z, filt[:, o]) + bias[o] * z)
    return z


def _lin_rec(e1, e2):
    a1, b1 = e1
    a2, b2 = e2
    return a1 * a2, a2 * b1 + b2


def s5_direction(ug, lam_re, lam_im, log_dt, b_re, b_im, x0, reverse):
    lam = lax.complex(lam_re.astype(jnp.float32), lam_im.astype(jnp.float32))
    lam_dt = lam * jnp.exp(log_dt.astype(jnp.float32))[:, None]
    a_bar = jnp.exp(lam_dt)
    b_bar = ((a_bar - 1.0) / lam)[..., None] * lax.complex(b_re.astype(jnp.float32), b_im.astype(jnp.float32))
    bu = jnp.einsum('gpc,blgc->blgp', b_bar, ug.astype(jnp.complex64))
    n = ug.shape[1]
    a = jnp.broadcast_to(a_bar, (1, n) + a_bar.shape)
    _, xs = lax.associative_scan(_lin_rec, (a, bu), axis=1, reverse=reverse)
    if x0 is not None:
        steps = jnp.arange(n, dtype=jnp.float32)
        k = (n - steps) if reverse else (steps + 1.0)
        xs = xs + jnp.exp(lam_dt[None] * k[:, None, None])[None] * x0[:, None]
    final = xs[:, 0] if reverse else xs[:, -1]
    return xs, final


def s5_readout(ug, xs_f, xs_b, c_re, c_im, d, glu_w, glu_b, out_dtype):
    cf = lax.complex(c_re[0].astype(jnp.float32), c_im[0].astype(jnp.float32))
    cb = lax.complex(c_re[1].astype(jnp.float32), c_im[1].astype(jnp.float32))
    y = (jnp.einsum('gcp,blgp->blgc', cf, xs_f).real
         + jnp.einsum('gcp,blgp->blgc', cb, xs_b).real
         + d.astype(jnp.float32) * ug)
    bsz, n = ug.shape[:2]
    y = jax.nn.gelu(y).reshape(bsz, n, S5_W).astype(out_dtype)
    gl = y @ glu_w + glu_b
    return gl[..., :S5_W] * jax.nn.sigmoid(gl[..., S5_W:])


def swiglu(h, wg, wu, wd):
    return (jax.nn.silu(h @ wg) * (h @ wu)) @ wd


def setup_inputs(seed: int = 0) -> dict:
    key = jax.random.key(seed)
    ks = iter(jax.random.split(key, 48))
    f32 = jnp.float32

    def nrm(shape, scale):
        return scale * jax.random.normal(next(ks), shape, f32)

    def gain(shape):
        return 1.0 + nrm(shape, 0.02)

    L, D, G, P = DEPTH, D_MODEL, S5_GROUPS, S5_STATE
    x = nrm((BATCH, SEQ, D), 1.0)
    c = nrm((BATCH, D), 1.0)
    ctx = nrm((BATCH, CTX_LEN, D), 1.0)
    c_ctx = nrm((D,), 1.0)
    ada_w = nrm((L, D, 6 * D), 0.5 * D ** -0.5)
    ada_b = nrm((L, 6 * D), 0.01)
    norm1_g = gain((L, D))
    w_in = nrm((L, D, S5_W + 3 * HY_W), D ** -0.5)
    conv_w = nrm((L, SHORT_K, 3 * HY_W), SHORT_K ** -0.5)
    conv_b = nrm((L, 3 * HY_W), 0.01)
    hy_w1 = nrm((L, HY_EMB, HY_FFN), HY_EMB ** -0.5)
    hy_b1 = nrm((L, HY_FFN), 0.02)
    hy_w2 = nrm((L, HY_FFN, HY_FFN), HY_FFN ** -0.5)
    hy_b2 = nrm((L, HY_FFN), 0.02)
    hy_w3 = nrm((L, HY_FFN, HY_ORDER * 2 * HY_W), HY_FFN ** -0.5)
    hy_sin_freq = 1.0 + nrm((L, 2, HY_FFN), 0.1)
    decay_base = jnp.abs(jnp.linspace(math.log(HY_DECAY_TARGET) / HY_DECAY_FAST,
                                      math.log(HY_DECAY_TARGET) / HY_DECAY_SLOW, HY_W, dtype=f32))
    hy_decay = decay_base * (1.0 + nrm((L, HY_ORDER, 2, HY_W), 0.05))
    hy_bias = nrm((L, HY_ORDER, HY_W), 0.5)
    s5_lam_re = -0.5 + nrm((L, 2, G, P), 0.01)
    s5_lam_im = jnp.broadcast_to(math.pi * jnp.arange(P, dtype=f32), (L, 2, G, P)) + nrm((L, 2, G, P), 0.001)
    s5_log_dt = jax.random.uniform(next(ks), (L, 2, G), f32, math.log(1e-3), math.log(1e-1))
    s5_b_re = nrm((L, 2, G, P, S5_GROUP), (2 * S5_GROUP) ** -0.5)
    s5_b_im = nrm((L, 2, G, P, S5_GROUP), (2 * S5_GROUP) ** -0.5)
    s5_c_re = nrm((L, 2, G, S5_GROUP, P), (2 * P) ** -0.5)
    s5_c_im = nrm((L, 2, G, S5_GROUP, P), (2 * P) ** -0.5)
    s5_d = nrm((L, G, S5_GROUP), 1.0)
    s5_glu_w = nrm((L, S5_W, 2 * S5_W), S5_W ** -0.5)
    s5_glu_b = nrm((L, 2 * S5_W), 0.01)
    branch_g_s5 = gain((L, S5_W))
    branch_g_hy = gain((L, HY_W))
    w_out = nrm((L, MIX_W, D), MIX_W ** -0.5)
    norm2_g = gain((L, D))
    ffn_w_gate = nrm((L, D, D_FF), D ** -0.5)
    ffn_w_up = nrm((L, D, D_FF), D ** -0.5)
    ffn_w_down = nrm((L, D_FF, D), D_FF ** -0.5)
    final_g = gain((D,))
    return {'x': x, 'c': c, 'ctx': ctx, 'c_ctx': c_ctx, 'ada_w': ada_w, 'ada_b': ada_b,
            'norm1_g': norm1_g, 'w_in': w_in, 'conv_w': conv_w, 'conv_b': conv_b,
            'hy_w1': hy_w1, 'hy_b1': hy_b1, 'hy_w2': hy_w2, 'hy_b2': hy_b2, 'hy_w3': hy_w3,
            'hy_sin_freq': hy_sin_freq, 'hy_decay': hy_decay, 'hy_bias': hy_bias,
            's5_lam_re': s5_lam_re, 's5_lam_im': s5_lam_im, 's5_log_dt': s5_log_dt,
            's5_b_re': s5_b_re, 's5_b_im': s5_b_im, 's5_c_re': s5_c_re, 's5_c_im': s5_c_im,
            's5_d': s5_d, 's5_glu_w': s5_glu_w, 's5_glu_b': s5_glu_b,
            'branch_g_s5': branch_g_s5, 'branch_g_hy': branch_g_hy, 'w_out': w_out,
            'norm2_g': norm2_g, 'ffn_w_gate': ffn_w_gate, 'ffn_w_up': ffn_w_up,
            'ffn_w_down': ffn_w_down, 'final_g': final_g}


def reference(x, c, ctx, c_ctx, ada_w, ada_b, norm1_g, w_in, conv_w, conv_b,
              hy_w1, hy_b1, hy_w2, hy_b2, hy_w3, hy_sin_freq, hy_decay, hy_bias,
              s5_lam_re, s5_lam_im, s5_log_dt, s5_b_re, s5_b_im, s5_c_re, s5_c_im,
              s5_d, s5_glu_w, s5_glu_b, branch_g_s5, branch_g_hy, w_out,
              norm2_g, ffn_w_gate, ffn_w_up, ffn_w_down, final_g):
    bsz, n_lat, _ = x.shape
    n_ctx = ctx.shape[1]
    rows = n_lat // GRID_W
    for l in range(DEPTH):
        last = l == DEPTH - 1
        mod = jax.nn.silu(c) @ ada_w[l] + ada_b[l]
        mod_c = jax.nn.silu(c_ctx) @ ada_w[l] + ada_b[l]
        sh1, sc1, g1, sh2, sc2, g2 = jnp.split(mod[:, None, :], 6, axis=-1)
        csh1, csc1, cg1, csh2, csc2, cg2 = jnp.split(mod_c, 6, axis=-1)
        s5_fwd = (s5_lam_re[l, 0], s5_lam_im[l, 0], s5_log_dt[l, 0], s5_b_re[l, 0], s5_b_im[l, 0])
        s5_bwd = (s5_lam_re[l, 1], s5_lam_im[l, 1], s5_log_dt[l, 1], s5_b_re[l, 1], s5_b_im[l, 1])

        hc = modulate(rmsnorm(ctx, norm1_g[l]), csh1, csc1)
        pc = hc @ (w_in[l, :, :S5_W] if last else w_in[l])
        ugc = pc[..., :S5_W].astype(jnp.float32).reshape(bsz, n_ctx, S5_GROUPS, S5_GROUP)
        xs_cf, fin_f = s5_direction(ugc, *s5_fwd, None, False)
        xs_cb, fin_b = s5_direction(ugc, *s5_bwd, None, True)

        h = modulate(rmsnorm(x, norm1_g[l]), sh1, sc1)
        p = h @ w_in[l]
        ug = p[..., :S5_W].astype(jnp.float32).reshape(bsz, n_lat, S5_GROUPS, S5_GROUP)
        xs_f, _ = s5_direction(ug, *s5_fwd, fin_f, False)
        xs_b, _ = s5_direction(ug, *s5_bwd, fin_b, True)
        y_s5 = s5_readout(ug, xs_f, xs_b, s5_c_re[l], s5_c_im[l], s5_d[l], s5_glu_w[l], s5_glu_b[l], x.dtype)
        filt = hyena_filters(n_lat, hy_w1[l], hy_b1[l], hy_w2[l], hy_b2[l], hy_w3[l], hy_sin_freq[l], hy_decay[l])
        y_hy = hyena_mixer(p[..., S5_W:], filt, conv_w[l], conv_b[l], hy_bias[l], rows)
        mix = jnp.concatenate([rmsnorm(y_s5, branch_g_s5[l]), rmsnorm(y_hy, branch_g_hy[l])], axis=-1)
        x = x + g1 * (mix @ w_out[l])

        h2 = modulate(rmsnorm(x, norm2_g[l]), sh2, sc2)
        x = x + g2 * swiglu(h2, ffn_w_gate[l], ffn_w_up[l], ffn_w_down[l])

        if not last:
            yc_s5 = s5_readout(ugc, xs_cf, xs_cb, s5_c_re[l], s5_c_im[l], s5_d[l], s5_glu_w[l], s5_glu_b[l], ctx.dtype)
            filt_c = hyena_filters(n_ctx, hy_w1[l], hy_b1[l], hy_w2[l], hy_b2[l], hy_w3[l], hy_sin_freq[l], hy_decay[l])
            yc_hy = hyena_mixer(pc[..., S5_W:], filt_c, conv_w[l], conv_b[l], hy_bias[l], None)
            mixc = jnp.concatenate([rmsnorm(yc_s5, branch_g_s5[l]), rmsnorm(yc_hy, branch_g_hy[l])], axis=-1)
            ctx = ctx + cg1 * (mixc @ w_out[l])
            hc2 = modulate(rmsnorm(ctx, norm2_g[l]), csh2, csc2)
            ctx = ctx + cg2 * swiglu(hc2, ffn_w_gate[l], ffn_w_up[l], ffn_w_down[l])
    return rmsnorm(x, final_g)
```

```python
import os
import math
from contextlib import ExitStack

import numpy as np
import ml_dtypes

import concourse.bass as bass
import concourse.mybir as mybir
from concourse.bass_utils import run_bass_kernel_spmd

F32 = mybir.dt.float32
BF16 = mybir.dt.bfloat16
AF = mybir.ActivationFunctionType
ALU = mybir.AluOpType

D = 2048
T = 2048
CT = 256
NK = 16
S5W = 1024
HYW = 1024
G = 64
DFF = 5632
NFC = DFF // 128
EPS = 1e-6
MAGIC = 12582912.0
TWO_PI_LO = 6.28318
HALF_PI_LO = 1.570795

ENGS = ["tensor", "vector", "scalar", "gpsimd", "sync"]
CONV_MODE = "dram"
CONV_NBUF = 3


class Res:
    _n = 0

    def __init__(self, name):
        Res._n += 1
        self.id = Res._n
        self.name = name
        self.w = None
        self.r = []
        self.sem = None
        self.semid = None
        self.dtot = 0


class Tl:
    def __init__(self, t, R):
        self.t = t
        self.R = R

    def __getitem__(self, k):
        return self.t[k]


def _R(x):
    return x.R if isinstance(x, Tl) else x


class Sched:
    def __init__(self, nc, stack):
        self.nc = nc
        self.stack = stack
        self.ops = {e: [] for e in ENGS}
        self.cnt = {e: 0 for e in ENGS}
        self.esem = {e: stack.enter_context(nc.semaphore("es_" + e)) for e in ENGS if e != "sync"}
        self.waited = {e: {} for e in ENGS}
        self.dres = []
        self.pool = {"sync": [], "gpsimd": [], "scalar": []}
        self.nsem = 0
        self.epoch = 0
        self.bar_tile = stack.enter_context(nc.sbuf_tensor("bar_tile", [128, 1], F32))

    def _need(self, eng, ev):
        if ev is None:
            return
        if ev[0] == "E":
            q, v = ev[1], ev[2]
            if q == eng and (q == "tensor" or (q in ("vector", "scalar") and len(ev) > 3 and not ev[3])):
                return
            if self.waited[eng].get(q, 0) >= v:
                return
            self.waited[eng][q] = v
            sem = self.esem[q]
            self.ops[eng].append(lambda e, sem=sem, v=v: e.wait_ge(sem, v))
        else:
            r = ev[1]
            if ev[2] < self.epoch or r.sem is None:
                return
            v = ev[3] if len(ev) > 3 else r.dtot
            key = ("S", r.semid)
            if self.waited[eng].get(key, 0) >= v:
                return
            self.waited[eng][key] = v
            sem = r.sem
            self.ops[eng].append(lambda e, sem=sem, v=v: e.wait_ge(sem, v))

    def _deps(self, eng, reads, writes):
        for r in reads:
            self._need(eng, r.w)
        for r in writes:
            self._need(eng, r.w)
            for ev in r.r:
                self._need(eng, ev)

    def _commit(self, ev, reads, writes):
        for r in reads:
            r.r.append(ev)
            if len(r.r) > 16:
                best = {}
                keep = []
                for x in r.r:
                    if x[0] == "E":
                        if best.get(x[1], 0) < x[2]:
                            best[x[1]] = x[2]
                    elif x[2] >= self.epoch and x not in keep:
                        keep.append(x)
                r.r = keep + [("E", q, v, True) for q, v in best.items()]
        for r in writes:
            r.w = ev
            r.r = []

    def op(self, eng, fn, reads=(), writes=(), small=True):
        reads = [_R(x) for x in reads]
        writes = [_R(x) for x in writes]
        self._deps(eng, reads, writes)
        self.cnt[eng] += 1
        v = self.cnt[eng]
        sem = self.esem[eng]
        self.ops[eng].append(lambda e, fn=fn, sem=sem: fn(e).then_inc(sem, 1))
        self._commit(("E", eng, v, small), reads, writes)

    def dma(self, q, out, in_, reads=(), writes=(), owner=None, exact=False):
        reads = [_R(x) for x in reads]
        writes = [_R(x) for x in writes]
        if owner is None:
            owner = writes[0] if writes else reads[0]
        owner = _R(owner)
        if owner.sem is None:
            if self.pool[q]:
                owner.sem, owner.semid, owner.dtot = self.pool[q].pop()
            else:
                self.nsem += 1
                owner.sem = self.stack.enter_context(self.nc.semaphore("ds_%d" % self.nsem))
                owner.semid = self.nsem
                owner.dtot = 0
            owner.semq = q
            self.dres.append(owner)
        assert owner.semq == q, "DMA owner semaphore shared between queues"
        self._deps(q, reads, writes)
        owner.dtot += 16
        sem = owner.sem
        self.ops[q].append(lambda e, out=out, in_=in_, sem=sem: e.dma_start(out=out, in_=in_).then_inc(sem, 16))
        if exact:
            self._commit(("D", owner, self.epoch, owner.dtot), reads, writes)
        else:
            self._commit(("D", owner, self.epoch), reads, writes)

    def barrier(self):
        c = "gpsimd"
        for q in ENGS:
            if q not in ("sync", c) and self.cnt[q] > 0:
                self._need(c, ("E", q, self.cnt[q]))
        for r in self.dres:
            self._need(c, ("D", r, self.epoch))
        bt = self.bar_tile
        self.op(c, lambda e: e.memset(bt[:], 0.0), [], [])
        v = self.cnt[c]
        for e in ENGS:
            if e != c:
                self._need(e, ("E", c, v))
            for q in ENGS:
                if q != "sync":
                    self.waited[e][q] = max(self.waited[e].get(q, 0), self.cnt[q])
        for r in self.dres:
            self.pool[r.semq].append((r.sem, r.semid, r.dtot))
            r.sem = None
        self.dres = []
        self.epoch += 1

    def finish(self):
        self.barrier()
        nc = self.nc
        ops = self.ops
        with nc.Block() as block:
            @block.tensor
            def _(e):
                for f in ops["tensor"]:
                    f(e)

            @block.vector
            def _(e):
                for f in ops["vector"]:
                    f(e)

            @block.scalar
            def _(e):
                for f in ops["scalar"]:
                    f(e)

            @block.gpsimd
            def _(e):
                for f in ops["gpsimd"]:
                    f(e)

            @block.sync
            def _(e):
                for f in ops["sync"]:
                    f(e)


class K:
    def __init__(self, nc, st):
        self.nc = nc
        self.st = st
        self.S = Sched(nc, st)
        self.banks = []
        for i in range(8):
            t = st.enter_context(nc.psum_tensor("psb%d" % i, [128, 512], F32))
            self.banks.append(Tl(t, Res("psb%d" % i)))
        self.bi = 0
        self.brange = (0, 8)
        self.nuniq = 0

    def bank(self):
        lo, hi = self.brange
        b = self.banks[lo + self.bi % (hi - lo)]
        self.bi += 1
        return b

    def sb(self, st, name, shape, dt=F32):
        self.nuniq += 1
        nm = "%s_%d" % (name, self.nuniq)
        return Tl(st.enter_context(self.nc.sbuf_tensor(nm, list(shape), dt)), Res(nm))

    def ld(self, out, in_, w, r=()):
        self.S.dma("sync", out, in_, reads=r, writes=[w])

    def sto(self, out, in_, r, w=(), own_src=False, tracked=False):
        rl = [r] if not isinstance(r, (list, tuple)) else list(r)
        if not tracked and not own_src:
            if not hasattr(self, "_rot"):
                self._rot = [Res("rot%d" % i) for i in range(4)]
                self._nrot = 0
            owner = self._rot[self._nrot % 4]
            self._nrot += 1
            self.S.dma("gpsimd", out, in_, reads=rl, writes=[], owner=owner, exact=True)
            return
        rl = [r] if not isinstance(r, (list, tuple)) else list(r)
        owner = w[0] if w else None
        if own_src:
            src = _R(rl[0])
            if src.sem is None or getattr(src, "semq", None) == "gpsimd":
                owner = src
        self.S.dma("gpsimd", out, in_, reads=rl, writes=list(w), owner=owner)

    @staticmethod
    def _small(ap):
        n = 1
        for d in list(ap.shape)[1:]:
            n *= int(d)
        return n < 128

    def cp(self, out, in_, r, w, eng="vector"):
        if eng == "scalar":
            self.S.op(eng, lambda e: e.activation(out=out, in_=in_, func=AF.Copy), r, w, small=self._small(out))
        else:
            self.S.op(eng, lambda e: e.tensor_copy(out=out, in_=in_), r, w, small=self._small(out))

    def tt(self, out, in0, in1, op, r, w, eng="vector"):
        self.S.op(eng, lambda e: e.tensor_tensor(out=out, in0=in0, in1=in1, op=op), r, w, small=self._small(out))

    def ts(self, out, in0, s1, s2, op0, op1, r, w, eng="vector"):
        sm = self._small(out)
        if s2 is None:
            self.S.op(eng, lambda e: e.tensor_scalar(out=out, in0=in0, scalar1=s1, scalar2=None, op0=op0), r, w, small=sm)
        else:
            self.S.op(eng, lambda e: e.tensor_scalar(out=out, in0=in0, scalar1=s1, scalar2=s2, op0=op0, op1=op1), r, w, small=sm)

    def stt(self, out, in0, scalar, in1, op0, op1, r, w):
        self.S.op("vector", lambda e: e.scalar_tensor_tensor(out=out, in0=in0, scalar=scalar, in1=in1, op0=op0, op1=op1), r, w,
                  small=self._small(out))

    def act(self, out, in_, func, r, w, scale=1.0, bias=None, accum=None):
        def f(e):
            kw = {}
            if bias is not None:
                kw["bias"] = bias
            if accum is not None:
                kw["accum_out"] = accum
            return e.activation(out=out, in_=in_, func=func, scale=scale, **kw)
        self.S.op("scalar", f, r, w, small=(accum is not None) or self._small(out))

    def mm(self, out, lhsT, rhs, start, stop, r, w):
        self.S.op("tensor", lambda e: e.matmul(out, lhsT=lhsT, rhs=rhs, start=start, stop=stop), r, w)

    def tr(self, out, in_, ident, r, w):
        self.S.op("tensor", lambda e: e.transpose(out, in_, ident), r, w)

    def recip(self, out, in_, r, w):
        self.S.op("vector", lambda e: e.reciprocal(out=out, in_=in_), r, w)

    def memset(self, out, val, w, eng="vector"):
        self.S.op(eng, lambda e: e.memset(out, val), [], w)

    def scan(self, out, d0, d1, init, r, w):
        self.S.op("vector", lambda e: e.tensor_tensor_scan(out=out, data0=d0, data1=d1, initial=init, op0=ALU.mult, op1=ALU.add), r, w,
                  small=False)

    def rsum(self, out, in_, r, w):
        self.S.op("vector", lambda e: e.reduce_sum(out=out, in_=in_, axis=mybir.AxisListType.X), r, w)

    def frac_turns(self, y, rr, yT, rrT):
        self.ts(rr, y, MAGIC, None, ALU.add, None, [yT], [rrT])
        self.ts(rr, rr, MAGIC, None, ALU.subtract, None, [rrT], [rrT])
        self.tt(y, y, rr, ALU.subtract, [yT, rrT], [yT])

    def sincos(self, st, ang, shape, sin_out, cos_out, r, w_sin, w_cos, tag):
        y = self.sb(st, tag + "_y", shape)
        rr = self.sb(st, tag + "_r", shape)
        self.sincos2(ang, y, rr, sin_out, cos_out, r, w_sin, w_cos)

    def sincos2(self, ang, y, rr, sin_out, cos_out, r, w_sin, w_cos, turns=False):
        if turns:
            self.cp(y[:], ang, r, [y])
        else:
            self.ts(y[:], ang, 1.0 / (2 * math.pi), None, ALU.mult, None, r, [y])
        self._sincos_tail(y, rr, sin_out, cos_out, w_sin, w_cos)

    def _sincos_tail(self, y, rr, sin_out, cos_out, w_sin, w_cos):
        r = None
        self.ts(rr[:], y[:], MAGIC, None, ALU.add, None, [y], [rr])
        self.ts(rr[:], rr[:], MAGIC, None, ALU.subtract, None, [rr], [rr])
        self.tt(y[:], y[:], rr[:], ALU.subtract, [y, rr], [y])
        if sin_out is not None:
            self.act(sin_out, y[:], AF.Sin, [y], w_sin, scale=TWO_PI_LO)
        if cos_out is not None:
            self.stt(rr[:], y[:], -1.0, y[:], ALU.mult, ALU.max, [y], [rr])
            self.act(cos_out, rr[:], AF.Sin, [rr], w_cos, scale=-TWO_PI_LO, bias=self.halfpi[:, 0:1])


_CONST_CACHE = {}


def host_consts():
    if _CONST_CACHE:
        return _CONST_CACHE
    N = 4096
    n = 2048
    idx = np.arange(n, dtype=np.int64)
    ang = 2.0 * np.pi * ((idx[:, None] * idx[None, :]) % N).astype(np.float64) / N
    cs = np.cos(ang)
    sn = np.sin(ang)
    alt = (-1.0) ** idx
    fwd = np.empty((n, N), np.float64)
    fwd[:, :n] = cs
    fwd[:, n:] = -sn
    fwd[:, n] = alt
    inv = np.empty((N, n), np.float64)
    inv[:n] = (2.0 / N) * cs
    inv[0] = 1.0 / N
    inv[n:] = -(2.0 / N) * sn
    inv[n] = alt / N
    c = _CONST_CACHE
    fw16 = fwd.astype(np.float32).astype(ml_dtypes.bfloat16)
    iv16 = inv.astype(np.float32).astype(ml_dtypes.bfloat16)
    c["FWD"] = np.ascontiguousarray(fw16.reshape(16, 128, 32, 128).transpose(2, 1, 0, 3))
    c["INV"] = np.ascontiguousarray(iv16.reshape(32, 128, 16, 128).transpose(2, 1, 0, 3))
    pos = np.arange(n, dtype=np.float32)
    tt = pos[:, None] / np.float32(n)
    bands = np.linspace(1e-4, 15, 16, dtype=np.float32)
    a2 = (np.float32(2.0 * math.pi) * pos[:, None] * bands[None, :] / np.float32(n)).astype(np.float32)
    z = np.concatenate([tt, np.cos(a2), -np.sin(a2)], axis=-1).astype(np.float32)
    c["ZT"] = np.ascontiguousarray(z.T)
    c["IDENT"] = np.eye(128, dtype=np.float32)
    c["IDENTB"] = np.eye(128, dtype=np.float32).astype(ml_dtypes.bfloat16)
    sig = np.arange(128) // 16
    c["MASKF"] = (sig[None, :] >= sig[:, None]).astype(np.float32)
    c["MASKB"] = (sig[:, None] >= sig[None, :]).astype(np.float32)
    ex = np.zeros((3, 128, 8), np.float32)
    i8 = np.arange(8, dtype=np.float32)
    ex[0, :64] = 7 - i8
    ex[0, 64:] = i8
    ex[1, :64] = i8 - 7
    ex[1, 64:] = -i8
    ex[2, :64] = i8 + 1
    ex[2, 64:] = 8 - i8
    c["EXPO"] = np.ascontiguousarray(np.broadcast_to(ex[None], (128, 3, 128, 8))).astype(np.float32)
    cols = np.zeros((128, 16), np.float32)
    p = np.arange(128)
    cols[:, 0] = 1.0
    cols[:, 1] = np.where(p < 64, -1.0, 1.0)
    cols[:, 2] = np.where(p < 64, 1.0, -1.0)
    cols[:, 3] = -1.0
    cols[:, 4] = (p % 64 != 0)
    cols[:, 5] = (p % 64 != 63)
    cols[:, 6] = HALF_PI_LO
    cols[:, 7] = (p != 0)
    cols[:, 8] = alt[:128]
    for q in range(16):
        pass
    c["COLS"] = cols
    shp = np.zeros((128, 128), np.float32)
    shn = np.zeros((128, 128), np.float32)
    for m in range(128):
        if m % 64 != 0:
            shp[m - 1, m] = 1.0
        if m % 64 != 63:
            shn[m + 1, m] = 1.0
    c["SHP"] = shp
    c["SHN"] = shn
    c["POSF"] = (-(np.arange(16)[None, :] * 128 + p[:, None]) / 2048.0).astype(np.float32)
    c["IOTA"] = np.ascontiguousarray(np.broadcast_to(np.arange(1, 257, dtype=np.float32)[None], (128, 256)))
    kk = np.zeros((128, 128, 32), np.float32)
    kk[:, :64] = 31 - np.arange(32)
    kk[:, 64:] = np.arange(32)
    c["KCTX"] = kk
    return c


def prep_shared(inp):
    f = lambda k: np.ascontiguousarray(np.asarray(inp[k], np.float32))
    o = {}
    o["ada_w"] = f("ada_w")[0]
    o["ada_b"] = f("ada_b")
    o["w_in"] = f("w_in")[0]
    o["rows1"] = np.ascontiguousarray(np.stack([f("norm1_g")[0], f("norm2_g")[0], f("final_g"),
                                                np.concatenate([f("branch_g_s5")[0], f("branch_g_hy")[0]])]))
    o["conv_w"] = f("conv_w")[0]
    o["conv_b"] = f("conv_b")
    o["hy_w1"] = f("hy_w1")[0]
    o["hy_w2"] = f("hy_w2")[0]
    o["hy_w3"] = f("hy_w3")[0]
    o["hy_cols"] = np.ascontiguousarray(np.stack([f("hy_b1")[0], f("hy_b2")[0], f("hy_sin_freq")[0, 0], f("hy_sin_freq")[0, 1]], axis=1))
    o["hy_decay"] = f("hy_decay").reshape(1, 4096)
    o["hy_bias"] = f("hy_bias").reshape(1, 2048)
    lr = f("s5_lam_re")[0].reshape(128, 64).T
    li = f("s5_lam_im")[0].reshape(128, 64).T
    o["LAMR2"] = np.ascontiguousarray(np.concatenate([lr, lr], 0))
    o["LAMI2"] = np.ascontiguousarray(np.concatenate([li, li], 0))
    o["LDT"] = np.ascontiguousarray(np.broadcast_to(f("s5_log_dt")[0].reshape(1, 128), (128, 128)))
    br = f("s5_b_re")[0].reshape(128, 64, 16).transpose(1, 0, 2)
    bi = f("s5_b_im")[0].reshape(128, 64, 16).transpose(1, 0, 2)
    o["XB1"] = np.ascontiguousarray(np.concatenate([br, bi], 0))
    o["XB2"] = np.ascontiguousarray(np.concatenate([bi, br], 0))
    cr = f("s5_c_re")[0].reshape(128, 16, 64).transpose(2, 0, 1)
    ci = f("s5_c_im")[0].reshape(128, 16, 64).transpose(2, 0, 1)
    o["XC1"] = np.ascontiguousarray(np.concatenate([cr, ci], 0))
    o["XC2"] = np.ascontiguousarray(np.concatenate([ci, cr], 0))
    dd = f("s5_d")[0]
    o["DCOL"] = np.ascontiguousarray(np.tile(dd.T, (8, 1)))
    o["glu_w"] = f("s5_glu_w")[0]
    o["glu_b"] = f("s5_glu_b")
    o["w_out"] = f("w_out")[0]
    o["wg"] = f("ffn_w_gate")[0]
    o["wu"] = f("ffn_w_up")[0]
    o["wd"] = f("ffn_w_down")[0]
    o.update(host_consts())
    return o


IN_SPECS = None


def in_specs():
    return [
        ("x", [T, D], F32), ("ctx", [CT, D], F32), ("cvec", [128, NK, 2], F32),
        ("ada_w", [D, 6 * D], F32), ("ada_b", [1, 6 * D], F32), ("w_in", [D, 4096], F32),
        ("rows1", [4, D], F32), ("conv_w", [3, 3072], F32), ("conv_b", [1, 3072], F32),
        ("hy_w1", [33, 64], F32), ("hy_w2", [64, 64], F32), ("hy_w3", [64, 4096], F32), ("hy_cols", [64, 4], F32),
        ("hy_decay", [1, 4096], F32), ("hy_bias", [1, 2048], F32),
        ("LAMR2", [128, 128], F32), ("LAMI2", [128, 128], F32), ("LDT", [128, 128], F32),
        ("XB1", [128, 128, 16], F32), ("XB2", [128, 128, 16], F32), ("XC1", [128, 128, 16], F32), ("XC2", [128, 128, 16], F32),
        ("DCOL", [128, 64], F32), ("glu_w", [S5W, 2048], F32), ("glu_b", [1, 2048], F32), ("w_out", [D, D], F32),
        ("wg", [D, DFF], F32), ("wu", [D, DFF], F32), ("wd", [DFF, D], F32),
        ("FWD", [32, 128, 16, 128], BF16), ("INV", [16, 128, 32, 128], BF16), ("ZT", [33, 2048], F32),
        ("IDENT", [128, 128], F32), ("IDENTB", [128, 128], BF16), ("MASKF", [128, 128], F32), ("MASKB", [128, 128], F32),
        ("EXPO", [128, 3, 128, 8], F32), ("COLS", [128, 16], F32), ("POSF", [128, 16], F32), ("IOTA", [128, 256], F32),
        ("KCTX", [128, 128, 32], F32), ("SHP", [128, 128], F32), ("SHN", [128, 128], F32),
    ]


ALL_STAGES = ["mod", "proj", "filt", "s5prep", "s5", "glu", "hyena", "out", "ffn"]


def build_program(dbg=(), stages=None):
    stages = ALL_STAGES if stages is None else stages
    nc = bass.Bass("TRN2", target_bir_lowering=False)
    I = {}
    for name, shape, dt in in_specs():
        I[name] = nc.dram_tensor(name, list(shape), dt, kind="ExternalInput").ap()
    OUT = nc.dram_tensor("out", [T, D], F32, kind="ExternalOutput").ap()

    def scr(name, shape, dt=F32):
        if name in dbg:
            return nc.dram_tensor(name, list(shape), dt, kind="ExternalOutput").ap()
        return nc.dram_tensor(name, list(shape), dt).ap()

    SC = {}
    SC["MOD"] = scr("MOD", [2, 6 * D])
    SC["U"] = scr("U", [T, S5W], BF16)
    SC["UC"] = scr("UC", [CT, S5W], BF16)
    SC["PH"] = scr("PH", [T, 3 * HYW])
    SC["HT"] = scr("HT", [2048, 4096])
    SC["HF"] = scr("HF", [4096, 2048])
    SC["S5WT"] = scr("S5WT", [G, 128, 9 * 128], BF16)
    SC["S5P"] = scr("S5P", [128, 2, 128])
    SC["YS5"] = scr("YS5", [T, S5W], BF16)
    SC["YH"] = scr("YH", [T, HYW])
    SC["MIX"] = scr("MIX", [T, D], BF16)
    SC["X1"] = scr("X1", [T, D])
    SC["X0"] = scr("X0", [128, 128])
    SC["HTS"] = scr("HTS", [T // 128, 128, NFC, 128], BF16)
    SC["X2"] = scr("X2", [T, D])
    SC["H2"] = scr("H2", [T, D], BF16)
    RS = {k: Res("scr_" + k) for k in SC}

    with ExitStack() as st0:
        k = K(nc, st0)
        S = k.S
        cols = k.sb(st0, "cols", [128, 16])
        k.ld(cols[:], I["COLS"], cols)
        k.halfpi = cols[:, 6:7]
        k.cols = cols
        ident = k.sb(st0, "ident", [128, 128])
        k.ld(ident[:], I["IDENT"], ident)
        identb = k.sb(st0, "identb", [128, 128], BF16)
        k.ld(identb[:], I["IDENTB"], identb)
        k.ident = ident
        k.identb = identb
        env = dict(I=I, SC=SC, RS=RS, OUT=OUT, dbg=dbg)
        if "mod" in stages and "s5prep" in stages:
            with ExitStack() as stm:
                stage_s5prep(k, env, side=gen_mod(k, env, stm))
            S.barrier()
        elif "mod" in stages:
            stage_mod(k, env)
            S.barrier()
        if "proj" in stages:
            stage_proj_all(k, env)
            S.barrier()
        if "filt" in stages:
            stage_filt(k, env)
            S.barrier()
        if "s5prep" in stages and "mod" not in stages:
            stage_s5prep(k, env)
            S.barrier()
        if "s5" in stages:
            stage_s5(k, env)
            S.barrier()
        if "glu" in stages:
            stage_glu(k, env)
            S.barrier()
        if "hyena" in stages:
            stage_hyena(k, env)
            S.barrier()
        if "out" in stages:
            stage_out(k, env)
            S.barrier()
        if "ffn" in stages:
            stage_ffn(k, env)
        S.finish()
    return nc


def gen_mod(k, env, st):
    I, SC, RS = env["I"], env["SC"], env["RS"]
    NB, W = 48, 256
    cv = k.sb(st, "cv", [128, NK, 2])
    k.ld(cv[:], I["cvec"], cv)
    sl = k.sb(st, "sl", [128, NK, 2])
    k.act(sl[:], cv[:], AF.Silu, [cv], [sl])
    wts = [k.sb(st, "adaw%d" % i, [128, NK, W]) for i in range(3)]
    adab = [k.sb(st, "adab%d" % i, [2, W]) for i in range(3)]
    mods = [k.sb(st, "modsb%d" % i, [2, W]) for i in range(3)]

    def load(nb):
        k.ld(wts[nb % 3][:], I["ada_w"][:, nb * W:(nb + 1) * W].rearrange("(kc p) n -> p kc n", p=128), wts[nb % 3])
        k.ld(adab[nb % 3][:], I["ada_b"][0:1, nb * W:(nb + 1) * W].broadcast_to([2, W]), adab[nb % 3])
    load(0)
    load(1)
    for nb in range(NB):
        if nb + 2 < NB:
            load(nb + 2)
        yield
        wt, ab, mo = wts[nb % 3], adab[nb % 3], mods[nb % 3]
        pb = k.bank()
        for kc in range(NK):
            k.mm(pb[0:2, 0:W], sl[:, kc, :], wt[:, kc, :], kc == 0, kc == NK - 1, [sl, wt], [pb])
        k.tt(mo[:], pb[0:2, 0:W], ab[:], ALU.add, [pb, ab], [mo])
        k.sto(SC["MOD"][:, nb * W:(nb + 1) * W], mo[:], mo, [RS["MOD"]])


def stage_mod(k, env):
    with ExitStack() as st:
        for _ in gen_mod(k, env, st):
            pass


def run_with_side(main_gen, side_gen, every=2):
    n = 0
    side_live = True
    for _ in main_gen:
        n += 1
        if side_live and n % every == 0:
            try:
                next(side_gen)
            except StopIteration:
                side_live = False
    if side_live:
        for _ in side_gen:
            pass


def norm_tmps(k, st, tag):
    return (k.sb(st, tag + "junk", [128, D], BF16), k.sb(st, tag + "ssq", [128, 1]), k.sb(st, tag + "tmp", [128, D]))


def norm_rows(k, tmps, xt, n_part, srow, shrow, out_bf):
    junk, ssq, tmp = tmps
    k.act(junk[0:n_part, :], xt[0:n_part, :], AF.Square, [xt], [junk, ssq], accum=ssq[0:n_part, :])
    k.ts(ssq[0:n_part, :], ssq[0:n_part, :], 1.0 / D, EPS, ALU.mult, ALU.add, [ssq], [ssq])
    k.act(ssq[0:n_part, :], ssq[0:n_part, :], AF.Sqrt, [ssq], [ssq])
    k.recip(ssq[0:n_part, :], ssq[0:n_part, :], [ssq], [ssq])
    k.stt(tmp[0:n_part, :], xt[0:n_part, :], ssq[0:n_part, 0:1], srow[0:n_part, :], ALU.mult, ALU.mult, [xt, ssq, srow], [tmp])
    if shrow is not None:
        k.tt(out_bf[0:n_part, :], tmp[0:n_part, :], shrow[0:n_part, :], ALU.add, [tmp, shrow], [out_bf])
    else:
        k.cp(out_bf[0:n_part, :], tmp[0:n_part, :], [tmp], [out_bf])


def transpose_to(k, src_bf, nkc, dstT, col0, ncols=128):
    for h in range(0, nkc, 8):
        pb = k.bank()
        pbb = pb[:].bitcast(BF16)
        n = min(8, nkc - h)
        for j in range(n):
            k.tr(pbb[:, j * 128:j * 128 + ncols], src_bf[0:ncols, (h + j) * 128:(h + j + 1) * 128], k.identb[0:ncols, 0:ncols],
                 [src_bf, k.identb], [pb])
        src = pbb[:, 0:n * 128].rearrange("p (j c) -> p j c", c=128)[:, :, 0:ncols]
        k.cp(dstT[:, h:h + n, col0:col0 + ncols], src, [pb], [dstT], eng="scalar" if (h // 8) % 2 else "vector")


def stage_proj(k, env, ctx):
    I, SC, RS = env["I"], env["SC"], env["RS"]
    ntok = CT if ctx else T
    ntile = ntok // 128
    xin = I["ctx"] if ctx else I["x"]
    mrow = 1 if ctx else 0
    nbs = 2 if ctx else 8
    with ExitStack() as st:
        srow = k.sb(st, "srow", [128, D])
        shrow = k.sb(st, "shrow", [128, D])
        g1r = k.sb(st, "g1r", [128, D])
        k.ld(srow[:], SC["MOD"][mrow:mrow + 1, D:2 * D].broadcast_to([128, D]), srow, r=[RS["MOD"]])
        k.ld(shrow[:], SC["MOD"][mrow:mrow + 1, 0:D].broadcast_to([128, D]), shrow, r=[RS["MOD"]])
        k.ld(g1r[:], I["rows1"][0:1, :].broadcast_to([128, D]), g1r)
        k.stt(srow[:], srow[:], 1.0, g1r[:], ALU.add, ALU.mult, [srow, g1r], [srow])
        xnT = [k.sb(st, "xnT%d" % i, [128, NK, 128], BF16) for i in range(ntile)]
        stg = [k.sb(st, "wstg%d" % i, [128, 4, 512]) for i in range(2)]
        wbs = [k.sb(st, "wb%d" % i, [128, NK, 512], BF16) for i in range(2)]
        ob = [k.sb(st, "ob%d" % i, [128, 512], BF16) for i in range(4)]
        of = [k.sb(st, "of%d" % i, [128, 512]) for i in range(2)]
        xts = [k.sb(st, "xt%d" % i, [128, D]) for i in range(2)]
        xbs = [k.sb(st, "xb%d" % i, [128, D], BF16) for i in range(2)]
        ntm = norm_tmps(k, st, "n1")
        qq = [0]

        def load_w(nb):
            wb = wbs[nb % 2]
            for kq in range(4):
                sg = stg[qq[0] % 2]
                qq[0] += 1
                k.ld(sg[:], I["w_in"][kq * 512:(kq + 1) * 512, nb * 512:(nb + 1) * 512].rearrange("(kc p) n -> p kc n", p=128), sg)
                k.cp(wb[:, kq * 4:(kq + 1) * 4, :], sg[:], [sg], [wb], eng="scalar" if kq % 2 else "vector")

        def mm_tile(nb, tt):
            wb = wbs[nb % 2]
            pb = k.bank()
            for kc in range(NK):
                k.mm(pb[:], xnT[tt][:, kc, :], wb[:, kc, :], kc == 0, kc == NK - 1, [xnT[tt], wb], [pb])
            if nb < 2:
                o = ob[(2 * tt + nb) % 4]
                k.cp(o[:], pb[:], [pb], [o], eng="scalar" if tt % 2 else "vector")
                dst = SC["UC"] if ctx else SC["U"]
                k.sto(dst[tt * 128:(tt + 1) * 128, nb * 512:(nb + 1) * 512], o[:], o, [RS["UC" if ctx else "U"]])
            else:
                o = of[tt % 2]
                k.cp(o[:], pb[:], [pb], [o], eng="scalar" if tt % 2 else "vector")
                k.sto(SC["PH"][tt * 128:(tt + 1) * 128, (nb - 2) * 512:(nb - 1) * 512], o[:], o, [RS["PH"]])

        load_w(0)
        load_w(1)
        for tt in range(ntile):
            xt = xts[tt % 2]
            xb = xbs[tt % 2]
            k.ld(xt[:], xin[tt * 128:(tt + 1) * 128, :], xt)
            norm_rows(k, ntm, xt, 128, srow, shrow, xb)
            transpose_to(k, xb, NK, xnT[tt], 0)
            mm_tile(0, tt)
            mm_tile(1, tt)
        if nbs > 2:
            load_w(2)
        for nb in range(2, nbs):
            if nb + 1 < nbs:
                load_w(nb + 1)
            for tt in range(ntile):
                mm_tile(nb, tt)


def stage_proj_all(k, env):
    I, SC, RS = env["I"], env["SC"], env["RS"]
    ntile = T // 128
    with ExitStack() as st:
        srow = k.sb(st, "srow", [128, D])
        shrow = k.sb(st, "shrow", [128, D])
        g1r = k.sb(st, "g1r", [128, D])
        k.ld(g1r[:], I["rows1"][0:1, :].broadcast_to([128, D]), g1r)

        def set_rows(mrow):
            k.ld(srow[:], SC["MOD"][mrow:mrow + 1, D:2 * D].broadcast_to([128, D]), srow, r=[RS["MOD"]])
            k.ld(shrow[:], SC["MOD"][mrow:mrow + 1, 0:D].broadcast_to([128, D]), shrow, r=[RS["MOD"]])
            k.stt(srow[:], srow[:], 1.0, g1r[:], ALU.add, ALU.mult, [srow, g1r], [srow])
        xnT = [k.sb(st, "xnT%d" % i, [128, NK, 128], BF16) for i in range(ntile)]
        stg = [k.sb(st, "wstg%d" % i, [128, 4, 512]) for i in range(2)]
        wbs = [k.sb(st, "wb%d" % i, [128, NK, 512], BF16) for i in range(2)]
        ob = [k.sb(st, "ob%d" % i, [128, 512], BF16) for i in range(4)]
        of = [k.sb(st, "of%d" % i, [128, 512]) for i in range(2)]
        xts = [k.sb(st, "xt%d" % i, [128, D]) for i in range(2)]
        xbs = [k.sb(st, "xb%d" % i, [128, D], BF16) for i in range(2)]
        ntm = norm_tmps(k, st, "n1")
        qq = [0]
        cnt = [0]

        def load_w(nb):
            wb = wbs[nb % 2]
            for kq in range(4):
                sg = stg[qq[0] % 2]
                qq[0] += 1
                k.ld(sg[:], I["w_in"][kq * 512:(kq + 1) * 512, nb * 512:(nb + 1) * 512].rearrange("(kc p) n -> p kc n", p=128), sg)
                k.cp(wb[:, kq * 4:(kq + 1) * 4, :], sg[:], [sg], [wb], eng="scalar" if kq % 2 else "vector")

        def mm_tile(nb, tt, is_ctx=False):
            wb = wbs[nb % 2]
            pb = k.bank()
            for kc in range(NK):
                k.mm(pb[:], xnT[tt][:, kc, :], wb[:, kc, :], kc == 0, kc == NK - 1, [xnT[tt], wb], [pb])
            cnt[0] += 1
            eng = "scalar" if cnt[0] % 2 else "vector"
            if nb < 2:
                o = ob[cnt[0] % 4]
                k.cp(o[:], pb[:], [pb], [o], eng=eng)
                dst = SC["UC"] if is_ctx else SC["U"]
                k.sto(dst[tt * 128:(tt + 1) * 128, nb * 512:(nb + 1) * 512], o[:], o, [RS["UC" if is_ctx else "U"]])
            else:
                o = of[cnt[0] % 2]
                k.cp(o[:], pb[:], [pb], [o], eng=eng)
                k.sto(SC["PH"][tt * 128:(tt + 1) * 128, (nb - 2) * 512:(nb - 1) * 512], o[:], o, [RS["PH"]])

        def prep(xin, tt, it):
            xt = xts[it % 2]
            xb = xbs[it % 2]
            k.ld(xt[:], xin[tt * 128:(tt + 1) * 128, :], xt)
            norm_rows(k, ntm, xt, 128, srow, shrow, xb)
            transpose_to(k, xb, NK, xnT[tt], 0)

        load_w(0)
        load_w(1)
        set_rows(1)
        for c in range(CT // 128):
            prep(I["ctx"], c, c)
            mm_tile(0, c, True)
            mm_tile(1, c, True)
        set_rows(0)
        for tt in range(ntile):
            prep(I["x"], tt, tt)
            mm_tile(0, tt)
            mm_tile(1, tt)
        load_w(2)
        for nb in range(2, 8):
            if nb + 1 < 8:
                load_w(nb + 1)
            for tt in range(ntile):
                mm_tile(nb, tt)


def filt_pers(k, st):
    return (k.sb(st, "rn", [128, 2048]), k.sb(st, "brow", [128, 2048]), k.sb(st, "altb", [128, 1], BF16))


def gen_filt1(k, env, st, pers):
    I, SC, RS = env["I"], env["SC"], env["RS"]
    if True:
        zt = k.sb(st, "zt", [33, 2048])
        k.ld(zt[:], I["ZT"], zt)
        w1 = k.sb(st, "w1", [33, 64])
        k.ld(w1[:], I["hy_w1"], w1)
        w2 = k.sb(st, "w2", [64, 64])
        k.ld(w2[:], I["hy_w2"], w2)
        w3 = k.sb(st, "w3", [64, 4096])
        k.ld(w3[:], I["hy_w3"], w3)
        hc = k.sb(st, "hc", [64, 4])
        k.ld(hc[:], I["hy_cols"], hc)
        fs = k.sb(st, "fs", [64, 2])
        k.ts(fs[:], hc[:, 2:4], 1.0 / (2 * math.pi), None, ALU.mult, None, [hc], [fs])
        hh = [k.sb(st, "hmlp%d" % i, [64, 2048]) for i in range(2)]
        rr = k.sb(st, "hrr", [64, 2048])
        for layer in range(2):
            src = zt if layer == 0 else hh[0]
            kk = 33 if layer == 0 else 64
            w = w1 if layer == 0 else w2
            dst = hh[layer]
            for nb in range(4):
                pb = k.bank()
                k.mm(pb[0:64, :], w[0:kk, :], src[0:kk, nb * 512:(nb + 1) * 512], True, True, [w, src], [pb])
                k.ts(dst[:, nb * 512:(nb + 1) * 512], pb[0:64, :], hc[:, layer:layer + 1], fs[:, layer:layer + 1], ALU.add, ALU.mult,
                     [pb, hc, fs], [dst])
            k.frac_turns(dst[:], rr[:], dst, rr)
            k.act(dst[:], dst[:], AF.Sin, [dst], [dst], scale=TWO_PI_LO)
        h2 = k.sb(st, "h2b", [64, 2048], BF16)
        k.cp(h2[:], hh[1][:], [hh[1]], [h2])
        w3b = k.sb(st, "w3b", [64, 4096], BF16)
        k.cp(w3b[:], w3[:], [w3], [w3b], eng="scalar")
        w3 = w3b
        dec = k.sb(st, "dec", [128, 4096])
        k.ld(dec[:], I["hy_decay"].broadcast_to([128, 4096]), dec)
        k.act(dec[:], dec[:], AF.Abs, [dec], [dec])
        posf = k.sb(st, "posf", [128, 16])
        k.ld(posf[:], I["POSF"], posf)
        ones = k.sb(st, "ones", [128, 128], BF16)
        k.memset(ones[:], 1.0, [ones])
        l1 = k.sb(st, "l1", [128, 4096])
        wt = [k.sb(st, "fwin%d" % i, [128, 512]) for i in range(2)]
        ht = [k.sb(st, "fh%d" % i, [128, 512]) for i in range(3)]
        ab = [k.sb(st, "fab%d" % i, [128, 512], BF16) for i in range(2)]
        items = [(nb, pc) for nb in range(8) for pc in range(16)]
        pbs_ = {}

        def h3mm(i):
            nb, pc = items[i]
            pb = k.bank()
            k.mm(pb[:], h2[:, pc * 128:(pc + 1) * 128], w3[:, nb * 512:(nb + 1) * 512], True, True, [h2, w3], [pb])
            pbs_[i] = pb
        yield
        h3mm(0)
        for i, (nb, pc) in enumerate(items):
            if i + 1 < len(items):
                h3mm(i + 1)
            s_is_bwd = (nb // 2) % 2 == 1
            pl = k.banks[6 + nb % 2]
            pb = pbs_.pop(i)
            wti = wt[i % 2]
            hti = ht[i % 3]
            abi = ab[i % 2]
            k.act(wti[:], dec[:, nb * 512:(nb + 1) * 512], AF.Exp, [dec, posf], [wti], scale=posf[:, pc:pc + 1])
            k.tt(hti[:], pb[:], wti[:], ALU.mult, [pb, wti], [hti])
            if s_is_bwd and pc == 0:
                k.ts(hti[:], hti[:], k.cols[:, 7:8], None, ALU.mult, None, [hti, k.cols], [hti])
            k.sto(SC["HT"][pc * 128:(pc + 1) * 128, nb * 512:(nb + 1) * 512], hti[:], hti, [RS["HT"]])
            k.stt(abi[:], hti[:], -1.0, hti[:], ALU.mult, ALU.max, [hti], [abi])
            k.mm(pl[:], ones[:], abi[:], pc == 0, pc == 15, [ones, abi], [pl])
            if pc == 15:
                k.cp(l1[:, nb * 512:(nb + 1) * 512], pl[:], [pl], [l1], eng="scalar")
            yield
        rn, brow, altb = pers
        for o in range(2):
            k.tt(rn[:, o * 1024:(o + 1) * 1024], l1[:, o * 2048:o * 2048 + 1024], l1[:, o * 2048 + 1024:(o + 1) * 2048], ALU.add, [l1], [rn])
        k.ts(rn[:], rn[:], EPS, None, ALU.add, None, [rn], [rn])
        k.recip(rn[:], rn[:], [rn], [rn])
        k.ld(brow[:], I["hy_bias"].broadcast_to([128, 2048]), brow)
        k.cp(altb[:], k.cols[:, 8:9], [k.cols], [altb])


def gen_filt2(k, env, st, pers):
    I, SC, RS = env["I"], env["SC"], env["RS"]
    if True:
        e_ts = [k.sb(st, "e_t%d" % i, [128, 16, 512], BF16) for i in range(2)]
        d_ts = [k.sb(st, "d_t%d" % i, [128, 16, 512], BF16) for i in range(2)]
        ff = [k.sb(st, "ff%d" % i, [128, 512]) for i in range(2)]
        fb = [k.sb(st, "fb%d" % i, [128, 512]) for i in range(2)]
        fw = [k.sb(st, "fwt%d" % i, [128, 16, 128], BF16) for i in range(4)]
        ot = [k.sb(st, "fo%d" % i, [128, 512]) for i in range(2)]
        nyqs = [k.sb(st, "nyq%d" % i, [1, 512]) for i in range(2)]
        rn, brow, altb = pers
        yield

        def cols_of(q):
            o, half = q // 2, q % 2
            return o * 2048 + half * 512, o * 2048 + 1024 + half * 512, o * 1024 + half * 512

        def build(q):
            cf, cb, co = cols_of(q)
            e_t, d_t, nyq = e_ts[q % 2], d_ts[q % 2], nyqs[q % 2]
            pn = k.banks[7]
            for lc in range(16):
                a = ff[lc % 2]
                b = fb[lc % 2]
                k.ld(a[:], SC["HT"][lc * 128:(lc + 1) * 128, cf:cf + 512], a, r=[RS["HT"]])
                k.ld(b[:], SC["HT"][lc * 128:(lc + 1) * 128, cb:cb + 512], b, r=[RS["HT"]])
                k.tt(e_t[:, lc, :], a[:], b[:], ALU.add, [a, b], [e_t])
                k.tt(d_t[:, lc, :], a[:], b[:], ALU.subtract, [a, b], [d_t])
                k.mm(pn[0:1, :], altb[:, 0:1], e_t[:, lc, :], lc == 0, lc == 15, [altb, e_t], [pn])
            k.tt(nyq[:], pn[0:1, :], rn[0:1, co:co + 512], ALU.mult, [pn, rn], [nyq])
            k.tt(nyq[:], nyq[:], brow[0:1, co:co + 512], ALU.add, [nyq, brow], [nyq])

        build(0)
        mi = 0
        for q in range(4):
            cf, cb, co = cols_of(q)
            e_t, d_t, nyq = e_ts[q % 2], d_ts[q % 2], nyqs[q % 2]
            if q + 1 < 4:
                build(q + 1)
            for m in range(32):
                fwt = fw[mi % 4]
                mi += 1
                k.ld(fwt[:], I["FWD"][m], fwt)
                pb = k.bank()
                src = e_t if m < 16 else d_t
                for lc in range(16):
                    k.mm(pb[:], fwt[:, lc, :], src[:, lc, :], lc == 0, lc == 15, [fwt, src], [pb])
                o_t = ot[m % 2]
                k.tt(o_t[:], pb[:], rn[:, co:co + 512], ALU.mult, [pb, rn], [o_t])
                if m < 16:
                    k.tt(o_t[:], o_t[:], brow[:, co:co + 512], ALU.add, [o_t, brow], [o_t])
                if m == 16:
                    k.cp(o_t[0:1, :], nyq[:], [nyq, o_t], [o_t])
                k.sto(SC["HF"][m * 128:(m + 1) * 128, co:co + 512], o_t[:], o_t, [RS["HF"]])
                yield


def stage_filt(k, env):
    with ExitStack() as stp:
        pers = filt_pers(k, stp)
        k.brange = (0, 6)
        with ExitStack() as sa:
            for _ in gen_filt1(k, env, sa, pers):
                pass
        k.S.barrier()
        k.brange = (0, 7)
        with ExitStack() as sb_:
            for _ in gen_filt2(k, env, sb_, pers):
                pass
        k.brange = (0, 8)


def bc_last(ap, n):
    shp = list(ap.shape)
    return ap.unsqueeze(len(shp)).broadcast_to(shp + [n])


def gen_s5prep(k, env):
    I, SC, RS = env["I"], env["SC"], env["RS"]
    cols = k.cols
    sA, sB, sC, sD = (cols[:, i:i + 1] for i in range(4))
    with ExitStack() as st:
        def ldt(name, shape, stx=st):
            t = k.sb(stx, name, shape)
            k.ld(t[:], I[name], t)
            return t
        xc1 = ldt("XC1", [128, 128, 16])
        xc2 = ldt("XC2", [128, 128, 16])
        dcol = ldt("DCOL", [128, 64])
        maskf = ldt("MASKF", [128, 128])
        maskb = ldt("MASKB", [128, 128])
        lr = k.sb(st, "lr", [128, 128])
        li = k.sb(st, "li", [128, 128])
        xbb1 = k.sb(st, "xbb1", [128, 128, 16])
        xbb2 = k.sb(st, "xbb2", [128, 128, 16])
        with ExitStack() as sta:
            lamr = ldt("LAMR2", [128, 128], sta)
            lami = ldt("LAMI2", [128, 128], sta)
            ldtt = ldt("LDT", [128, 128], sta)
            xb1 = ldt("XB1", [128, 128, 16], sta)
            xb2 = ldt("XB2", [128, 128, 16], sta)
            sm = lambda n: k.sb(sta, n, [128, 128])
            dt = sm("dt"); ea = sm("ea"); sn = sm("sn"); cs = sm("cs")
            ar = sm("ar"); ai = sm("ai"); den = sm("den"); t1 = sm("t1"); t2 = sm("t2"); cr = sm("cr"); ci = sm("ci")
            cib = sm("cib"); cic = sm("cic")
            k.act(dt[:], ldtt[:], AF.Exp, [ldtt], [dt])
            k.tt(lr[:], lamr[:], dt[:], ALU.mult, [lamr, dt], [lr])
            k.tt(li[:], lami[:], dt[:], ALU.mult, [lami, dt], [li])
            k.act(ea[:], lr[:], AF.Exp, [lr], [ea])
            k.sincos(sta, li[:], [128, 128], sn[:], cs[:], [li], [sn], [cs], "sc0")
            yield
            k.tt(ar[:], ea[:], cs[:], ALU.mult, [ea, cs], [ar])
            k.tt(ai[:], ea[:], sn[:], ALU.mult, [ea, sn], [ai])
            k.ts(ar[:], ar[:], -1.0, None, ALU.add, None, [ar], [ar])
            k.tt(den[:], lamr[:], lamr[:], ALU.mult, [lamr], [den])
            k.tt(t1[:], lami[:], lami[:], ALU.mult, [lami], [t1])
            k.tt(den[:], den[:], t1[:], ALU.add, [den, t1], [den])
            k.recip(den[:], den[:], [den], [den])
            k.tt(t1[:], ar[:], lamr[:], ALU.mult, [ar, lamr], [t1])
            k.tt(t2[:], ai[:], lami[:], ALU.mult, [ai, lami], [t2])
            k.tt(t1[:], t1[:], t2[:], ALU.add, [t1, t2], [t1])
            k.tt(cr[:], t1[:], den[:], ALU.mult, [t1, den], [cr])
            k.tt(t1[:], ai[:], lamr[:], ALU.mult, [ai, lamr], [t1])
            k.tt(t2[:], ar[:], lami[:], ALU.mult, [ar, lami], [t2])
            k.tt(t1[:], t1[:], t2[:], ALU.subtract, [t1, t2], [t1])
            k.tt(ci[:], t1[:], den[:], ALU.mult, [t1, den], [ci])
            k.ts(cib[:], ci[:], sB, None, ALU.mult, None, [ci, cols], [cib])
            k.ts(cic[:], ci[:], sC, None, ALU.mult, None, [ci, cols], [cic])
            yield
            s5p = k.sb(sta, "s5p", [128, 2, 128])
            k.ts(s5p[:, 0, :], lr[:], 8.0, None, ALU.mult, None, [lr], [s5p])
            k.ts(s5p[:, 1, :], li[:], 8.0 / (2 * math.pi), None, ALU.mult, None, [li], [s5p])
            k.sto(SC["S5P"], s5p[:], s5p, [RS["S5P"]])
            tmpa = k.sb(sta, "tmpa", [128, 128, 16])
            k.tt(xbb1[:], bc_last(cr[:], 16), xb1[:], ALU.mult, [cr, xb1], [xbb1])
            k.tt(tmpa[:], bc_last(cib[:], 16), xb2[:], ALU.mult, [cib, xb2], [tmpa])
            k.tt(xbb1[:], xbb1[:], tmpa[:], ALU.add, [xbb1, tmpa], [xbb1])
            k.tt(xbb2[:], bc_last(cr[:], 16), xb2[:], ALU.mult, [cr, xb2], [xbb2])
            k.tt(tmpa[:], bc_last(cic[:], 16), xb1[:], ALU.mult, [cic, xb1], [tmpa])
            k.tt(xbb2[:], xbb2[:], tmpa[:], ALU.add, [xbb2, tmpa], [xbb2])
            yield
        yield "barrier"
        pw = {}
        for fam in range(3):
            pw[(fam, "r")] = k.sb(st, "pr%d" % fam, [128, 128, 8])
            pw[(fam, "i")] = k.sb(st, "pi%d" % fam, [128, 128, 8])
        with ExitStack() as stb:
            expo = k.sb(stb, "expo", [128, 3, 128, 8])
            k.ld(expo[:], I["EXPO"], expo)
            marg = k.sb(stb, "marg", [128, 128, 8])
            ang = k.sb(stb, "ang", [128, 128, 8])
            ytm = k.sb(stb, "scf_y", [128, 128, 8])
            rtm = k.sb(stb, "scf_r", [128, 128, 8])
            for fam in range(3):
                pr, pi = pw[(fam, "r")], pw[(fam, "i")]
                k.tt(marg[:], expo[:, fam, :, :], bc_last(lr[:], 8), ALU.mult, [expo, lr], [marg])
                k.act(marg[:], marg[:], AF.Exp, [marg], [marg])
                k.tt(ang[:], expo[:, fam, :, :], bc_last(li[:], 8), ALU.mult, [expo, li], [ang])
                k.sincos2(ang[:], ytm, rtm, pi[:], pr[:], [ang], [pi], [pr])
                k.tt(pr[:], pr[:], marg[:], ALU.mult, [pr, marg], [pr])
                k.tt(pi[:], pi[:], marg[:], ALU.mult, [pi, marg], [pi])
                yield
        yield "barrier"

        def signed(src, col, name):
            t = k.sb(st, name, [128, 128, 8])
            k.ts(t[:], src[:], col, None, ALU.mult, None, [src, cols], [t])
            return t
        p0r, p0i = pw[(0, "r")], pw[(0, "i")]
        p0iB = signed(p0i, sB, "p0iB")
        p0rC = signed(p0r, sC, "p0rC")
        p1rC = signed(pw[(1, "r")], sC, "p1rC")
        p1iD = signed(pw[(1, "i")], sD, "p1iD")
        p2rC = signed(pw[(2, "r")], sC, "p2rC")
        p2iD = signed(pw[(2, "i")], sD, "p2iD")
        p2rD = signed(pw[(2, "r")], sD, "p2rD")
        p2iB = signed(pw[(2, "i")], sB, "p2iB")
        fams = {
            "T1": (p0r, xbb1, p0iB, xbb2),
            "T1sw": (p0rC, xbb2, p0i, xbb1),
            "R": (p1rC, xc1, p1iD, xc2),
            "G1": (p2rC, xc1, p2iD, xc2),
            "G2": (p2rD, xc2, p2iB, xc1),
        }
        stg = [k.sb(st, "s5stg%d" % i, [128, 8, 9, 128], BF16) for i in range(1)]
        mtmp = [k.sb(st, "mtmp%d" % i, [128, 8, 128]) for i in range(2)]
        fa = k.sb(st, "fam_a", [128, 8, 8, 16])
        fbt = k.sb(st, "fam_b", [128, 8, 8, 16])
        fa2 = k.sb(st, "fam_a2", [128, 8, 8, 16])
        fbt2 = k.sb(st, "fam_b2", [128, 8, 8, 16])
        t1b = [k.sb(st, "T1b%d" % i, [128, 8, 128], BF16) for i in range(2)]
        t1s = [k.sb(st, "T1s%d" % i, [128, 8, 128], BF16) for i in range(2)]
        rmb = [k.sb(st, "Rmb%d" % i, [128, 8, 128], BF16) for i in range(2)]
        mt2 = k.sb(st, "mt2", [128, 4, 128])
        it = 0
        for gb in range(8):
            g0 = gb * 8
            sg = stg[0]
            mt = mtmp[gb % 2]
            for d in range(2):
                gd0 = d * 64 + g0
                T1b, T1s, Rmb = t1b[it % 2], t1s[it % 2], rmb[it % 2]
                it += 1

                def fam(name, out_ap, outT, eng="vector"):
                    pa, xa, pb_, xb_ = fams[name]
                    fa_, fb_ = (fa, fbt) if eng == "vector" else (fa2, fbt2)
                    pa_bc = bc_last(pa[:, gd0:gd0 + 8, :], 16)
                    pb_bc = bc_last(pb_[:, gd0:gd0 + 8, :], 16)
                    xa_bc = xa[:, gd0:gd0 + 8, :].unsqueeze(2).broadcast_to([128, 8, 8, 16])
                    xb_bc = xb_[:, gd0:gd0 + 8, :].unsqueeze(2).broadcast_to([128, 8, 8, 16])
                    k.tt(fa_[:], pa_bc, xa_bc, ALU.mult, [pa, xa], [fa_], eng=eng)
                    k.tt(fb_[:], pb_bc, xb_bc, ALU.mult, [pb_, xb_], [fb_], eng=eng)
                    k.tt(out_ap, fa_[:], fb_[:], ALU.add, [fa_, fb_], [outT], eng=eng)
                v4 = lambda ap: ap.rearrange("p g (i c) -> p g i c", c=16)
                fam("T1", v4(T1b[:]), T1b)
                yield
                fam("T1sw", v4(T1s[:]), T1s)
                fam("R", v4(Rmb[:]), Rmb)
                yield
                fam("G1", v4(sg[:, :, 5 + 2 * d, :]), sg, eng="gpsimd")
                fam("G2", v4(sg[:, :, 6 + 2 * d, :]), sg)
                yield
                for h in range(2):
                    pbw = k.bank()
                    pbwb = pbw[:].bitcast(BF16)
                    pbm = k.bank()
                    for gi in range(4):
                        g = h * 4 + gi
                        k.tr(pbwb[:, (2 * gi) * 128:(2 * gi + 1) * 128], T1b[:, g, :], k.identb[:], [T1b, k.identb], [pbw])
                        k.tr(pbwb[:, (2 * gi + 1) * 128:(2 * gi + 2) * 128], T1s[:, g, :], k.identb[:], [T1s, k.identb], [pbw])
                        k.mm(pbm[:, gi * 128:(gi + 1) * 128], T1b[:, g, :], Rmb[:, g, :], True, True, [T1b, Rmb], [pbm])
                    k.cp(sg[:, h * 4:(h + 1) * 4, 1 + 2 * d:3 + 2 * d, :], pbwb.rearrange("p (g w c) -> p g w c", g=4, w=2), [pbw], [sg], eng="scalar")
                    msk = (maskf if d == 0 else maskb)
                    mbc = msk[:].unsqueeze(1).broadcast_to([128, 4, 128])
                    pm3 = pbm[:].rearrange("p (g c) -> p g c", g=4)
                    if d == 0:
                        k.tt(mt[:, h * 4:(h + 1) * 4, :], pm3, mbc, ALU.mult, [pbm, msk], [mt])
                    else:
                        k.tt(mt2[:], pm3, mbc, ALU.mult, [pbm, msk], [mt2])
                        k.tt(mt[:, h * 4:(h + 1) * 4, :], mt[:, h * 4:(h + 1) * 4, :], mt2[:], ALU.add, [mt, mt2], [mt])
                    yield
            for gi in range(8):
                k.stt(sg[:, gi, 0, :], k.ident[:], dcol[:, g0 + gi:g0 + gi + 1], mt[:, gi, :], ALU.mult, ALU.add, [k.ident, dcol, mt], [sg])
            k.sto(SC["S5WT"][g0:g0 + 8].rearrange("g p n -> p g n"), sg[:].rearrange("p g w c -> p g (w c)"), sg, [RS["S5WT"]])


def stage_s5prep(k, env, side=None):
    def main():
        for y in gen_s5prep(k, env):
            if y == "barrier":
                k.S.barrier()
            yield
    if side is None:
        for _ in main():
            pass
    else:
        next(side)
        run_with_side(main(), side, every=2)


def stage_s5(k, env):
    I, SC, RS = env["I"], env["SC"], env["RS"]
    idb = k.identb
    with ExitStack() as st:
        s5p = k.sb(st, "s5p", [128, 2, 128])
        k.ld(s5p[:], SC["S5P"], s5p, r=[RS["S5P"]])
        r8 = k.sb(st, "r8", [128, 128])
        k.act(r8[:], s5p[:, 0, :], AF.Exp, [s5p], [r8])
        x0 = k.sb(st, "x0", [128, 128])
        wts = [k.sb(st, "s5w%d" % i, [128, 9, 128], BF16) for i in range(2)]
        with ExitStack() as sc:
            tmc = k.sb(sc, "tmc", [32, 8, 1024], BF16)
            k.ld(tmc[:], SC["UC"].rearrange("(jj s) n -> jj s n", s=8), tmc, r=[RS["UC"]])
            ucs = k.sb(sc, "ucs", [128, 64, 32], BF16)
            tmcg = k.sb(sc, "tmcg", [32, 64, 128], BF16)
            k.cp(tmcg[:].rearrange("p g (s c) -> p s g c", c=16), tmc[:].rearrange("p s (g c) -> p s g c", c=16), [tmc], [tmcg])
            for gq in range(4):
                pb = k.bank()
                pbb = pb[:].bitcast(BF16)
                for gi in range(16):
                    g = gq * 16 + gi
                    k.tr(pbb[:, gi * 32:(gi + 1) * 32], tmcg[0:32, g, :], idb[0:32, 0:32], [tmcg, idb], [pb])
                k.cp(ucs[:, gq * 16:(gq + 1) * 16, :], pbb[:, 0:512].rearrange("p (g j) -> p g j", j=32), [pb], [ucs])
            kc = k.sb(sc, "kc", [128, 128, 32])
            k.ld(kc[:], I["KCTX"], kc)
            ppr = k.sb(sc, "ppr", [128, 128, 32])
            ppi = k.sb(sc, "ppi", [128, 128, 32])
            marg = k.sb(sc, "cmarg", [128, 128, 32])
            ytm_ = k.sb(sc, "cy", [128, 128, 32])
            rtm_ = k.sb(sc, "cr", [128, 128, 32])
            k.tt(marg[:], kc[:], bc_last(s5p[:, 0, :], 32), ALU.mult, [kc, s5p], [marg])
            k.act(marg[:], marg[:], AF.Exp, [marg], [marg])
            k.tt(ytm_[:], kc[:], bc_last(s5p[:, 1, :], 32), ALU.mult, [kc, s5p], [ytm_])
            k._sincos_tail(ytm_, rtm_, ppi[:], ppr[:], [ppi], [ppr])
            k.tt(ppr[:], ppr[:], marg[:], ALU.mult, [ppr, marg], [ppr])
            k.tt(ppi[:], ppi[:], marg[:], ALU.mult, [ppi, marg], [ppi])
            ct1 = [k.sb(sc, "ct1_%d" % i, [128, 2, 4, 32]) for i in range(2)]
            ct2 = [k.sb(sc, "ct2_%d" % i, [128, 2, 4, 32]) for i in range(2)]
            wc = [k.sb(sc, "s5wc%d" % i, [128, 9, 128], BF16) for i in range(8)]
            ppr4 = ppr[:].rearrange("p (d g) j -> p d g j", d=2)
            ppi4 = ppi[:].rearrange("p (d g) j -> p d g j", d=2)
            x04 = x0[:].rearrange("p (d g) -> p d g", d=2)
            for g4 in range(16):
                pb = k.bank()
                for gi in range(4):
                    g = g4 * 4 + gi
                    w = wc[g % 8]
                    k.ld(w[:], SC["S5WT"][g].rearrange("p (w c) -> p w c", c=128), w, r=[RS["S5WT"]])
                    for q in range(4):
                        k.mm(pb[:, gi * 128 + q * 32:gi * 128 + (q + 1) * 32], w[:, 1 + q, :], ucs[:, g, :], True, True, [w, ucs], [pb])
                v = pb[:].rearrange("p (gi d s j) -> p d gi s j", gi=4, d=2, s=2)
                a1, a2 = ct1[g4 % 2], ct2[g4 % 2]
                gs = slice(g4 * 4, g4 * 4 + 4)
                k.tt(a1[:], ppr4[:, :, gs, :], v[:, :, :, 0, :], ALU.mult, [ppr, pb], [a1])
                k.tt(a2[:], ppi4[:, :, gs, :], v[:, :, :, 1, :], ALU.mult, [ppi, pb], [a2])
                k.tt(a1[:], a1[:], a2[:], ALU.subtract, [a1, a2], [a1])
                k.rsum(x04[:, :, gs], a1[:], [a1], [x0])
        if "X0" in env.get("dbg", ()):
            k.sto(SC["X0"], x0[:], x0, [RS["X0"]])
        k.S.barrier()
        with ExitStack() as sm:
            tm = [k.sb(sm, "tm%d" % i, [128, 8, 1024], BF16) for i in range(2)]
            yt = [k.sb(sm, "ytm%d" % i, [128, 8, 1024], BF16) for i in range(2)]
            for jb in range(2):
                k.ld(tm[jb][:], SC["U"][jb * 1024:(jb + 1) * 1024, :].rearrange("(jj s) n -> jj s n", s=8), tm[jb], r=[RS["U"]])
            tmg = [k.sb(sm, "tmg%d" % i, [128, 64, 128], BF16) for i in range(2)]
            for jb in range(2):
                k.cp(tmg[jb][:].rearrange("p g (s c) -> p s g c", c=16), tm[jb][:].rearrange("p s (g c) -> p s g c", c=16), [tm[jb]], [tmg[jb]],
                     eng="scalar" if jb else "vector")
            iota = k.sb(sm, "iota", [128, 256])
            k.ld(iota[:], I["IOTA"], iota)
            tabc = [k.sb(sm, "tabc%d" % i, [128, 2, 4, 256]) for i in range(2)]
            tabs = [k.sb(sm, "tabs%d" % i, [128, 2, 4, 256]) for i in range(2)]
            ty = k.sb(sm, "ty", [128, 2, 4, 256])
            tr_ = k.sb(sm, "trr", [128, 2, 4, 256])
            ust = [k.sb(sm, "ust%d" % i, [128, 256], BF16) for i in range(2)]
            m1 = [k.sb(sm, "m1_%d" % i, [128, 256]) for i in range(2)]
            m2 = [k.sb(sm, "m2_%d" % i, [128, 256]) for i in range(2)]
            zz = [k.sb(sm, "zz%d" % i, [128, 256]) for i in range(2)]
            zc = [[k.sb(sm, "zc%d_%d" % (d, i), [128, 257], BF16) for i in range(2)] for d in range(2)]
            zs = [[k.sb(sm, "zs%d_%d" % (d, i), [128, 257], BF16) for i in range(2)] for d in range(2)]
            for d in range(2):
                for i in range(2):
                    k.memset(zs[d][i][:], 0.0, [zs[d][i]])
            k.brange = (0, 6)
            zz4 = [[k.sb(sm, "zz%d_%d" % (d, i), [128, 256]) for i in range(2)] for d in range(2)]
            yb = [k.banks[6], k.banks[7]]
            state = {}

            def tables(g4):
                tc_, tsn_ = tabc[g4 % 2], tabs[g4 % 2]
                phi4 = s5p[:, 1, :].rearrange("p (d g) -> p d g", d=2)[:, :, g4 * 4:(g4 + 1) * 4]
                k.tt(ty[:], iota[:].unsqueeze(1).unsqueeze(1).broadcast_to([128, 2, 4, 256]), bc_last(phi4, 256), ALU.mult, [iota, s5p], [ty])
                k._sincos_tail(ty, tr_, tsn_[:], tc_[:], [tsn_], [tc_])

            def front(g):
                w = wts[g % 2]
                k.ld(w[:], SC["S5WT"][g].rearrange("p (w c) -> p w c", c=128), w, r=[RS["S5WT"]])
                pbu = k.bank()
                pbub = pbu[:].bitcast(BF16)
                for jb in range(2):
                    k.tr(pbub[:, jb * 128:(jb + 1) * 128], tmg[jb][:, g, :], idb[:], [tmg[jb], idb], [pbu])
                us = ust[g % 2]
                k.cp(us[:], pbub[:, 0:256], [pbu], [us], eng="scalar")
                pbs2 = []
                for d in range(2):
                    pbs = k.bank()
                    k.mm(pbs[:, 0:256], w[:, 1 + 2 * d, :], us[:], True, True, [w, us], [pbs])
                    k.mm(pbs[:, 256:512], w[:, 2 + 2 * d, :], us[:], True, True, [w, us], [pbs])
                    pbs2.append(pbs)
                state[g] = (w, us, pbs2)

            def mid(g):
                w, us, pbs2 = state[g]
                g4, gi = g // 4, g % 4
                tc_, tsn_ = tabc[g4 % 2], tabs[g4 % 2]
                tcs = [tc_[:, 0, gi, :], tc_[:, 1, gi, ::-1]]
                tsns = [tsn_[:, 0, gi, :], tsn_[:, 1, gi, ::-1]]
                zcd = [zc[0][g % 2], zc[1][g % 2]]
                zsd = [zs[0][g % 2], zs[1][g % 2]]
                abz = [(m1[d], m2[d], zz4[d][g % 2]) for d in range(2)]
                gds = [g, 64 + g]
                for d in range(2):
                    k.tt(abz[d][0][:], tcs[d], pbs2[d][:, 0:256], ALU.mult, [tc_, pbs2[d]], [abz[d][0]])
                for d in range(2):
                    k.tt(abz[d][1][:], tsns[d], pbs2[d][:, 256:512], ALU.mult, [tsn_, pbs2[d]], [abz[d][1]])
                for d in range(2):
                    k.tt(abz[d][0][:], abz[d][0][:], abz[d][1][:], ALU.add, [abz[d][0], abz[d][1]], [abz[d][0]])
                for d in range(2):
                    a, z, gd = abz[d][0], abz[d][2], gds[d]
                    r8b = r8[:, gd:gd + 1].broadcast_to([128, 256])
                    if d == 0:
                        k.scan(z[:], r8b, a[:], x0[:, gd:gd + 1], [r8, a, x0], [z])
                    else:
                        k.scan(z[:, ::-1], r8b, a[:, ::-1], x0[:, gd:gd + 1], [r8, a, x0], [z])
                for d in range(2):
                    z = abz[d][2]
                    dst = zcd[d][:, 1:257] if d == 0 else zcd[d][:, 0:256]
                    k.tt(dst, tcs[d], z[:], ALU.mult, [tc_, z], [zcd[d]], eng="gpsimd" if d else "vector")
                for d in range(2):
                    z = abz[d][2]
                    dst = zsd[d][:, 1:257] if d == 0 else zsd[d][:, 0:256]
                    k.tt(dst, tsns[d], z[:], ALU.mult, [tsn_, z], [zsd[d]], eng="gpsimd" if d else "vector")
                k.cp(zcd[0][:, 0:1], x0[:, g:g + 1], [x0], [zcd[0]], eng="scalar")
                k.cp(zcd[1][:, 256:257], x0[:, 64 + g:65 + g], [x0], [zcd[1]], eng="scalar")
                state[g] = (w, us, zcd, zsd)

            def back(g):
                w, us, zcd, zsd = state.pop(g)
                g4, gi = g // 4, g % 4
                for jb in range(2):
                    o = yb[jb][:, gi * 128:(gi + 1) * 128]
                    c0 = jb * 128
                    k.mm(o, us[:, c0:c0 + 128], w[:, 0, :], True, False, [us, w], [yb[jb]])
                    k.mm(o, zcd[0][:, c0:c0 + 128], w[:, 5, :], False, False, [zcd[0], w], [yb[jb]])
                    k.mm(o, zsd[0][:, c0:c0 + 128], w[:, 6, :], False, False, [zsd[0], w], [yb[jb]])
                    k.mm(o, zcd[1][:, c0 + 1:c0 + 129], w[:, 7, :], False, False, [zcd[1], w], [yb[jb]])
                    k.mm(o, zsd[1][:, c0 + 1:c0 + 129], w[:, 8, :], False, True, [zsd[1], w], [yb[jb]])
                if gi == 3:
                    for jb in range(2):
                        src = yb[jb][:].rearrange("p (g t c) -> p t g c", g=4, t=8)
                        dst = yt[jb][:, :, g4 * 64:(g4 + 1) * 64].rearrange("p t (g c) -> p t g c", c=16)
                        k.act(dst, src, AF.Gelu, [yb[jb]], [yt[jb]])

            tables(0)
            front(0)
            for g in range(G):
                if g + 1 < G:
                    front(g + 1)
                if g % 4 == 0 and g // 4 + 1 < 16:
                    tables(g // 4 + 1)
                mid(g)
                back(g)
            k.brange = (0, 8)
            for jb in range(2):
                k.sto(SC["YS5"][jb * 1024:(jb + 1) * 1024, :].rearrange("(jj s) n -> jj s n", s=8), yt[jb][:], yt[jb], [RS["YS5"]])


def rms_rows(k, tmps, yt, width, grow, out_ap, outT):
    junk, ssq, _ = tmps
    k.act(junk[:, 0:width], yt[:, 0:width], AF.Square, [yt], [junk, ssq], accum=ssq[:])
    k.ts(ssq[:], ssq[:], 1.0 / width, EPS, ALU.mult, ALU.add, [ssq], [ssq])
    k.act(ssq[:], ssq[:], AF.Sqrt, [ssq], [ssq])
    k.recip(ssq[:], ssq[:], [ssq], [ssq])
    k.stt(out_ap, yt[:, 0:width], ssq[:, 0:1], grow, ALU.mult, ALU.mult, [yt, ssq], [outT])


def load_cast_weight(k, wdram, nkc, wb, stg, n0=0, ncols=None, c0=0):
    ncols = wdram.shape[1] if ncols is None else ncols
    q = 0
    for nb in range(ncols // 512):
        for kq in range(0, nkc, 4):
            sg = stg[q % len(stg)]
            q += 1
            k.ld(sg[:], wdram[kq * 128:(kq + 4) * 128, c0 + nb * 512:c0 + (nb + 1) * 512].rearrange("(kc p) n -> p kc n", p=128), sg)
            k.cp(wb[:, kq:kq + 4, n0 + nb * 512:n0 + (nb + 1) * 512], sg[:], [sg], [wb], eng="scalar" if q % 2 else "vector")


def stage_glu(k, env):
    I, SC, RS = env["I"], env["SC"], env["RS"]
    with ExitStack() as st:
        wgl = k.sb(st, "wgl", [128, 8, 2048], BF16)
        stg = [k.sb(st, "gstg%d" % i, [128, 4, 512]) for i in range(2)]
        load_cast_weight(k, I["glu_w"], 8, wgl, stg)
        glub = k.sb(st, "glub", [128, 2048])
        k.ld(glub[:], I["glu_b"].broadcast_to([128, 2048]), glub)
        bgs = k.sb(st, "bgs", [128, 1024])
        k.ld(bgs[:], I["rows1"][3:4, 0:1024].broadcast_to([128, 1024]), bgs)
        ytl = [k.sb(st, "gy%d" % i, [128, 1024], BF16) for i in range(2)]
        yT = [k.sb(st, "gyT%d" % i, [128, 8, 128], BF16) for i in range(2)]
        av = [k.sb(st, "ga%d" % i, [128, 1024]) for i in range(2)]
        sv = [k.sb(st, "gs%d" % i, [128, 1024]) for i in range(2)]
        mo = [k.sb(st, "gm%d" % i, [128, 1024], BF16) for i in range(2)]
        tmps = norm_tmps(k, st, "gl")
        def prep(tt):
            y, yt_ = ytl[tt % 2], yT[tt % 2]
            k.ld(y[:], SC["YS5"][tt * 128:(tt + 1) * 128, :], y, r=[RS["YS5"]])
            transpose_to(k, y, 8, yt_, 0)
        prep(0)
        for tt in range(16):
            y, yt_, a, sg_, m = ytl[tt % 2], yT[tt % 2], av[tt % 2], sv[tt % 2], mo[tt % 2]
            if tt + 1 < 16:
                prep(tt + 1)
            for nb in range(4):
                pb = k.bank()
                for kc in range(8):
                    k.mm(pb[:], yt_[:, kc, :], wgl[:, kc, nb * 512:(nb + 1) * 512], kc == 0, kc == 7, [yt_, wgl], [pb])
                if nb < 2:
                    k.tt(a[:, nb * 512:(nb + 1) * 512], pb[:], glub[:, nb * 512:(nb + 1) * 512], ALU.add, [pb, glub], [a])
                else:
                    k.tt(sg_[:, (nb - 2) * 512:(nb - 1) * 512], pb[:], glub[:, nb * 512:(nb + 1) * 512], ALU.add, [pb, glub], [sg_])
            k.act(sg_[:], sg_[:], AF.Sigmoid, [sg_], [sg_])
            k.tt(a[:], a[:], sg_[:], ALU.mult, [a, sg_], [a])
            rms_rows(k, tmps, a, 1024, bgs[:], m[:], m)
            k.sto(SC["MIX"][tt * 128:(tt + 1) * 128, 0:1024], m[:], m, [RS["MIX"]])


def stage_hyena(k, env):
    I, SC, RS = env["I"], env["SC"], env["RS"]
    cols = k.cols
    m0, m2 = cols[:, 4:5], cols[:, 5:6]
    with ExitStack() as st:
        z = k.sb(st, "hz", [128, 16, 512], BF16)
        xg = [k.sb(st, "hx%d" % i, [128, 16, 512]) for i in range(2)]
        yf = k.sb(st, "hyf", [128, 32, 512], BF16)
        for cb in range(2):
            sf = ExitStack()
            sf.__enter__()
            fwr = [k.sb(sf, "hfwr%d" % i, [128, 16, 128], BF16) for i in range(2)]
            fwi = [k.sb(sf, "hfwi%d" % i, [128, 16, 128], BF16) for i in range(2)]
            hre = [k.sb(sf, "hre%d" % i, [128, 512]) for i in range(2)]
            him = [k.sb(sf, "him%d" % i, [128, 512]) for i in range(2)]
            ya = [k.sb(sf, "hya%d" % i, [128, 512]) for i in range(2)]
            yb_ = [k.sb(sf, "hyb%d" % i, [128, 512]) for i in range(2)]
            yo = [k.sb(sf, "hyo%d" % i, [128, 512]) for i in range(2)]
            sa_ = ExitStack()
            sa_.__enter__()
            NB_ = 2
            cw = k.sb(sa_, "hcw", [128, 3, 3, 512])
            cbv = k.sb(sa_, "hcb", [128, 3, 512])
            cu = [k.sb(sa_, "hcu%d" % i, [128, 512]) for i in range(NB_)]
            t1 = [k.sb(sa_, "ht1%d" % i, [128, 512]) for i in range(NB_)]
            t2 = [k.sb(sa_, "ht2%d" % i, [128, 512]) for i in range(NB_)]
            t3 = [k.sb(sa_, "ht3%d" % i, [128, 512]) for i in range(NB_)]
            for sg in range(3):
                c0 = sg * 1024 + cb * 512
                for tap in range(3):
                    k.ld(cw[:, tap, sg, :], I["conv_w"][tap:tap + 1, c0:c0 + 512].broadcast_to([128, 512]), cw)
                k.ld(cbv[:, sg, :], I["conv_b"][0:1, c0:c0 + 512].broadcast_to([128, 512]), cbv)
            shp = k.sb(sa_, "shp", [128, 128])
            shn = k.sb(sa_, "shn", [128, 128])
            k.ld(shp[:], I["SHP"], shp)
            k.ld(shn[:], I["SHN"], shn)
            itc = [0]

            def conv_tile(sg, tt):
                c0 = sg * 1024 + cb * 512
                c_, a_, b_, e_ = (x[itc[0] % NB_] for x in (cu, t1, t2, t3))
                itc[0] += 1
                r0 = tt * 128
                k.ld(c_[:], SC["PH"][r0:r0 + 128, c0:c0 + 512], c_, r=[RS["PH"]])
                pp = k.bank()
                pn = k.bank()
                k.mm(pp[:], shp[:], c_[:], True, True, [shp, c_], [pp])
                k.mm(pn[:], shn[:], c_[:], True, True, [shn, c_], [pn])
                k.tt(b_[:], c_[:], cw[:, 1, sg, :], ALU.mult, [c_, cw], [b_], eng="gpsimd")
                k.tt(a_[:], pp[:], cw[:, 0, sg, :], ALU.mult, [pp, cw], [a_])
                k.tt(e_[:], pn[:], cw[:, 2, sg, :], ALU.mult, [pn, cw], [e_])
                k.tt(a_[:], a_[:], b_[:], ALU.add, [a_, b_], [a_])
                k.tt(e_[:], e_[:], cbv[:, sg, :], ALU.add, [e_, cbv], [e_])
                if sg == 0:
                    k.tt(z[:, tt, :], a_[:], e_[:], ALU.add, [a_, e_], [z])
                else:
                    k.tt(xg[sg - 1][:, tt, :], a_[:], e_[:], ALU.add, [a_, e_], [xg[sg - 1]])

            def conv_rest():
                for sg in (1, 2):
                    for tt in range(16):
                        conv_tile(sg, tt)
                        yield

            def fwd(o):
                hc0 = o * 1024 + cb * 512
                for kk in range(16):
                    fr, fi = fwr[kk % 2], fwi[kk % 2]
                    k.ld(fr[:], I["FWD"][kk], fr)
                    k.ld(fi[:], I["FWD"][16 + kk], fi)
                    hr, hi = hre[kk % 2], him[kk % 2]
                    k.ld(hr[:], SC["HF"][kk * 128:(kk + 1) * 128, hc0:hc0 + 512], hr, r=[RS["HF"]])
                    k.ld(hi[:], SC["HF"][2048 + kk * 128:2048 + (kk + 1) * 128, hc0:hc0 + 512], hi, r=[RS["HF"]])
                    pre = k.bank()
                    pim = k.bank()
                    for tt in range(16):
                        k.mm(pre[:], fr[:, tt, :], z[:, tt, :], tt == 0, tt == 15, [fr, z], [pre])
                    for tt in range(16):
                        k.mm(pim[:], fi[:, tt, :], z[:, tt, :], tt == 0, tt == 15, [fi, z], [pim])
                    a_, b_ = ya[kk % 2], yb_[kk % 2]
                    k.tt(a_[:], pre[:], hr[:], ALU.mult, [pre, hr], [a_])
                    k.tt(b_[:], pim[:], hi[:], ALU.mult, [pim, hi], [b_])
                    k.tt(yf[:, kk, :], a_[:], b_[:], ALU.subtract, [a_, b_], [yf])
                    c_, d_ = (yo[0], yo[1]) if kk == 0 else (a_, b_)
                    k.tt(c_[:], pre[:], hi[:], ALU.mult, [pre, hi], [c_])
                    k.tt(d_[:], pim[:], hr[:], ALU.mult, [pim, hr], [d_])
                    k.tt(yf[:, 16 + kk, :], c_[:], d_[:], ALU.add, [c_, d_], [yf])
                    if kk == 0:
                        k.cp(yf[0:1, 0, :], a_[0:1, :], [a_, yf], [yf])
                        k.cp(yf[0:1, 16, :], b_[0:1, :], [b_, yf], [yf])
                    yield

            for tt in range(16):
                conv_tile(0, tt)
            cg = conv_rest()
            for _ in fwd(0):
                for _j in range(2):
                    next(cg, None)
            for _ in cg:
                pass
            sa_.__exit__(None, None, None)
            k.S.barrier()
            si = ExitStack()
            si.__enter__()
            ivt = [k.sb(si, "hivt%d" % i, [128, 32, 128], BF16) for i in range(2)]
            for o in range(2):
                if o == 1:
                    for _ in fwd(1):
                        pass
                for tt in range(16):
                    iv = ivt[tt % 2]
                    k.ld(iv[:], I["INV"][tt], iv)
                    pb = k.bank()
                    for m in range(32):
                        k.mm(pb[:], iv[:, m, :], yf[:, m, :], m == 0, m == 31, [iv, yf], [pb])
                    if o == 0:
                        k.tt(z[:, tt, :], pb[:], xg[0][:, tt, :], ALU.mult, [pb, xg[0]], [z])
                    else:
                        y_ = yo[tt % 2]
                        k.tt(y_[:], pb[:], xg[1][:, tt, :], ALU.mult, [pb, xg[1]], [y_])
                        k.sto(SC["YH"][tt * 128:(tt + 1) * 128, cb * 512:(cb + 1) * 512], y_[:], y_, [RS["YH"]])
            si.__exit__(None, None, None)
            sf.__exit__(None, None, None)
            k.S.barrier()
    k.S.barrier()
    with ExitStack() as st:
        bgh = k.sb(st, "bgh", [128, 1024])
        k.ld(bgh[:], I["rows1"][3:4, 1024:2048].broadcast_to([128, 1024]), bgh)
        tmps = norm_tmps(k, st, "hy")
        yv = [k.sb(st, "hyv%d" % i, [128, 1024]) for i in range(2)]
        mo = [k.sb(st, "hmo%d" % i, [128, 1024], BF16) for i in range(2)]
        for tt in range(16):
            y, m = yv[tt % 2], mo[tt % 2]
            k.ld(y[:], SC["YH"][tt * 128:(tt + 1) * 128, :], y, r=[RS["YH"]])
            rms_rows(k, tmps, y, 1024, bgh[:], m[:], m)
            k.sto(SC["MIX"][tt * 128:(tt + 1) * 128, 1024:2048], m[:], m, [RS["MIX"]])


def stage_out(k, env):
    I, SC, RS = env["I"], env["SC"], env["RS"]
    with ExitStack() as st:
        wo = k.sb(st, "wo", [128, NK, 2048], BF16)
        stg = [k.sb(st, "ostg%d" % i, [128, 4, 512]) for i in range(2)]
        load_cast_weight(k, I["w_out"], NK, wo, stg)
        g1row = k.sb(st, "g1row", [128, D])
        k.ld(g1row[:], SC["MOD"][0:1, 2 * D:3 * D].broadcast_to([128, D]), g1row, r=[RS["MOD"]])
        mx = [k.sb(st, "omx%d" % i, [128, D], BF16) for i in range(2)]
        mT = [k.sb(st, "omT%d" % i, [128, NK, 128], BF16) for i in range(2)]
        xt = [k.sb(st, "oxt%d" % i, [128, D]) for i in range(2)]
        x1 = [k.sb(st, "ox1%d" % i, [128, D]) for i in range(2)]
        srow2 = k.sb(st, "o_srow2", [128, D])
        shrow2 = k.sb(st, "o_shrow2", [128, D])
        n2 = k.sb(st, "o_n2", [128, D])
        k.ld(srow2[:], SC["MOD"][0:1, 4 * D:5 * D].broadcast_to([128, D]), srow2, r=[RS["MOD"]])
        k.ld(shrow2[:], SC["MOD"][0:1, 3 * D:4 * D].broadcast_to([128, D]), shrow2, r=[RS["MOD"]])
        k.ld(n2[:], I["rows1"][1:2, :].broadcast_to([128, D]), n2)
        k.stt(srow2[:], srow2[:], 1.0, n2[:], ALU.add, ALU.mult, [srow2, n2], [srow2])
        tmps2 = norm_tmps(k, st, "o2")
        hb2 = [k.sb(st, "o_hb%d" % i, [128, D], BF16) for i in range(2)]
        def prep(tt):
            m, mt, x = mx[tt % 2], mT[tt % 2], xt[tt % 2]
            k.ld(m[:], SC["MIX"][tt * 128:(tt + 1) * 128, :], m, r=[RS["MIX"]])
            k.ld(x[:], I["x"][tt * 128:(tt + 1) * 128, :], x)
            transpose_to(k, m, NK, mt, 0)
        prep(0)
        for tt in range(16):
            m, mt, x, o = mx[tt % 2], mT[tt % 2], xt[tt % 2], x1[tt % 2]
            if tt + 1 < 16:
                prep(tt + 1)
            for nb in range(4):
                pb = k.bank()
                for kc in range(NK):
                    k.mm(pb[:], mt[:, kc, :], wo[:, kc, nb * 512:(nb + 1) * 512], kc == 0, kc == NK - 1, [mt, wo], [pb])
                sl = slice(nb * 512, (nb + 1) * 512)
                k.tt(o[:, sl], pb[:], g1row[:, sl], ALU.mult, [pb, g1row], [o])
                k.tt(o[:, sl], o[:, sl], x[:, sl], ALU.add, [o, x], [o])
            k.sto(SC["X1"][tt * 128:(tt + 1) * 128, :], o[:], o, [RS["X1"]])
            h_ = hb2[tt % 2]
            norm_rows(k, tmps2, o, 128, srow2, shrow2, h_)
            k.sto(SC["H2"][tt * 128:(tt + 1) * 128, :], h_[:], h_, [RS["H2"]])


def stage_ffn(k, env):
    I, SC, RS, OUT = env["I"], env["SC"], env["RS"], env["OUT"]
    RO = Res("OUT")
    NT = T // 128
    sto_ = ExitStack()
    sto_.__enter__()
    wdb0 = k.sb(sto_, "f_wdb0", [128, NFC, 512], BF16)
    stg = [k.sb(sto_, "f_stg%d" % i, [128, 4, 512]) for i in range(2)]
    with ExitStack() as st:
        h2T = k.sb(st, "h2T", [128, NK, T], BF16)
        with ExitStack() as s0:
            hb = [k.sb(s0, "f_h%d" % i, [128, D], BF16) for i in range(2)]
            for tt in range(NT):
                h = hb[tt % 2]
                k.ld(h[:], SC["H2"][tt * 128:(tt + 1) * 128, :], h, r=[RS["H2"]])
                transpose_to(k, h, NK, h2T, tt * 128)
        k.S.barrier()
        with ExitStack() as sa:
            wgs = [k.sb(sa, "f_wgs%d" % i, [128, NK, 128]) for i in range(2)]
            wus = [k.sb(sa, "f_wus%d" % i, [128, NK, 128]) for i in range(2)]
            wgb = [k.sb(sa, "f_wgb%d" % i, [128, NK, 128], BF16) for i in range(2)]
            wub = [k.sb(sa, "f_wub%d" % i, [128, NK, 128], BF16) for i in range(2)]
            sil = [k.sb(sa, "f_sil%d" % i, [128, 512]) for i in range(2)]
            hc = [k.sb(sa, "f_hc%d" % i, [128, T], BF16) for i in range(2)]
            it = 0
            for fc in range(NFC):
                a, b, ab, bb, hcc = wgs[fc % 2], wus[fc % 2], wgb[fc % 2], wub[fc % 2], hc[fc % 2]
                k.ld(a[:], I["wg"][:, fc * 128:(fc + 1) * 128].rearrange("(kc p) n -> p kc n", p=128), a)
                k.ld(b[:], I["wu"][:, fc * 128:(fc + 1) * 128].rearrange("(kc p) n -> p kc n", p=128), b)
                k.cp(ab[:], a[:], [a], [ab], eng="scalar")
                k.cp(bb[:], b[:], [b], [bb], eng="vector")
                for tb in range(T // 512):
                    sl_ = sil[it % 2]
                    it += 1
                    pg = k.bank()
                    pu = k.bank()
                    ts_ = slice(tb * 512, (tb + 1) * 512)
                    for kc in range(NK):
                        k.mm(pg[:], ab[:, kc, :], h2T[:, kc, ts_], kc == 0, kc == NK - 1, [ab, h2T], [pg])
                    for kc in range(NK):
                        k.mm(pu[:], bb[:, kc, :], h2T[:, kc, ts_], kc == 0, kc == NK - 1, [bb, h2T], [pu])
                    k.act(sl_[:], pg[:], AF.Silu, [pg], [sl_])
                    k.tt(hcc[:, ts_], sl_[:], pu[:], ALU.mult, [sl_, pu], [hcc])
                k.sto(SC["HTS"][:, :, fc, :].rearrange("tt p t -> p tt t"), hcc[:].rearrange("p (tt t) -> p tt t", t=128), hcc, [RS["HTS"]])
                if fc == 16:
                    load_cast_weight(k, I["wd"], NFC, wdb0, stg, n0=0, ncols=512, c0=0)
    k.S.barrier()
    with ExitStack() as st:
        wdb = [wdb0, k.sb(st, "f_wdb1", [128, NFC, 512], BF16)]
        hts = [k.sb(st, "f_hts%d" % i, [128, NFC, 128], BF16) for i in range(2)]
        g2row = k.sb(st, "g2row", [128, D])
        k.ld(g2row[:], SC["MOD"][0:1, 5 * D:6 * D].broadcast_to([128, D]), g2row, r=[RS["MOD"]])
        x1p = [k.sb(st, "f_x1p%d" % i, [128, 512]) for i in range(2)]
        op_ = [k.sb(st, "f_op%d" % i, [128, 512]) for i in range(2)]
        finrow = k.sb(st, "finrow", [128, D])
        k.ld(finrow[:], I["rows1"][2:3, :].broadcast_to([128, D]), finrow)
        tmps = norm_tmps(k, st, "f1")
        xs = [k.sb(st, "f_x2%d" % i, [128, D]) for i in range(2)]
        os_ = [k.sb(st, "f_o%d" % i, [128, D]) for i in range(2)]
        it = 0
        for nb in range(4):
            wb_ = wdb[nb % 2]
            if nb + 1 < 4:
                load_cast_weight(k, I["wd"], NFC, wdb[(nb + 1) % 2], stg, n0=0, ncols=512, c0=(nb + 1) * 512)
            for tt in range(NT):
                h_, xp, o_ = hts[it % 2], x1p[it % 2], op_[it % 2]
                it += 1
                k.ld(h_[:], SC["HTS"][tt], h_, r=[RS["HTS"]])
                k.ld(xp[:], SC["X1"][tt * 128:(tt + 1) * 128, nb * 512:(nb + 1) * 512], xp, r=[RS["X1"]])
                pb = k.bank()
                for fc in range(NFC):
                    k.mm(pb[:], h_[:, fc, :], wb_[:, fc, :], fc == 0, fc == NFC - 1, [h_, wb_], [pb])
                k.tt(o_[:], pb[:], g2row[:, nb * 512:(nb + 1) * 512], ALU.mult, [pb, g2row], [o_])
                if nb < 3:
                    k.tt(o_[:], o_[:], xp[:], ALU.add, [o_, xp], [o_])
                    k.sto(SC["X2"][tt * 128:(tt + 1) * 128, nb * 512:(nb + 1) * 512], o_[:], o_, [RS["X2"]], tracked=True)
                else:
                    x, o = xs[tt % 2], os_[tt % 2]
                    k.ld(x[:, 0:1536], SC["X2"][tt * 128:(tt + 1) * 128, 0:1536], x, r=[RS["X2"]])
                    k.tt(x[:, 1536:2048], o_[:], xp[:], ALU.add, [o_, xp], [x])
                    rms_rows(k, tmps, x, D, finrow[:], o[:], o)
                    k.sto(OUT[tt * 128:(tt + 1) * 128, :], o[:], o, [RO])
    sto_.__exit__(None, None, None)


def kernel(**inputs):
    shared = prep_shared(inputs)
    x = np.asarray(inputs["x"], np.float32)
    c = np.asarray(inputs["c"], np.float32)
    ctx = np.asarray(inputs["ctx"], np.float32)
    c_ctx = np.asarray(inputs["c_ctx"], np.float32)
    ncores = 8
    nc = build_program()
    in_maps = []
    for b in range(ncores):
        m = dict(shared)
        m["x"] = np.ascontiguousarray(x[b])
        m["ctx"] = np.ascontiguousarray(ctx[b])
        cv = np.stack([c[b].reshape(NK, 128).T, c_ctx.reshape(NK, 128).T], axis=-1)
        m["cvec"] = np.ascontiguousarray(cv.astype(np.float32))
        in_maps.append(m)
    res = run_bass_kernel_spmd(nc, in_maps, core_ids=list(range(ncores)))
    out = np.stack([np.asarray(r["out"], np.float32) for r in res.results], axis=0)
    return out
```

```python
import os
import math
from contextlib import ExitStack

import numpy as np
import ml_dtypes

import concourse.bass as bass
import concourse.mybir as mybir
from concourse.bass_utils import run_bass_kernel_spmd

F32 = mybir.dt.float32
BF16 = mybir.dt.bfloat16
AF = mybir.ActivationFunctionType
ALU = mybir.AluOpType

D = 2048
T = 2048
CT = 256
NK = 16
S5W = 1024
HYW = 1024
G = 64
DFF = 5632
NFC = DFF // 128
EPS = 1e-6
MAGIC = 12582912.0
TWO_PI_LO = 6.28318
HALF_PI_LO = 1.570795

ENGS = ["tensor", "vector", "scalar", "gpsimd", "sync"]
CONV_MODE = "dram"
CONV_NBUF = 3


class Res:
    _n = 0

    def __init__(self, name):
        Res._n += 1
        self.id = Res._n
        self.name = name
        self.w = None
        self.r = []
        self.sem = None
        self.semid = None
        self.dtot = 0


class Tl:
    def __init__(self, t, R):
        self.t = t
        self.R = R

    def __getitem__(self, k):
        return self.t[k]


def _R(x):
    return x.R if isinstance(x, Tl) else x


class Sched:
    def __init__(self, nc, stack):
        self.nc = nc
        self.stack = stack
        self.ops = {e: [] for e in ENGS}
        self.cnt = {e: 0 for e in ENGS}
        self.esem = {e: stack.enter_context(nc.semaphore("es_" + e)) for e in ENGS if e != "sync"}
        self.waited = {e: {} for e in ENGS}
        self.dres = []
        self.pool = {"sync": [], "gpsimd": [], "scalar": []}
        self.nsem = 0
        self.epoch = 0
        self.bar_tile = stack.enter_context(nc.sbuf_tensor("bar_tile", [128, 1], F32))

    def _need(self, eng, ev):
        if ev is None:
            return
        if ev[0] == "E":
            q, v = ev[1], ev[2]
            if q == eng and (q == "tensor" or (q in ("vector", "scalar") and len(ev) > 3 and not ev[3])):
                return
            if self.waited[eng].get(q, 0) >= v:
                return
            self.waited[eng][q] = v
            sem = self.esem[q]
            self.ops[eng].append(lambda e, sem=sem, v=v: e.wait_ge(sem, v))
        else:
            r = ev[1]
            if ev[2] < self.epoch or r.sem is None:
                return
            v = ev[3] if len(ev) > 3 else r.dtot
            key = ("S", r.semid)
            if self.waited[eng].get(key, 0) >= v:
                return
            self.waited[eng][key] = v
            sem = r.sem
            self.ops[eng].append(lambda e, sem=sem, v=v: e.wait_ge(sem, v))

    def _deps(self, eng, reads, writes):
        for r in reads:
            self._need(eng, r.w)
        for r in writes:
            self._need(eng, r.w)
            for ev in r.r:
                self._need(eng, ev)

    def _commit(self, ev, reads, writes):
        for r in reads:
            r.r.append(ev)
            if len(r.r) > 16:
                best = {}
                keep = []
                for x in r.r:
                    if x[0] == "E":
                        if best.get(x[1], 0) < x[2]:
                            best[x[1]] = x[2]
                    elif x[2] >= self.epoch and x not in keep:
                        keep.append(x)
                r.r = keep + [("E", q, v, True) for q, v in best.items()]
        for r in writes:
            r.w = ev
            r.r = []

    def op(self, eng, fn, reads=(), writes=(), small=True):
        reads = [_R(x) for x in reads]
        writes = [_R(x) for x in writes]
        self._deps(eng, reads, writes)
        self.cnt[eng] += 1
        v = self.cnt[eng]
        sem = self.esem[eng]
        self.ops[eng].append(lambda e, fn=fn, sem=sem: fn(e).then_inc(sem, 1))
        self._commit(("E", eng, v, small), reads, writes)

    def dma(self, q, out, in_, reads=(), writes=(), owner=None, exact=False):
        reads = [_R(x) for x in reads]
        writes = [_R(x) for x in writes]
        if owner is None:
            owner = writes[0] if writes else reads[0]
        owner = _R(owner)
        if owner.sem is None:
            if self.pool[q]:
                owner.sem, owner.semid, owner.dtot = self.pool[q].pop()
            else:
                self.nsem += 1
                owner.sem = self.stack.enter_context(self.nc.semaphore("ds_%d" % self.nsem))
                owner.semid = self.nsem
                owner.dtot = 0
            owner.semq = q
            self.dres.append(owner)
        assert owner.semq == q, "DMA owner semaphore shared between queues"
        self._deps(q, reads, writes)
        owner.dtot += 16
        sem = owner.sem
        self.ops[q].append(lambda e, out=out, in_=in_, sem=sem: e.dma_start(out=out, in_=in_).then_inc(sem, 16))
        if exact:
            self._commit(("D", owner, self.epoch, owner.dtot), reads, writes)
        else:
            self._commit(("D", owner, self.epoch), reads, writes)

    def barrier(self):
        c = "gpsimd"
        for q in ENGS:
            if q not in ("sync", c) and self.cnt[q] > 0:
                self._need(c, ("E", q, self.cnt[q]))
        for r in self.dres:
            self._need(c, ("D", r, self.epoch))
        bt = self.bar_tile
        self.op(c, lambda e: e.memset(bt[:], 0.0), [], [])
        v = self.cnt[c]
        for e in ENGS:
            if e != c:
                self._need(e, ("E", c, v))
            for q in ENGS:
                if q != "sync":
                    self.waited[e][q] = max(self.waited[e].get(q, 0), self.cnt[q])
        for r in self.dres:
            self.pool[r.semq].append((r.sem, r.semid, r.dtot))
            r.sem = None
        self.dres = []
        self.epoch += 1

    def finish(self):
        self.barrier()
        nc = self.nc
        ops = self.ops
        with nc.Block() as block:
            @block.tensor
            def _(e):
                for f in ops["tensor"]:
                    f(e)

            @block.vector
            def _(e):
                for f in ops["vector"]:
                    f(e)

            @block.scalar
            def _(e):
                for f in ops["scalar"]:
                    f(e)

            @block.gpsimd
            def _(e):
                for f in ops["gpsimd"]:
                    f(e)

            @block.sync
            def _(e):
                for f in ops["sync"]:
                    f(e)


class K:
    def __init__(self, nc, st):
        self.nc = nc
        self.st = st
        self.S = Sched(nc, st)
        self.banks = []
        for i in range(8):
            t = st.enter_context(nc.psum_tensor("psb%d" % i, [128, 512], F32))
            self.banks.append(Tl(t, Res("psb%d" % i)))
        self.bi = 0
        self.brange = (0, 8)
        self.nuniq = 0

    def bank(self):
        lo, hi = self.brange
        b = self.banks[lo + self.bi % (hi - lo)]
        self.bi += 1
        return b

    def sb(self, st, name, shape, dt=F32):
        self.nuniq += 1
        nm = "%s_%d" % (name, self.nuniq)
        return Tl(st.enter_context(self.nc.sbuf_tensor(nm, list(shape), dt)), Res(nm))

    def ld(self, out, in_, w, r=()):
        self.S.dma("sync", out, in_, reads=r, writes=[w])

    def sto(self, out, in_, r, w=(), own_src=False, tracked=False):
        rl = [r] if not isinstance(r, (list, tuple)) else list(r)
        if not tracked and not own_src:
            if not hasattr(self, "_rot"):
                self._rot = [Res("rot%d" % i) for i in range(4)]
                self._nrot = 0
            owner = self._rot[self._nrot % 4]
            self._nrot += 1
            self.S.dma("gpsimd", out, in_, reads=rl, writes=[], owner=owner, exact=True)
            return
        rl = [r] if not isinstance(r, (list, tuple)) else list(r)
        owner = w[0] if w else None
        if own_src:
            src = _R(rl[0])
            if src.sem is None or getattr(src, "semq", None) == "gpsimd":
                owner = src
        self.S.dma("gpsimd", out, in_, reads=rl, writes=list(w), owner=owner)

    @staticmethod
    def _small(ap):
        n = 1
        for d in list(ap.shape)[1:]:
            n *= int(d)
        return n < 128

    def cp(self, out, in_, r, w, eng="vector"):
        if eng == "scalar":
            self.S.op(eng, lambda e: e.activation(out=out, in_=in_, func=AF.Copy), r, w, small=self._small(out))
        else:
            self.S.op(eng, lambda e: e.tensor_copy(out=out, in_=in_), r, w, small=self._small(out))

    def tt(self, out, in0, in1, op, r, w, eng="vector"):
        self.S.op(eng, lambda e: e.tensor_tensor(out=out, in0=in0, in1=in1, op=op), r, w, small=self._small(out))

    def ts(self, out, in0, s1, s2, op0, op1, r, w, eng="vector"):
        sm = self._small(out)
        if s2 is None:
            self.S.op(eng, lambda e: e.tensor_scalar(out=out, in0=in0, scalar1=s1, scalar2=None, op0=op0), r, w, small=sm)
        else:
            self.S.op(eng, lambda e: e.tensor_scalar(out=out, in0=in0, scalar1=s1, scalar2=s2, op0=op0, op1=op1), r, w, small=sm)

    def stt(self, out, in0, scalar, in1, op0, op1, r, w):
        self.S.op("vector", lambda e: e.scalar_tensor_tensor(out=out, in0=in0, scalar=scalar, in1=in1, op0=op0, op1=op1), r, w,
                  small=self._small(out))

    def act(self, out, in_, func, r, w, scale=1.0, bias=None, accum=None):
        def f(e):
            kw = {}
            if bias is not None:
                kw["bias"] = bias
            if accum is not None:
                kw["accum_out"] = accum
            return e.activation(out=out, in_=in_, func=func, scale=scale, **kw)
        self.S.op("scalar", f, r, w, small=(accum is not None) or self._small(out))

    def mm(self, out, lhsT, rhs, start, stop, r, w):
        self.S.op("tensor", lambda e: e.matmul(out, lhsT=lhsT, rhs=rhs, start=start, stop=stop), r, w)

    def tr(self, out, in_, ident, r, w):
        self.S.op("tensor", lambda e: e.transpose(out, in_, ident), r, w)

    def recip(self, out, in_, r, w):
        self.S.op("vector", lambda e: e.reciprocal(out=out, in_=in_), r, w)

    def memset(self, out, val, w, eng="vector"):
        self.S.op(eng, lambda e: e.memset(out, val), [], w)

    def scan(self, out, d0, d1, init, r, w):
        self.S.op("vector", lambda e: e.tensor_tensor_scan(out=out, data0=d0, data1=d1, initial=init, op0=ALU.mult, op1=ALU.add), r, w,
                  small=False)

    def rsum(self, out, in_, r, w):
        self.S.op("vector", lambda e: e.reduce_sum(out=out, in_=in_, axis=mybir.AxisListType.X), r, w)

    def frac_turns(self, y, rr, yT, rrT):
        self.ts(rr, y, MAGIC, None, ALU.add, None, [yT], [rrT])
        self.ts(rr, rr, MAGIC, None, ALU.subtract, None, [rrT], [rrT])
        self.tt(y, y, rr, ALU.subtract, [yT, rrT], [yT])

    def sincos(self, st, ang, shape, sin_out, cos_out, r, w_sin, w_cos, tag):
        y = self.sb(st, tag + "_y", shape)
        rr = self.sb(st, tag + "_r", shape)
        self.sincos2(ang, y, rr, sin_out, cos_out, r, w_sin, w_cos)

    def sincos2(self, ang, y, rr, sin_out, cos_out, r, w_sin, w_cos, turns=False):
        if turns:
            self.cp(y[:], ang, r, [y])
        else:
            self.ts(y[:], ang, 1.0 / (2 * math.pi), None, ALU.mult, None, r, [y])
        self._sincos_tail(y, rr, sin_out, cos_out, w_sin, w_cos)

    def _sincos_tail(self, y, rr, sin_out, cos_out, w_sin, w_cos):
        r = None
        self.ts(rr[:], y[:], MAGIC, None, ALU.add, None, [y], [rr])
        self.ts(rr[:], rr[:], MAGIC, None, ALU.subtract, None, [rr], [rr])
        self.tt(y[:], y[:], rr[:], ALU.subtract, [y, rr], [y])
        if sin_out is not None:
            self.act(sin_out, y[:], AF.Sin, [y], w_sin, scale=TWO_PI_LO)
        if cos_out is not None:
            self.stt(rr[:], y[:], -1.0, y[:], ALU.mult, ALU.max, [y], [rr])
            self.act(cos_out, rr[:], AF.Sin, [rr], w_cos, scale=-TWO_PI_LO, bias=self.halfpi[:, 0:1])


_CONST_CACHE = {}


def host_consts():
    if _CONST_CACHE:
        return _CONST_CACHE
    N = 4096
    n = 2048
    idx = np.arange(n, dtype=np.int64)
    ang = 2.0 * np.pi * ((idx[:, None] * idx[None, :]) % N).astype(np.float64) / N
    cs = np.cos(ang)
    sn = np.sin(ang)
    alt = (-1.0) ** idx
    fwd = np.empty((n, N), np.float64)
    fwd[:, :n] = cs
    fwd[:, n:] = -sn
    fwd[:, n] = alt
    inv = np.empty((N, n), np.float64)
    inv[:n] = (2.0 / N) * cs
    inv[0] = 1.0 / N
    inv[n:] = -(2.0 / N) * sn
    inv[n] = alt / N
    c = _CONST_CACHE
    fw16 = fwd.astype(np.float32).astype(ml_dtypes.bfloat16)
    iv16 = inv.astype(np.float32).astype(ml_dtypes.bfloat16)
    c["FWD"] = np.ascontiguousarray(fw16.reshape(16, 128, 32, 128).transpose(2, 1, 0, 3))
    c["INV"] = np.ascontiguousarray(iv16.reshape(32, 128, 16, 128).transpose(2, 1, 0, 3))
    pos = np.arange(n, dtype=np.float32)
    tt = pos[:, None] / np.float32(n)
    bands = np.linspace(1e-4, 15, 16, dtype=np.float32)
    a2 = (np.float32(2.0 * math.pi) * pos[:, None] * bands[None, :] / np.float32(n)).astype(np.float32)
    z = np.concatenate([tt, np.cos(a2), -np.sin(a2)], axis=-1).astype(np.float32)
    c["ZT"] = np.ascontiguousarray(z.T)
    c["IDENT"] = np.eye(128, dtype=np.float32)
    c["IDENTB"] = np.eye(128, dtype=np.float32).astype(ml_dtypes.bfloat16)
    sig = np.arange(128) // 16
    c["MASKF"] = (sig[None, :] >= sig[:, None]).astype(np.float32)
    c["MASKB"] = (sig[:, None] >= sig[None, :]).astype(np.float32)
    ex = np.zeros((3, 128, 8), np.float32)
    i8 = np.arange(8, dtype=np.float32)
    ex[0, :64] = 7 - i8
    ex[0, 64:] = i8
    ex[1, :64] = i8 - 7
    ex[1, 64:] = -i8
    ex[2, :64] = i8 + 1
    ex[2, 64:] = 8 - i8
    c["EXPO"] = np.ascontiguousarray(np.broadcast_to(ex[None], (128, 3, 128, 8))).astype(np.float32)
    cols = np.zeros((128, 16), np.float32)
    p = np.arange(128)
    cols[:, 0] = 1.0
    cols[:, 1] = np.where(p < 64, -1.0, 1.0)
    cols[:, 2] = np.where(p < 64, 1.0, -1.0)
    cols[:, 3] = -1.0
    cols[:, 4] = (p % 64 != 0)
    cols[:, 5] = (p % 64 != 63)
    cols[:, 6] = HALF_PI_LO
    cols[:, 7] = (p != 0)
    cols[:, 8] = alt[:128]
    for q in range(16):
        pass
    c["COLS"] = cols
    shp = np.zeros((128, 128), np.float32)
    shn = np.zeros((128, 128), np.float32)
    for m in range(128):
        if m % 64 != 0:
            shp[m - 1, m] = 1.0
        if m % 64 != 63:
            shn[m + 1, m] = 1.0
    c["SHP"] = shp
    c["SHN"] = shn
    c["POSF"] = (-(np.arange(16)[None, :] * 128 + p[:, None]) / 2048.0).astype(np.float32)
    c["IOTA"] = np.ascontiguousarray(np.broadcast_to(np.arange(1, 257, dtype=np.float32)[None], (128, 256)))
    kk = np.zeros((128, 128, 32), np.float32)
    kk[:, :64] = 31 - np.arange(32)
    kk[:, 64:] = np.arange(32)
    c["KCTX"] = kk
    return c


def prep_shared(inp):
    f = lambda k: np.ascontiguousarray(np.asarray(inp[k], np.float32))
    o = {}
    o["ada_w"] = f("ada_w")[0]
    o["ada_b"] = f("ada_b")
    o["w_in"] = f("w_in")[0]
    o["rows1"] = np.ascontiguousarray(np.stack([f("norm1_g")[0], f("norm2_g")[0], f("final_g"),
                                                np.concatenate([f("branch_g_s5")[0], f("branch_g_hy")[0]])]))
    o["conv_w"] = f("conv_w")[0]
    o["conv_b"] = f("conv_b")
    o["hy_w1"] = f("hy_w1")[0]
    o["hy_w2"] = f("hy_w2")[0]
    o["hy_w3"] = f("hy_w3")[0]
    o["hy_cols"] = np.ascontiguousarray(np.stack([f("hy_b1")[0], f("hy_b2")[0], f("hy_sin_freq")[0, 0], f("hy_sin_freq")[0, 1]], axis=1))
    o["hy_decay"] = f("hy_decay").reshape(1, 4096)
    o["hy_bias"] = f("hy_bias").reshape(1, 2048)
    lr = f("s5_lam_re")[0].reshape(128, 64).T
    li = f("s5_lam_im")[0].reshape(128, 64).T
    o["LAMR2"] = np.ascontiguousarray(np.concatenate([lr, lr], 0))
    o["LAMI2"] = np.ascontiguousarray(np.concatenate([li, li], 0))
    o["LDT"] = np.ascontiguousarray(np.broadcast_to(f("s5_log_dt")[0].reshape(1, 128), (128, 128)))
    br = f("s5_b_re")[0].reshape(128, 64, 16).transpose(1, 0, 2)
    bi = f("s5_b_im")[0].reshape(128, 64, 16).transpose(1, 0, 2)
    o["XB1"] = np.ascontiguousarray(np.concatenate([br, bi], 0))
    o["XB2"] = np.ascontiguousarray(np.concatenate([bi, br], 0))
    cr = f("s5_c_re")[0].reshape(128, 16, 64).transpose(2, 0, 1)
    ci = f("s5_c_im")[0].reshape(128, 16, 64).transpose(2, 0, 1)
    o["XC1"] = np.ascontiguousarray(np.concatenate([cr, ci], 0))
    o["XC2"] = np.ascontiguousarray(np.concatenate([ci, cr], 0))
    dd = f("s5_d")[0]
    o["DCOL"] = np.ascontiguousarray(np.tile(dd.T, (8, 1)))
    o["glu_w"] = f("s5_glu_w")[0]
    o["glu_b"] = f("s5_glu_b")
    o["w_out"] = f("w_out")[0]
    o["wg"] = f("ffn_w_gate")[0]
    o["wu"] = f("ffn_w_up")[0]
    o["wd"] = f("ffn_w_down")[0]
    o.update(host_consts())
    return o


IN_SPECS = None


def in_specs():
    return [
        ("x", [T, D], F32), ("ctx", [CT, D], F32), ("cvec", [128, NK, 2], F32),
        ("ada_w", [D, 6 * D], F32), ("ada_b", [1, 6 * D], F32), ("w_in", [D, 4096], F32),
        ("rows1", [4, D], F32), ("conv_w", [3, 3072], F32), ("conv_b", [1, 3072], F32),
        ("hy_w1", [33, 64], F32), ("hy_w2", [64, 64], F32), ("hy_w3", [64, 4096], F32), ("hy_cols", [64, 4], F32),
        ("hy_decay", [1, 4096], F32), ("hy_bias", [1, 2048], F32),
        ("LAMR2", [128, 128], F32), ("LAMI2", [128, 128], F32), ("LDT", [128, 128], F32),
        ("XB1", [128, 128, 16], F32), ("XB2", [128, 128, 16], F32), ("XC1", [128, 128, 16], F32), ("XC2", [128, 128, 16], F32),
        ("DCOL", [128, 64], F32), ("glu_w", [S5W, 2048], F32), ("glu_b", [1, 2048], F32), ("w_out", [D, D], F32),
        ("wg", [D, DFF], F32), ("wu", [D, DFF], F32), ("wd", [DFF, D], F32),
        ("FWD", [32, 128, 16, 128], BF16), ("INV", [16, 128, 32, 128], BF16), ("ZT", [33, 2048], F32),
        ("IDENT", [128, 128], F32), ("IDENTB", [128, 128], BF16), ("MASKF", [128, 128], F32), ("MASKB", [128, 128], F32),
        ("EXPO", [128, 3, 128, 8], F32), ("COLS", [128, 16], F32), ("POSF", [128, 16], F32), ("IOTA", [128, 256], F32),
        ("KCTX", [128, 128, 32], F32), ("SHP", [128, 128], F32), ("SHN", [128, 128], F32),
    ]


ALL_STAGES = ["mod", "proj", "filt", "s5prep", "s5", "glu", "hyena", "out", "ffn"]


def build_program(dbg=(), stages=None):
    stages = ALL_STAGES if stages is None else stages
    nc = bass.Bass("TRN2", target_bir_lowering=False)
    I = {}
    for name, shape, dt in in_specs():
        I[name] = nc.dram_tensor(name, list(shape), dt, kind="ExternalInput").ap()
    OUT = nc.dram_tensor("out", [T, D], F32, kind="ExternalOutput").ap()

    def scr(name, shape, dt=F32):
        if name in dbg:
            return nc.dram_tensor(name, list(shape), dt, kind="ExternalOutput").ap()
        return nc.dram_tensor(name, list(shape), dt).ap()

    SC = {}
    SC["MOD"] = scr("MOD", [2, 6 * D])
    SC["U"] = scr("U", [T, S5W], BF16)
    SC["UC"] = scr("UC", [CT, S5W], BF16)
    SC["PH"] = scr("PH", [T, 3 * HYW])
    SC["HT"] = scr("HT", [2048, 4096])
    SC["HF"] = scr("HF", [4096, 2048])
    SC["S5WT"] = scr("S5WT", [G, 128, 9 * 128], BF16)
    SC["S5P"] = scr("S5P", [128, 2, 128])
    SC["YS5"] = scr("YS5", [T, S5W], BF16)
    SC["YH"] = scr("YH", [T, HYW])
    SC["MIX"] = scr("MIX", [T, D], BF16)
    SC["X1"] = scr("X1", [T, D])
    SC["X0"] = scr("X0", [128, 128])
    SC["HTS"] = scr("HTS", [T // 128, 128, NFC, 128], BF16)
    SC["X2"] = scr("X2", [T, D])
    SC["H2"] = scr("H2", [T, D], BF16)
    RS = {k: Res("scr_" + k) for k in SC}

    with ExitStack() as st0:
        k = K(nc, st0)
        S = k.S
        cols = k.sb(st0, "cols", [128, 16])
        k.ld(cols[:], I["COLS"], cols)
        k.halfpi = cols[:, 6:7]
        k.cols = cols
        ident = k.sb(st0, "ident", [128, 128])
        k.ld(ident[:], I["IDENT"], ident)
        identb = k.sb(st0, "identb", [128, 128], BF16)
        k.ld(identb[:], I["IDENTB"], identb)
        k.ident = ident
        k.identb = identb
        env = dict(I=I, SC=SC, RS=RS, OUT=OUT, dbg=dbg)
        if "mod" in stages and "s5prep" in stages:
            with ExitStack() as stm:
                stage_s5prep(k, env, side=gen_mod(k, env, stm))
            S.barrier()
        elif "mod" in stages:
            stage_mod(k, env)
            S.barrier()
        if "proj" in stages:
            stage_proj_all(k, env)
            S.barrier()
        if "filt" in stages:
            stage_filt(k, env)
            S.barrier()
        if "s5prep" in stages and "mod" not in stages:
            stage_s5prep(k, env)
            S.barrier()
        if "s5" in stages:
            stage_s5(k, env)
            S.barrier()
        if "glu" in stages:
            stage_glu(k, env)
            S.barrier()
        if "hyena" in stages:
            stage_hyena(k, env)
            S.barrier()
        if "out" in stages:
            stage_out(k, env)
            S.barrier()
        if "ffn" in stages:
            stage_ffn(k, env)
        S.finish()
    return nc


def gen_mod(k, env, st):
    I, SC, RS = env["I"], env["SC"], env["RS"]
    NB, W = 48, 256
    cv = k.sb(st, "cv", [128, NK, 2])
    k.ld(cv[:], I["cvec"], cv)
    sl = k.sb(st, "sl", [128, NK, 2])
    k.act(sl[:], cv[:], AF.Silu, [cv], [sl])
    wts = [k.sb(st, "adaw%d" % i, [128, NK, W]) for i in range(3)]
    adab = [k.sb(st, "adab%d" % i, [2, W]) for i in range(3)]
    mods = [k.sb(st, "modsb%d" % i, [2, W]) for i in range(3)]

    def load(nb):
        k.ld(wts[nb % 3][:], I["ada_w"][:, nb * W:(nb + 1) * W].rearrange("(kc p) n -> p kc n", p=128), wts[nb % 3])
        k.ld(adab[nb % 3][:], I["ada_b"][0:1, nb * W:(nb + 1) * W].broadcast_to([2, W]), adab[nb % 3])
    load(0)
    load(1)
    for nb in range(NB):
        if nb + 2 < NB:
            load(nb + 2)
        yield
        wt, ab, mo = wts[nb % 3], adab[nb % 3], mods[nb % 3]
        pb = k.bank()
        for kc in range(NK):
            k.mm(pb[0:2, 0:W], sl[:, kc, :], wt[:, kc, :], kc == 0, kc == NK - 1, [sl, wt], [pb])
        k.tt(mo[:], pb[0:2, 0:W], ab[:], ALU.add, [pb, ab], [mo])
        k.sto(SC["MOD"][:, nb * W:(nb + 1) * W], mo[:], mo, [RS["MOD"]])


def stage_mod(k, env):
    with ExitStack() as st:
        for _ in gen_mod(k, env, st):
            pass


def run_with_side(main_gen, side_gen, every=2):
    n = 0
    side_live = True
    for _ in main_gen:
        n += 1
        if side_live and n % every == 0:
            try:
                next(side_gen)
            except StopIteration:
                side_live = False
    if side_live:
        for _ in side_gen:
            pass


def norm_tmps(k, st, tag):
    return (k.sb(st, tag + "junk", [128, D], BF16), k.sb(st, tag + "ssq", [128, 1]), k.sb(st, tag + "tmp", [128, D]))


def norm_rows(k, tmps, xt, n_part, srow, shrow, out_bf):
    junk, ssq, tmp = tmps
    k.act(junk[0:n_part, :], xt[0:n_part, :], AF.Square, [xt], [junk, ssq], accum=ssq[0:n_part, :])
    k.ts(ssq[0:n_part, :], ssq[0:n_part, :], 1.0 / D, EPS, ALU.mult, ALU.add, [ssq], [ssq])
    k.act(ssq[0:n_part, :], ssq[0:n_part, :], AF.Sqrt, [ssq], [ssq])
    k.recip(ssq[0:n_part, :], ssq[0:n_part, :], [ssq], [ssq])
    k.stt(tmp[0:n_part, :], xt[0:n_part, :], ssq[0:n_part, 0:1], srow[0:n_part, :], ALU.mult, ALU.mult, [xt, ssq, srow], [tmp])
    if shrow is not None:
        k.tt(out_bf[0:n_part, :], tmp[0:n_part, :], shrow[0:n_part, :], ALU.add, [tmp, shrow], [out_bf])
    else:
        k.cp(out_bf[0:n_part, :], tmp[0:n_part, :], [tmp], [out_bf])


def transpose_to(k, src_bf, nkc, dstT, col0, ncols=128):
    for h in range(0, nkc, 8):
        pb = k.bank()
        pbb = pb[:].bitcast(BF16)
        n = min(8, nkc - h)
        for j in range(n):
            k.tr(pbb[:, j * 128:j * 128 + ncols], src_bf[0:ncols, (h + j) * 128:(h + j + 1) * 128], k.identb[0:ncols, 0:ncols],
                 [src_bf, k.identb], [pb])
        src = pbb[:, 0:n * 128].rearrange("p (j c) -> p j c", c=128)[:, :, 0:ncols]
        k.cp(dstT[:, h:h + n, col0:col0 + ncols], src, [pb], [dstT], eng="scalar" if (h // 8) % 2 else "vector")


def stage_proj(k, env, ctx):
    I, SC, RS = env["I"], env["SC"], env["RS"]
    ntok = CT if ctx else T
    ntile = ntok // 128
    xin = I["ctx"] if ctx else I["x"]
    mrow = 1 if ctx else 0
    nbs = 2 if ctx else 8
    with ExitStack() as st:
        srow = k.sb(st, "srow", [128, D])
        shrow = k.sb(st, "shrow", [128, D])
        g1r = k.sb(st, "g1r", [128, D])
        k.ld(srow[:], SC["MOD"][mrow:mrow + 1, D:2 * D].broadcast_to([128, D]), srow, r=[RS["MOD"]])
        k.ld(shrow[:], SC["MOD"][mrow:mrow + 1, 0:D].broadcast_to([128, D]), shrow, r=[RS["MOD"]])
        k.ld(g1r[:], I["rows1"][0:1, :].broadcast_to([128, D]), g1r)
        k.stt(srow[:], srow[:], 1.0, g1r[:], ALU.add, ALU.mult, [srow, g1r], [srow])
        xnT = [k.sb(st, "xnT%d" % i, [128, NK, 128], BF16) for i in range(ntile)]
        stg = [k.sb(st, "wstg%d" % i, [128, 4, 512]) for i in range(2)]
        wbs = [k.sb(st, "wb%d" % i, [128, NK, 512], BF16) for i in range(2)]
        ob = [k.sb(st, "ob%d" % i, [128, 512], BF16) for i in range(4)]
        of = [k.sb(st, "of%d" % i, [128, 512]) for i in range(2)]
        xts = [k.sb(st, "xt%d" % i, [128, D]) for i in range(2)]
        xbs = [k.sb(st, "xb%d" % i, [128, D], BF16) for i in range(2)]
        ntm = norm_tmps(k, st, "n1")
        qq = [0]

        def load_w(nb):
            wb = wbs[nb % 2]
            for kq in range(4):
                sg = stg[qq[0] % 2]
                qq[0] += 1
                k.ld(sg[:], I["w_in"][kq * 512:(kq + 1) * 512, nb * 512:(nb + 1) * 512].rearrange("(kc p) n -> p kc n", p=128), sg)
                k.cp(wb[:, kq * 4:(kq + 1) * 4, :], sg[:], [sg], [wb], eng="scalar" if kq % 2 else "vector")

        def mm_tile(nb, tt):
            wb = wbs[nb % 2]
            pb = k.bank()
            for kc in range(NK):
                k.mm(pb[:], xnT[tt][:, kc, :], wb[:, kc, :], kc == 0, kc == NK - 1, [xnT[tt], wb], [pb])
            if nb < 2:
                o = ob[(2 * tt + nb) % 4]
                k.cp(o[:], pb[:], [pb], [o], eng="scalar" if tt % 2 else "vector")
                dst = SC["UC"] if ctx else SC["U"]
                k.sto(dst[tt * 128:(tt + 1) * 128, nb * 512:(nb + 1) * 512], o[:], o, [RS["UC" if ctx else "U"]])
            else:
                o = of[tt % 2]
                k.cp(o[:], pb[:], [pb], [o], eng="scalar" if tt % 2 else "vector")
                k.sto(SC["PH"][tt * 128:(tt + 1) * 128, (nb - 2) * 512:(nb - 1) * 512], o[:], o, [RS["PH"]])

        load_w(0)
        load_w(1)
        for tt in range(ntile):
            xt = xts[tt % 2]
            xb = xbs[tt % 2]
            k.ld(xt[:], xin[tt * 128:(tt + 1) * 128, :], xt)
            norm_rows(k, ntm, xt, 128, srow, shrow, xb)
            transpose_to(k, xb, NK, xnT[tt], 0)
            mm_tile(0, tt)
            mm_tile(1, tt)
        if nbs > 2:
            load_w(2)
        for nb in range(2, nbs):
            if nb + 1 < nbs:
                load_w(nb + 1)
            for tt in range(ntile):
                mm_tile(nb, tt)


def stage_proj_all(k, env):
    I, SC, RS = env["I"], env["SC"], env["RS"]
    ntile = T // 128
    with ExitStack() as st:
        srow = k.sb(st, "srow", [128, D])
        shrow = k.sb(st, "shrow", [128, D])
        g1r = k.sb(st, "g1r", [128, D])
        k.ld(g1r[:], I["rows1"][0:1, :].broadcast_to([128, D]), g1r)

        def set_rows(mrow):
            k.ld(srow[:], SC["MOD"][mrow:mrow + 1, D:2 * D].broadcast_to([128, D]), srow, r=[RS["MOD"]])
            k.ld(shrow[:], SC["MOD"][mrow:mrow + 1, 0:D].broadcast_to([128, D]), shrow, r=[RS["MOD"]])
            k.stt(srow[:], srow[:], 1.0, g1r[:], ALU.add, ALU.mult, [srow, g1r], [srow])
        xnT = [k.sb(st, "xnT%d" % i, [128, NK, 128], BF16) for i in range(ntile)]
        stg = [k.sb(st, "wstg%d" % i, [128, 4, 512]) for i in range(2)]
        wbs = [k.sb(st, "wb%d" % i, [128, NK, 512], BF16) for i in range(2)]
        ob = [k.sb(st, "ob%d" % i, [128, 512], BF16) for i in range(4)]
        of = [k.sb(st, "of%d" % i, [128, 512]) for i in range(2)]
        xts = [k.sb(st, "xt%d" % i, [128, D]) for i in range(2)]
        xbs = [k.sb(st, "xb%d" % i, [128, D], BF16) for i in range(2)]
        ntm = norm_tmps(k, st, "n1")
        qq = [0]
        cnt = [0]

        def load_w(nb):
            wb = wbs[nb % 2]
            for kq in range(4):
                sg = stg[qq[0] % 2]
                qq[0] += 1
                k.ld(sg[:], I["w_in"][kq * 512:(kq + 1) * 512, nb * 512:(nb + 1) * 512].rearrange("(kc p) n -> p kc n", p=128), sg)
                k.cp(wb[:, kq * 4:(kq + 1) * 4, :], sg[:], [sg], [wb], eng="scalar" if kq % 2 else "vector")

        def mm_tile(nb, tt, is_ctx=False):
            wb = wbs[nb % 2]
            pb = k.bank()
            for kc in range(NK):
                k.mm(pb[:], xnT[tt][:, kc, :], wb[:, kc, :], kc == 0, kc == NK - 1, [xnT[tt], wb], [pb])
            cnt[0] += 1
            eng = "scalar" if cnt[0] % 2 else "vector"
            if nb < 2:
                o = ob[cnt[0] % 4]
                k.cp(o[:], pb[:], [pb], [o], eng=eng)
                dst = SC["UC"] if is_ctx else SC["U"]
                k.sto(dst[tt * 128:(tt + 1) * 128, nb * 512:(nb + 1) * 512], o[:], o, [RS["UC" if is_ctx else "U"]])
            else:
                o = of[cnt[0] % 2]
                k.cp(o[:], pb[:], [pb], [o], eng=eng)
                k.sto(SC["PH"][tt * 128:(tt + 1) * 128, (nb - 2) * 512:(nb - 1) * 512], o[:], o, [RS["PH"]])

        def prep(xin, tt, it):
            xt = xts[it % 2]
            xb = xbs[it % 2]
            k.ld(xt[:], xin[tt * 128:(tt + 1) * 128, :], xt)
            norm_rows(k, ntm, xt, 128, srow, shrow, xb)
            transpose_to(k, xb, NK, xnT[tt], 0)

        load_w(0)
        load_w(1)
        set_rows(1)
        for c in range(CT // 128):
            prep(I["ctx"], c, c)
            mm_tile(0, c, True)
            mm_tile(1, c, True)
        set_rows(0)
        for tt in range(ntile):
            prep(I["x"], tt, tt)
            mm_tile(0, tt)
            mm_tile(1, tt)
        load_w(2)
        for nb in range(2, 8):
            if nb + 1 < 8:
                load_w(nb + 1)
            for tt in range(ntile):
                mm_tile(nb, tt)


def filt_pers(k, st):
    return (k.sb(st, "rn", [128, 2048]), k.sb(st, "brow", [128, 2048]), k.sb(st, "altb", [128, 1], BF16))


def gen_filt1(k, env, st, pers):
    I, SC, RS = env["I"], env["SC"], env["RS"]
    if True:
        zt = k.sb(st, "zt", [33, 2048])
        k.ld(zt[:], I["ZT"], zt)
        w1 = k.sb(st, "w1", [33, 64])
        k.ld(w1[:], I["hy_w1"], w1)
        w2 = k.sb(st, "w2", [64, 64])
        k.ld(w2[:], I["hy_w2"], w2)
        w3 = k.sb(st, "w3", [64, 4096])
        k.ld(w3[:], I["hy_w3"], w3)
        hc = k.sb(st, "hc", [64, 4])
        k.ld(hc[:], I["hy_cols"], hc)
        fs = k.sb(st, "fs", [64, 2])
        k.ts(fs[:], hc[:, 2:4], 1.0 / (2 * math.pi), None, ALU.mult, None, [hc], [fs])
        hh = [k.sb(st, "hmlp%d" % i, [64, 2048]) for i in range(2)]
        rr = k.sb(st, "hrr", [64, 2048])
        for layer in range(2):
            src = zt if layer == 0 else hh[0]
            kk = 33 if layer == 0 else 64
            w = w1 if layer == 0 else w2
            dst = hh[layer]
            for nb in range(4):
                pb = k.bank()
                k.mm(pb[0:64, :], w[0:kk, :], src[0:kk, nb * 512:(nb + 1) * 512], True, True, [w, src], [pb])
                k.ts(dst[:, nb * 512:(nb + 1) * 512], pb[0:64, :], hc[:, layer:layer + 1], fs[:, layer:layer + 1], ALU.add, ALU.mult,
                     [pb, hc, fs], [dst])
            k.frac_turns(dst[:], rr[:], dst, rr)
            k.act(dst[:], dst[:], AF.Sin, [dst], [dst], scale=TWO_PI_LO)
        h2 = k.sb(st, "h2b", [64, 2048], BF16)
        k.cp(h2[:], hh[1][:], [hh[1]], [h2])
        w3b = k.sb(st, "w3b", [64, 4096], BF16)
        k.cp(w3b[:], w3[:], [w3], [w3b], eng="scalar")
        w3 = w3b
        dec = k.sb(st, "dec", [128, 4096])
        k.ld(dec[:], I["hy_decay"].broadcast_to([128, 4096]), dec)
        k.act(dec[:], dec[:], AF.Abs, [dec], [dec])
        posf = k.sb(st, "posf", [128, 16])
        k.ld(posf[:], I["POSF"], posf)
        ones = k.sb(st, "ones", [128, 128], BF16)
        k.memset(ones[:], 1.0, [ones])
        l1 = k.sb(st, "l1", [128, 4096])
        wt = [k.sb(st, "fwin%d" % i, [128, 512]) for i in range(2)]
        ht = [k.sb(st, "fh%d" % i, [128, 512]) for i in range(3)]
        ab = [k.sb(st, "fab%d" % i, [128, 512], BF16) for i in range(2)]
        items = [(nb, pc) for nb in range(8) for pc in range(16)]
        pbs_ = {}

        def h3mm(i):
            nb, pc = items[i]
            pb = k.bank()
            k.mm(pb[:], h2[:, pc * 128:(pc + 1) * 128], w3[:, nb * 512:(nb + 1) * 512], True, True, [h2, w3], [pb])
            pbs_[i] = pb
        yield
        h3mm(0)
        for i, (nb, pc) in enumerate(items):
            if i + 1 < len(items):
                h3mm(i + 1)
            s_is_bwd = (nb // 2) % 2 == 1
            pl = k.banks[6 + nb % 2]
            pb = pbs_.pop(i)
            wti = wt[i % 2]
            hti = ht[i % 3]
            abi = ab[i % 2]
            k.act(wti[:], dec[:, nb * 512:(nb + 1) * 512], AF.Exp, [dec, posf], [wti], scale=posf[:, pc:pc + 1])
            k.tt(hti[:], pb[:], wti[:], ALU.mult, [pb, wti], [hti])
            if s_is_bwd and pc == 0:
                k.ts(hti[:], hti[:], k.cols[:, 7:8], None, ALU.mult, None, [hti, k.cols], [hti])
            k.sto(SC["HT"][pc * 128:(pc + 1) * 128, nb * 512:(nb + 1) * 512], hti[:], hti, [RS["HT"]])
            k.stt(abi[:], hti[:], -1.0, hti[:], ALU.mult, ALU.max, [hti], [abi])
            k.mm(pl[:], ones[:], abi[:], pc == 0, pc == 15, [ones, abi], [pl])
            if pc == 15:
                k.cp(l1[:, nb * 512:(nb + 1) * 512], pl[:], [pl], [l1], eng="scalar")
            yield
        rn, brow, altb = pers
        for o in range(2):
            k.tt(rn[:, o * 1024:(o + 1) * 1024], l1[:, o * 2048:o * 2048 + 1024], l1[:, o * 2048 + 1024:(o + 1) * 2048], ALU.add, [l1], [rn])
        k.ts(rn[:], rn[:], EPS, None, ALU.add, None, [rn], [rn])
        k.recip(rn[:], rn[:], [rn], [rn])
        k.ld(brow[:], I["hy_bias"].broadcast_to([128, 2048]), brow)
        k.cp(altb[:], k.cols[:, 8:9], [k.cols], [altb])


def gen_filt2(k, env, st, pers):
    I, SC, RS = env["I"], env["SC"], env["RS"]
    if True:
        e_ts = [k.sb(st, "e_t%d" % i, [128, 16, 512], BF16) for i in range(2)]
        d_ts = [k.sb(st, "d_t%d" % i, [128, 16, 512], BF16) for i in range(2)]
        ff = [k.sb(st, "ff%d" % i, [128, 512]) for i in range(2)]
        fb = [k.sb(st, "fb%d" % i, [128, 512]) for i in range(2)]
        fw = [k.sb(st, "fwt%d" % i, [128, 16, 128], BF16) for i in range(4)]
        ot = [k.sb(st, "fo%d" % i, [128, 512]) for i in range(2)]
        nyqs = [k.sb(st, "nyq%d" % i, [1, 512]) for i in range(2)]
        rn, brow, altb = pers
        yield

        def cols_of(q):
            o, half = q // 2, q % 2
            return o * 2048 + half * 512, o * 2048 + 1024 + half * 512, o * 1024 + half * 512

        def build(q):
            cf, cb, co = cols_of(q)
            e_t, d_t, nyq = e_ts[q % 2], d_ts[q % 2], nyqs[q % 2]
            pn = k.banks[7]
            for lc in range(16):
                a = ff[lc % 2]
                b = fb[lc % 2]
                k.ld(a[:], SC["HT"][lc * 128:(lc + 1) * 128, cf:cf + 512], a, r=[RS["HT"]])
                k.ld(b[:], SC["HT"][lc * 128:(lc + 1) * 128, cb:cb + 512], b, r=[RS["HT"]])
                k.tt(e_t[:, lc, :], a[:], b[:], ALU.add, [a, b], [e_t])
                k.tt(d_t[:, lc, :], a[:], b[:], ALU.subtract, [a, b], [d_t])
                k.mm(pn[0:1, :], altb[:, 0:1], e_t[:, lc, :], lc == 0, lc == 15, [altb, e_t], [pn])
            k.tt(nyq[:], pn[0:1, :], rn[0:1, co:co + 512], ALU.mult, [pn, rn], [nyq])
            k.tt(nyq[:], nyq[:], brow[0:1, co:co + 512], ALU.add, [nyq, brow], [nyq])

        build(0)
        mi = 0
        for q in range(4):
            cf, cb, co = cols_of(q)
            e_t, d_t, nyq = e_ts[q % 2], d_ts[q % 2], nyqs[q % 2]
            if q + 1 < 4:
                build(q + 1)
            for m in range(32):
                fwt = fw[mi % 4]
                mi += 1
                k.ld(fwt[:], I["FWD"][m], fwt)
                pb = k.bank()
                src = e_t if m < 16 else d_t
                for lc in range(16):
                    k.mm(pb[:], fwt[:, lc, :], src[:, lc, :], lc == 0, lc == 15, [fwt, src], [pb])
                o_t = ot[m % 2]
                k.tt(o_t[:], pb[:], rn[:, co:co + 512], ALU.mult, [pb, rn], [o_t])
                if m < 16:
                    k.tt(o_t[:], o_t[:], brow[:, co:co + 512], ALU.add, [o_t, brow], [o_t])
                if m == 16:
                    k.cp(o_t[0:1, :], nyq[:], [nyq, o_t], [o_t])
                k.sto(SC["HF"][m * 128:(m + 1) * 128, co:co + 512], o_t[:], o_t, [RS["HF"]])
                yield


def stage_filt(k, env):
    with ExitStack() as stp:
        pers = filt_pers(k, stp)
        k.brange = (0, 6)
        with ExitStack() as sa:
            for _ in gen_filt1(k, env, sa, pers):
                pass
        k.S.barrier()
        k.brange = (0, 7)
        with ExitStack() as sb_:
            for _ in gen_filt2(k, env, sb_, pers):
                pass
        k.brange = (0, 8)


def bc_last(ap, n):
    shp = list(ap.shape)
    return ap.unsqueeze(len(shp)).broadcast_to(shp + [n])


def gen_s5prep(k, env):
    I, SC, RS = env["I"], env["SC"], env["RS"]
    cols = k.cols
    sA, sB, sC, sD = (cols[:, i:i + 1] for i in range(4))
    with ExitStack() as st:
        def ldt(name, shape, stx=st):
            t = k.sb(stx, name, shape)
            k.ld(t[:], I[name], t)
            return t
        xc1 = ldt("XC1", [128, 128, 16])
        xc2 = ldt("XC2", [128, 128, 16])
        dcol = ldt("DCOL", [128, 64])
        maskf = ldt("MASKF", [128, 128])
        maskb = ldt("MASKB", [128, 128])
        lr = k.sb(st, "lr", [128, 128])
        li = k.sb(st, "li", [128, 128])
        xbb1 = k.sb(st, "xbb1", [128, 128, 16])
        xbb2 = k.sb(st, "xbb2", [128, 128, 16])
        with ExitStack() as sta:
            lamr = ldt("LAMR2", [128, 128], sta)
            lami = ldt("LAMI2", [128, 128], sta)
            ldtt = ldt("LDT", [128, 128], sta)
            xb1 = ldt("XB1", [128, 128, 16], sta)
            xb2 = ldt("XB2", [128, 128, 16], sta)
            sm = lambda n: k.sb(sta, n, [128, 128])
            dt = sm("dt"); ea = sm("ea"); sn = sm("sn"); cs = sm("cs")
            ar = sm("ar"); ai = sm("ai"); den = sm("den"); t1 = sm("t1"); t2 = sm("t2"); cr = sm("cr"); ci = sm("ci")
            cib = sm("cib"); cic = sm("cic")
            k.act(dt[:], ldtt[:], AF.Exp, [ldtt], [dt])
            k.tt(lr[:], lamr[:], dt[:], ALU.mult, [lamr, dt], [lr])
            k.tt(li[:], lami[:], dt[:], ALU.mult, [lami, dt], [li])
            k.act(ea[:], lr[:], AF.Exp, [lr], [ea])
            k.sincos(sta, li[:], [128, 128], sn[:], cs[:], [li], [sn], [cs], "sc0")
            yield
            k.tt(ar[:], ea[:], cs[:], ALU.mult, [ea, cs], [ar])
            k.tt(ai[:], ea[:], sn[:], ALU.mult, [ea, sn], [ai])
            k.ts(ar[:], ar[:], -1.0, None, ALU.add, None, [ar], [ar])
            k.tt(den[:], lamr[:], lamr[:], ALU.mult, [lamr], [den])
            k.tt(t1[:], lami[:], lami[:], ALU.mult, [lami], [t1])
            k.tt(den[:], den[:], t1[:], ALU.add, [den, t1], [den])
            k.recip(den[:], den[:], [den], [den])
            k.tt(t1[:], ar[:], lamr[:], ALU.mult, [ar, lamr], [t1])
            k.tt(t2[:], ai[:], lami[:], ALU.mult, [ai, lami], [t2])
            k.tt(t1[:], t1[:], t2[:], ALU.add, [t1, t2], [t1])
            k.tt(cr[:], t1[:], den[:], ALU.mult, [t1, den], [cr])
            k.tt(t1[:], ai[:], lamr[:], ALU.mult, [ai, lamr], [t1])
            k.tt(t2[:], ar[:], lami[:], ALU.mult, [ar, lami], [t2])
            k.tt(t1[:], t1[:], t2[:], ALU.subtract, [t1, t2], [t1])
            k.tt(ci[:], t1[:], den[:], ALU.mult, [t1, den], [ci])
            k.ts(cib[:], ci[:], sB, None, ALU.mult, None, [ci, cols], [cib])
            k.ts(cic[:], ci[:], sC, None, ALU.mult, None, [ci, cols], [cic])
            yield
            s5p = k.sb(sta, "s5p", [128, 2, 128])
            k.ts(s5p[:, 0, :], lr[:], 8.0, None, ALU.mult, None, [lr], [s5p])
            k.ts(s5p[:, 1, :], li[:], 8.0 / (2 * math.pi), None, ALU.mult, None, [li], [s5p])
            k.sto(SC["S5P"], s5p[:], s5p, [RS["S5P"]])
            tmpa = k.sb(sta, "tmpa", [128, 128, 16])
            k.tt(xbb1[:], bc_last(cr[:], 16), xb1[:], ALU.mult, [cr, xb1], [xbb1])
            k.tt(tmpa[:], bc_last(cib[:], 16), xb2[:], ALU.mult, [cib, xb2], [tmpa])
            k.tt(xbb1[:], xbb1[:], tmpa[:], ALU.add, [xbb1, tmpa], [xbb1])
            k.tt(xbb2[:], bc_last(cr[:], 16), xb2[:], ALU.mult, [cr, xb2], [xbb2])
            k.tt(tmpa[:], bc_last(cic[:], 16), xb1[:], ALU.mult, [cic, xb1], [tmpa])
            k.tt(xbb2[:], xbb2[:], tmpa[:], ALU.add, [xbb2, tmpa], [xbb2])
            yield
        yield "barrier"
        pw = {}
        for fam in range(3):
            pw[(fam, "r")] = k.sb(st, "pr%d" % fam, [128, 128, 8])
            pw[(fam, "i")] = k.sb(st, "pi%d" % fam, [128, 128, 8])
        with ExitStack() as stb:
            expo = k.sb(stb, "expo", [128, 3, 128, 8])
            k.ld(expo[:], I["EXPO"], expo)
            marg = k.sb(stb, "marg", [128, 128, 8])
            ang = k.sb(stb, "ang", [128, 128, 8])
            ytm = k.sb(stb, "scf_y", [128, 128, 8])
            rtm = k.sb(stb, "scf_r", [128, 128, 8])
            for fam in range(3):
                pr, pi = pw[(fam, "r")], pw[(fam, "i")]
                k.tt(marg[:], expo[:, fam, :, :], bc_last(lr[:], 8), ALU.mult, [expo, lr], [marg])
                k.act(marg[:], marg[:], AF.Exp, [marg], [marg])
                k.tt(ang[:], expo[:, fam, :, :], bc_last(li[:], 8), ALU.mult, [expo, li], [ang])
                k.sincos2(ang[:], ytm, rtm, pi[:], pr[:], [ang], [pi], [pr])
                k.tt(pr[:], pr[:], marg[:], ALU.mult, [pr, marg], [pr])
                k.tt(pi[:], pi[:], marg[:], ALU.mult, [pi, marg], [pi])
                yield
        yield "barrier"

        def signed(src, col, name):
            t = k.sb(st, name, [128, 128, 8])
            k.ts(t[:], src[:], col, None, ALU.mult, None, [src, cols], [t])
            return t
        p0r, p0i = pw[(0, "r")], pw[(0, "i")]
        p0iB = signed(p0i, sB, "p0iB")
        p0rC = signed(p0r, sC, "p0rC")
        p1rC = signed(pw[(1, "r")], sC, "p1rC")
        p1iD = signed(pw[(1, "i")], sD, "p1iD")
        p2rC = signed(pw[(2, "r")], sC, "p2rC")
        p2iD = signed(pw[(2, "i")], sD, "p2iD")
        p2rD = signed(pw[(2, "r")], sD, "p2rD")
        p2iB = signed(pw[(2, "i")], sB, "p2iB")
        fams = {
            "T1": (p0r, xbb1, p0iB, xbb2),
            "T1sw": (p0rC, xbb2, p0i, xbb1),
            "R": (p1rC, xc1, p1iD, xc2),
            "G1": (p2rC, xc1, p2iD, xc2),
            "G2": (p2rD, xc2, p2iB, xc1),
        }
        stg = [k.sb(st, "s5stg%d" % i, [128, 8, 9, 128], BF16) for i in range(1)]
        mtmp = [k.sb(st, "mtmp%d" % i, [128, 8, 128]) for i in range(2)]
        fa = k.sb(st, "fam_a", [128, 8, 8, 16])
        fbt = k.sb(st, "fam_b", [128, 8, 8, 16])
        fa2 = k.sb(st, "fam_a2", [128, 8, 8, 16])
        fbt2 = k.sb(st, "fam_b2", [128, 8, 8, 16])
        t1b = [k.sb(st, "T1b%d" % i, [128, 8, 128], BF16) for i in range(2)]
        t1s = [k.sb(st, "T1s%d" % i, [128, 8, 128], BF16) for i in range(2)]
        rmb = [k.sb(st, "Rmb%d" % i, [128, 8, 128], BF16) for i in range(2)]
        mt2 = k.sb(st, "mt2", [128, 4, 128])
        v4 = lambda ap: ap.rearrange("p g (i c) -> p g i c", c=16)

        def fam(name, gd0, out_ap, outT):
            pa, xa, pb_, xb_ = fams[name]
            pa_bc = bc_last(pa[:, gd0:gd0 + 8, :], 16)
            pb_bc = bc_last(pb_[:, gd0:gd0 + 8, :], 16)
            xa_bc = xa[:, gd0:gd0 + 8, :].unsqueeze(2).broadcast_to([128, 8, 8, 16])
            xb_bc = xb_[:, gd0:gd0 + 8, :].unsqueeze(2).broadcast_to([128, 8, 8, 16])
            k.tt(fa[:], pa_bc, xa_bc, ALU.mult, [pa, xa], [fa])
            k.tt(fbt[:], pb_bc, xb_bc, ALU.mult, [pb_, xb_], [fbt])
            k.tt(out_ap, fa[:], fbt[:], ALU.add, [fa, fbt], [outT])

        for gb in range(8):
            g0 = gb * 8
            sg = stg[0]
            mt = mtmp[0]
            for d in range(2):
                gd0 = d * 64 + g0
                fam("T1", gd0, v4(t1b[d][:]), t1b[d])
                yield
                fam("T1sw", gd0, v4(t1s[d][:]), t1s[d])
                fam("R", gd0, v4(rmb[d][:]), rmb[d])
                yield
                fam("G1", gd0, v4(sg[:, :, 5 + 2 * d, :]), sg)
                fam("G2", gd0, v4(sg[:, :, 6 + 2 * d, :]), sg)
                yield
            for d in range(2):
                T1b, T1s, Rmb = t1b[d], t1s[d], rmb[d]
                for h in range(2):
                    pbw = k.bank()
                    pbwb = pbw[:].bitcast(BF16)
                    pbm = k.bank()
                    for gi in range(4):
                        g = h * 4 + gi
                        k.tr(pbwb[:, (2 * gi) * 128:(2 * gi + 1) * 128], T1b[:, g, :], k.identb[:], [T1b, k.identb], [pbw])
                        k.tr(pbwb[:, (2 * gi + 1) * 128:(2 * gi + 2) * 128], T1s[:, g, :], k.identb[:], [T1s, k.identb], [pbw])
                        k.mm(pbm[:, gi * 128:(gi + 1) * 128], T1b[:, g, :], Rmb[:, g, :], True, True, [T1b, Rmb], [pbm])
                    k.cp(sg[:, h * 4:(h + 1) * 4, 1 + 2 * d:3 + 2 * d, :], pbwb.rearrange("p (g w c) -> p g w c", g=4, w=2), [pbw], [sg], eng="scalar")
                    msk = (maskf if d == 0 else maskb)
                    mbc = msk[:].unsqueeze(1).broadcast_to([128, 4, 128])
                    pm3 = pbm[:].rearrange("p (g c) -> p g c", g=4)
                    if d == 0:
                        k.tt(mt[:, h * 4:(h + 1) * 4, :], pm3, mbc, ALU.mult, [pbm, msk], [mt])
                    else:
                        k.tt(mt2[:], pm3, mbc, ALU.mult, [pbm, msk], [mt2])
                        k.tt(mt[:, h * 4:(h + 1) * 4, :], mt[:, h * 4:(h + 1) * 4, :], mt2[:], ALU.add, [mt, mt2], [mt])
                    yield
            for gi in range(8):
                k.stt(sg[:, gi, 0, :], k.ident[:], dcol[:, g0 + gi:g0 + gi + 1], mt[:, gi, :], ALU.mult, ALU.add, [k.ident, dcol, mt], [sg])
            k.sto(SC["S5WT"][g0:g0 + 8].rearrange("g p n -> p g n"), sg[:].rearrange("p g w c -> p g (w c)"), sg, [RS["S5WT"]])


def stage_s5prep(k, env, side=None):
    def main():
        for y in gen_s5prep(k, env):
            if y == "barrier":
                k.S.barrier()
            yield
    if side is None:
        for _ in main():
            pass
    else:
        next(side)
        run_with_side(main(), side, every=2)


def stage_s5(k, env):
    I, SC, RS = env["I"], env["SC"], env["RS"]
    idb = k.identb
    with ExitStack() as st:
        s5p = k.sb(st, "s5p", [128, 2, 128])
        k.ld(s5p[:], SC["S5P"], s5p, r=[RS["S5P"]])
        r8 = k.sb(st, "r8", [128, 128])
        k.act(r8[:], s5p[:, 0, :], AF.Exp, [s5p], [r8])
        x0 = k.sb(st, "x0", [128, 128])
        wts = [k.sb(st, "s5w%d" % i, [128, 9, 128], BF16) for i in range(2)]
        with ExitStack() as sc:
            tmc = k.sb(sc, "tmc", [32, 8, 1024], BF16)
            k.ld(tmc[:], SC["UC"].rearrange("(jj s) n -> jj s n", s=8), tmc, r=[RS["UC"]])
            ucs = k.sb(sc, "ucs", [128, 64, 32], BF16)
            tmcg = k.sb(sc, "tmcg", [32, 64, 128], BF16)
            k.cp(tmcg[:].rearrange("p g (s c) -> p s g c", c=16), tmc[:].rearrange("p s (g c) -> p s g c", c=16), [tmc], [tmcg])
            for gq in range(4):
                pb = k.bank()
                pbb = pb[:].bitcast(BF16)
                for gi in range(16):
                    g = gq * 16 + gi
                    k.tr(pbb[:, gi * 32:(gi + 1) * 32], tmcg[0:32, g, :], idb[0:32, 0:32], [tmcg, idb], [pb])
                k.cp(ucs[:, gq * 16:(gq + 1) * 16, :], pbb[:, 0:512].rearrange("p (g j) -> p g j", j=32), [pb], [ucs])
            kc = k.sb(sc, "kc", [128, 128, 32])
            k.ld(kc[:], I["KCTX"], kc)
            ppr = k.sb(sc, "ppr", [128, 128, 32])
            ppi = k.sb(sc, "ppi", [128, 128, 32])
            marg = k.sb(sc, "cmarg", [128, 128, 32])
            ytm_ = k.sb(sc, "cy", [128, 128, 32])
            rtm_ = k.sb(sc, "cr", [128, 128, 32])
            k.tt(marg[:], kc[:], bc_last(s5p[:, 0, :], 32), ALU.mult, [kc, s5p], [marg])
            k.act(marg[:], marg[:], AF.Exp, [marg], [marg])
            k.tt(ytm_[:], kc[:], bc_last(s5p[:, 1, :], 32), ALU.mult, [kc, s5p], [ytm_])
            k._sincos_tail(ytm_, rtm_, ppi[:], ppr[:], [ppi], [ppr])
            k.tt(ppr[:], ppr[:], marg[:], ALU.mult, [ppr, marg], [ppr])
            k.tt(ppi[:], ppi[:], marg[:], ALU.mult, [ppi, marg], [ppi])
            ct1 = [k.sb(sc, "ct1_%d" % i, [128, 2, 4, 32]) for i in range(2)]
            ct2 = [k.sb(sc, "ct2_%d" % i, [128, 2, 4, 32]) for i in range(2)]
            wc = [k.sb(sc, "s5wc%d" % i, [128, 9, 128], BF16) for i in range(8)]
            ppr4 = ppr[:].rearrange("p (d g) j -> p d g j", d=2)
            ppi4 = ppi[:].rearrange("p (d g) j -> p d g j", d=2)
            x04 = x0[:].rearrange("p (d g) -> p d g", d=2)
            for g4 in range(16):
                pb = k.bank()
                for gi in range(4):
                    g = g4 * 4 + gi
                    w = wc[g % 8]
                    k.ld(w[:], SC["S5WT"][g].rearrange("p (w c) -> p w c", c=128), w, r=[RS["S5WT"]])
                    for q in range(4):
                        k.mm(pb[:, gi * 128 + q * 32:gi * 128 + (q + 1) * 32], w[:, 1 + q, :], ucs[:, g, :], True, True, [w, ucs], [pb])
                v = pb[:].rearrange("p (gi d s j) -> p d gi s j", gi=4, d=2, s=2)
                a1, a2 = ct1[g4 % 2], ct2[g4 % 2]
                gs = slice(g4 * 4, g4 * 4 + 4)
                k.tt(a1[:], ppr4[:, :, gs, :], v[:, :, :, 0, :], ALU.mult, [ppr, pb], [a1])
                k.tt(a2[:], ppi4[:, :, gs, :], v[:, :, :, 1, :], ALU.mult, [ppi, pb], [a2])
                k.tt(a1[:], a1[:], a2[:], ALU.subtract, [a1, a2], [a1])
                k.rsum(x04[:, :, gs], a1[:], [a1], [x0])
        if "X0" in env.get("dbg", ()):
            k.sto(SC["X0"], x0[:], x0, [RS["X0"]])
        k.S.barrier()
        with ExitStack() as sm:
            tm = [k.sb(sm, "tm%d" % i, [128, 8, 1024], BF16) for i in range(2)]
            yt = [k.sb(sm, "ytm%d" % i, [128, 8, 1024], BF16) for i in range(2)]
            for jb in range(2):
                k.ld(tm[jb][:], SC["U"][jb * 1024:(jb + 1) * 1024, :].rearrange("(jj s) n -> jj s n", s=8), tm[jb], r=[RS["U"]])
            tmg = [k.sb(sm, "tmg%d" % i, [128, 64, 128], BF16) for i in range(2)]
            for jb in range(2):
                k.cp(tmg[jb][:].rearrange("p g (s c) -> p s g c", c=16), tm[jb][:].rearrange("p s (g c) -> p s g c", c=16), [tm[jb]], [tmg[jb]],
                     eng="scalar" if jb else "vector")
            iota = k.sb(sm, "iota", [128, 256])
            k.ld(iota[:], I["IOTA"], iota)
            tabc = [k.sb(sm, "tabc%d" % i, [128, 2, 4, 256]) for i in range(2)]
            tabs = [k.sb(sm, "tabs%d" % i, [128, 2, 4, 256]) for i in range(2)]
            ty = k.sb(sm, "ty", [128, 2, 4, 256])
            tr_ = k.sb(sm, "trr", [128, 2, 4, 256])
            ust = [k.sb(sm, "ust%d" % i, [128, 256], BF16) for i in range(2)]
            m1 = [k.sb(sm, "m1_%d" % i, [128, 256]) for i in range(2)]
            m2 = [k.sb(sm, "m2_%d" % i, [128, 256]) for i in range(2)]
            zz = [k.sb(sm, "zz%d" % i, [128, 256]) for i in range(2)]
            zc = [[k.sb(sm, "zc%d_%d" % (d, i), [128, 257], BF16) for i in range(2)] for d in range(2)]
            zs = [[k.sb(sm, "zs%d_%d" % (d, i), [128, 257], BF16) for i in range(2)] for d in range(2)]
            for d in range(2):
                for i in range(2):
                    k.memset(zs[d][i][:], 0.0, [zs[d][i]])
            k.brange = (0, 6)
            zz4 = [[k.sb(sm, "zz%d_%d" % (d, i), [128, 256]) for i in range(2)] for d in range(2)]
            yb = [k.banks[6], k.banks[7]]
            state = {}

            def tables(g4):
                tc_, tsn_ = tabc[g4 % 2], tabs[g4 % 2]
                phi4 = s5p[:, 1, :].rearrange("p (d g) -> p d g", d=2)[:, :, g4 * 4:(g4 + 1) * 4]
                k.tt(ty[:], iota[:].unsqueeze(1).unsqueeze(1).broadcast_to([128, 2, 4, 256]), bc_last(phi4, 256), ALU.mult, [iota, s5p], [ty])
                k._sincos_tail(ty, tr_, tsn_[:], tc_[:], [tsn_], [tc_])

            def front(g):
                w = wts[g % 2]
                k.ld(w[:], SC["S5WT"][g].rearrange("p (w c) -> p w c", c=128), w, r=[RS["S5WT"]])
                pbu = k.bank()
                pbub = pbu[:].bitcast(BF16)
                for jb in range(2):
                    k.tr(pbub[:, jb * 128:(jb + 1) * 128], tmg[jb][:, g, :], idb[:], [tmg[jb], idb], [pbu])
                us = ust[g % 2]
                k.cp(us[:], pbub[:, 0:256], [pbu], [us], eng="scalar")
                pbs2 = []
                for d in range(2):
                    pbs = k.bank()
                    k.mm(pbs[:, 0:256], w[:, 1 + 2 * d, :], us[:], True, True, [w, us], [pbs])
                    k.mm(pbs[:, 256:512], w[:, 2 + 2 * d, :], us[:], True, True, [w, us], [pbs])
                    pbs2.append(pbs)
                state[g] = (w, us, pbs2)

            def mid(g):
                w, us, pbs2 = state[g]
                g4, gi = g // 4, g % 4
                tc_, tsn_ = tabc[g4 % 2], tabs[g4 % 2]
                tcs = [tc_[:, 0, gi, :], tc_[:, 1, gi, ::-1]]
                tsns = [tsn_[:, 0, gi, :], tsn_[:, 1, gi, ::-1]]
                zcd = [zc[0][g % 2], zc[1][g % 2]]
                zsd = [zs[0][g % 2], zs[1][g % 2]]
                abz = [(m1[d], m2[d], zz4[d][g % 2]) for d in range(2)]
                gds = [g, 64 + g]
                for d in range(2):
                    k.tt(abz[d][0][:], tcs[d], pbs2[d][:, 0:256], ALU.mult, [tc_, pbs2[d]], [abz[d][0]])
                for d in range(2):
                    k.tt(abz[d][1][:], tsns[d], pbs2[d][:, 256:512], ALU.mult, [tsn_, pbs2[d]], [abz[d][1]])
                for d in range(2):
                    k.tt(abz[d][0][:], abz[d][0][:], abz[d][1][:], ALU.add, [abz[d][0], abz[d][1]], [abz[d][0]])
                for d in range(2):
                    a, z, gd = abz[d][0], abz[d][2], gds[d]
                    r8b = r8[:, gd:gd + 1].broadcast_to([128, 256])
                    if d == 0:
                        k.scan(z[:], r8b, a[:], x0[:, gd:gd + 1], [r8, a, x0], [z])
                    else:
                        k.scan(z[:, ::-1], r8b, a[:, ::-1], x0[:, gd:gd + 1], [r8, a, x0], [z])
                for d in range(2):
                    z = abz[d][2]
                    dst = zcd[d][:, 1:257] if d == 0 else zcd[d][:, 0:256]
                    k.tt(dst, tcs[d], z[:], ALU.mult, [tc_, z], [zcd[d]], eng="gpsimd" if d else "vector")
                for d in range(2):
                    z = abz[d][2]
                    dst = zsd[d][:, 1:257] if d == 0 else zsd[d][:, 0:256]
                    k.tt(dst, tsns[d], z[:], ALU.mult, [tsn_, z], [zsd[d]], eng="gpsimd" if d else "vector")
                k.cp(zcd[0][:, 0:1], x0[:, g:g + 1], [x0], [zcd[0]], eng="scalar")
                k.cp(zcd[1][:, 256:257], x0[:, 64 + g:65 + g], [x0], [zcd[1]], eng="scalar")
                state[g] = (w, us, zcd, zsd)

            def back(g):
                w, us, zcd, zsd = state.pop(g)
                g4, gi = g // 4, g % 4
                for jb in range(2):
                    o = yb[jb][:, gi * 128:(gi + 1) * 128]
                    c0 = jb * 128
                    k.mm(o, us[:, c0:c0 + 128], w[:, 0, :], True, False, [us, w], [yb[jb]])
                    k.mm(o, zcd[0][:, c0:c0 + 128], w[:, 5, :], False, False, [zcd[0], w], [yb[jb]])
                    k.mm(o, zsd[0][:, c0:c0 + 128], w[:, 6, :], False, False, [zsd[0], w], [yb[jb]])
                    k.mm(o, zcd[1][:, c0 + 1:c0 + 129], w[:, 7, :], False, False, [zcd[1], w], [yb[jb]])
                    k.mm(o, zsd[1][:, c0 + 1:c0 + 129], w[:, 8, :], False, True, [zsd[1], w], [yb[jb]])
                if gi == 3:
                    for jb in range(2):
                        src = yb[jb][:].rearrange("p (g t c) -> p t g c", g=4, t=8)
                        dst = yt[jb][:, :, g4 * 64:(g4 + 1) * 64].rearrange("p t (g c) -> p t g c", c=16)
                        k.act(dst, src, AF.Gelu, [yb[jb]], [yt[jb]])

            tables(0)
            front(0)
            for g in range(G):
                if g + 1 < G:
                    front(g + 1)
                if g % 4 == 0 and g // 4 + 1 < 16:
                    tables(g // 4 + 1)
                mid(g)
                back(g)
            k.brange = (0, 8)
            for jb in range(2):
                k.sto(SC["YS5"][jb * 1024:(jb + 1) * 1024, :].rearrange("(jj s) n -> jj s n", s=8), yt[jb][:], yt[jb], [RS["YS5"]])


def rms_rows(k, tmps, yt, width, grow, out_ap, outT):
    junk, ssq, _ = tmps
    k.act(junk[:, 0:width], yt[:, 0:width], AF.Square, [yt], [junk, ssq], accum=ssq[:])
    k.ts(ssq[:], ssq[:], 1.0 / width, EPS, ALU.mult, ALU.add, [ssq], [ssq])
    k.act(ssq[:], ssq[:], AF.Sqrt, [ssq], [ssq])
    k.recip(ssq[:], ssq[:], [ssq], [ssq])
    k.stt(out_ap, yt[:, 0:width], ssq[:, 0:1], grow, ALU.mult, ALU.mult, [yt, ssq], [outT])


def load_cast_weight(k, wdram, nkc, wb, stg, n0=0, ncols=None, c0=0):
    ncols = wdram.shape[1] if ncols is None else ncols
    q = 0
    for nb in range(ncols // 512):
        for kq in range(0, nkc, 4):
            sg = stg[q % len(stg)]
            q += 1
            k.ld(sg[:], wdram[kq * 128:(kq + 4) * 128, c0 + nb * 512:c0 + (nb + 1) * 512].rearrange("(kc p) n -> p kc n", p=128), sg)
            k.cp(wb[:, kq:kq + 4, n0 + nb * 512:n0 + (nb + 1) * 512], sg[:], [sg], [wb], eng="scalar" if q % 2 else "vector")


def stage_glu(k, env):
    I, SC, RS = env["I"], env["SC"], env["RS"]
    with ExitStack() as st:
        wgl = k.sb(st, "wgl", [128, 8, 2048], BF16)
        stg = [k.sb(st, "gstg%d" % i, [128, 4, 512]) for i in range(2)]
        load_cast_weight(k, I["glu_w"], 8, wgl, stg)
        glub = k.sb(st, "glub", [128, 2048])
        k.ld(glub[:], I["glu_b"].broadcast_to([128, 2048]), glub)
        bgs = k.sb(st, "bgs", [128, 1024])
        k.ld(bgs[:], I["rows1"][3:4, 0:1024].broadcast_to([128, 1024]), bgs)
        ytl = [k.sb(st, "gy%d" % i, [128, 1024], BF16) for i in range(2)]
        yT = [k.sb(st, "gyT%d" % i, [128, 8, 128], BF16) for i in range(2)]
        av = [k.sb(st, "ga%d" % i, [128, 1024]) for i in range(2)]
        sv = [k.sb(st, "gs%d" % i, [128, 1024]) for i in range(2)]
        mo = [k.sb(st, "gm%d" % i, [128, 1024], BF16) for i in range(2)]
        tmps = norm_tmps(k, st, "gl")
        def prep(tt):
            y, yt_ = ytl[tt % 2], yT[tt % 2]
            k.ld(y[:], SC["YS5"][tt * 128:(tt + 1) * 128, :], y, r=[RS["YS5"]])
            transpose_to(k, y, 8, yt_, 0)
        prep(0)
        for tt in range(16):
            y, yt_, a, sg_, m = ytl[tt % 2], yT[tt % 2], av[tt % 2], sv[tt % 2], mo[tt % 2]
            if tt + 1 < 16:
                prep(tt + 1)
            for nb in range(4):
                pb = k.bank()
                for kc in range(8):
                    k.mm(pb[:], yt_[:, kc, :], wgl[:, kc, nb * 512:(nb + 1) * 512], kc == 0, kc == 7, [yt_, wgl], [pb])
                if nb < 2:
                    k.tt(a[:, nb * 512:(nb + 1) * 512], pb[:], glub[:, nb * 512:(nb + 1) * 512], ALU.add, [pb, glub], [a])
                else:
                    k.tt(sg_[:, (nb - 2) * 512:(nb - 1) * 512], pb[:], glub[:, nb * 512:(nb + 1) * 512], ALU.add, [pb, glub], [sg_])
            k.act(sg_[:], sg_[:], AF.Sigmoid, [sg_], [sg_])
            k.tt(a[:], a[:], sg_[:], ALU.mult, [a, sg_], [a])
            rms_rows(k, tmps, a, 1024, bgs[:], m[:], m)
            k.sto(SC["MIX"][tt * 128:(tt + 1) * 128, 0:1024], m[:], m, [RS["MIX"]])


def stage_hyena(k, env):
    I, SC, RS = env["I"], env["SC"], env["RS"]
    cols = k.cols
    m0, m2 = cols[:, 4:5], cols[:, 5:6]
    with ExitStack() as st:
        z = k.sb(st, "hz", [128, 16, 512], BF16)
        xg = [k.sb(st, "hx%d" % i, [128, 16, 512]) for i in range(2)]
        yf = k.sb(st, "hyf", [128, 32, 512], BF16)
        for cb in range(2):
            sf = ExitStack()
            sf.__enter__()
            fwr = [k.sb(sf, "hfwr%d" % i, [128, 16, 128], BF16) for i in range(2)]
            fwi = [k.sb(sf, "hfwi%d" % i, [128, 16, 128], BF16) for i in range(2)]
            hre = [k.sb(sf, "hre%d" % i, [128, 512]) for i in range(2)]
            him = [k.sb(sf, "him%d" % i, [128, 512]) for i in range(2)]
            ya = [k.sb(sf, "hya%d" % i, [128, 512]) for i in range(2)]
            yb_ = [k.sb(sf, "hyb%d" % i, [128, 512]) for i in range(2)]
            yo = [k.sb(sf, "hyo%d" % i, [128, 512]) for i in range(2)]
            sa_ = ExitStack()
            sa_.__enter__()
            NB_ = 2
            cw = k.sb(sa_, "hcw", [128, 3, 3, 512])
            cbv = k.sb(sa_, "hcb", [128, 3, 512])
            cu = [k.sb(sa_, "hcu%d" % i, [128, 512]) for i in range(NB_)]
            t1 = [k.sb(sa_, "ht1%d" % i, [128, 512]) for i in range(NB_)]
            t2 = [k.sb(sa_, "ht2%d" % i, [128, 512]) for i in range(NB_)]
            t3 = [k.sb(sa_, "ht3%d" % i, [128, 512]) for i in range(NB_)]
            for sg in range(3):
                c0 = sg * 1024 + cb * 512
                for tap in range(3):
                    k.ld(cw[:, tap, sg, :], I["conv_w"][tap:tap + 1, c0:c0 + 512].broadcast_to([128, 512]), cw)
                k.ld(cbv[:, sg, :], I["conv_b"][0:1, c0:c0 + 512].broadcast_to([128, 512]), cbv)
            shp = k.sb(sa_, "shp", [128, 128])
            shn = k.sb(sa_, "shn", [128, 128])
            k.ld(shp[:], I["SHP"], shp)
            k.ld(shn[:], I["SHN"], shn)
            itc = [0]

            def conv_tile(sg, tt):
                c0 = sg * 1024 + cb * 512
                c_, a_, b_, e_ = (x[itc[0] % NB_] for x in (cu, t1, t2, t3))
                itc[0] += 1
                r0 = tt * 128
                k.ld(c_[:], SC["PH"][r0:r0 + 128, c0:c0 + 512], c_, r=[RS["PH"]])
                pp = k.bank()
                pn = k.bank()
                k.mm(pp[:], shp[:], c_[:], True, True, [shp, c_], [pp])
                k.mm(pn[:], shn[:], c_[:], True, True, [shn, c_], [pn])
                k.tt(b_[:], c_[:], cw[:, 1, sg, :], ALU.mult, [c_, cw], [b_], eng="gpsimd")
                k.tt(a_[:], pp[:], cw[:, 0, sg, :], ALU.mult, [pp, cw], [a_])
                k.tt(e_[:], pn[:], cw[:, 2, sg, :], ALU.mult, [pn, cw], [e_])
                k.tt(a_[:], a_[:], b_[:], ALU.add, [a_, b_], [a_])
                k.tt(e_[:], e_[:], cbv[:, sg, :], ALU.add, [e_, cbv], [e_])
                if sg == 0:
                    k.tt(z[:, tt, :], a_[:], e_[:], ALU.add, [a_, e_], [z])
                else:
                    k.tt(xg[sg - 1][:, tt, :], a_[:], e_[:], ALU.add, [a_, e_], [xg[sg - 1]])

            def conv_rest():
                for sg in (1, 2):
                    for tt in range(16):
                        conv_tile(sg, tt)
                        yield

            def fwd(o):
                hc0 = o * 1024 + cb * 512
                for kk in range(16):
                    fr, fi = fwr[kk % 2], fwi[kk % 2]
                    k.ld(fr[:], I["FWD"][kk], fr)
                    k.ld(fi[:], I["FWD"][16 + kk], fi)
                    hr, hi = hre[kk % 2], him[kk % 2]
                    k.ld(hr[:], SC["HF"][kk * 128:(kk + 1) * 128, hc0:hc0 + 512], hr, r=[RS["HF"]])
                    k.ld(hi[:], SC["HF"][2048 + kk * 128:2048 + (kk + 1) * 128, hc0:hc0 + 512], hi, r=[RS["HF"]])
                    pre = k.bank()
                    pim = k.bank()
                    for tt in range(16):
                        k.mm(pre[:], fr[:, tt, :], z[:, tt, :], tt == 0, tt == 15, [fr, z], [pre])
                    for tt in range(16):
                        k.mm(pim[:], fi[:, tt, :], z[:, tt, :], tt == 0, tt == 15, [fi, z], [pim])
                    a_, b_ = ya[kk % 2], yb_[kk % 2]
                    k.tt(a_[:], pre[:], hr[:], ALU.mult, [pre, hr], [a_])
                    k.tt(b_[:], pim[:], hi[:], ALU.mult, [pim, hi], [b_])
                    k.tt(yf[:, kk, :], a_[:], b_[:], ALU.subtract, [a_, b_], [yf])
                    c_, d_ = (yo[0], yo[1]) if kk == 0 else (a_, b_)
                    k.tt(c_[:], pre[:], hi[:], ALU.mult, [pre, hi], [c_])
                    k.tt(d_[:], pim[:], hr[:], ALU.mult, [pim, hr], [d_])
                    k.tt(yf[:, 16 + kk, :], c_[:], d_[:], ALU.add, [c_, d_], [yf])
                    if kk == 0:
                        k.cp(yf[0:1, 0, :], a_[0:1, :], [a_, yf], [yf])
                        k.cp(yf[0:1, 16, :], b_[0:1, :], [b_, yf], [yf])
                    yield

            for tt in range(16):
                conv_tile(0, tt)
            cg = conv_rest()
            for _ in fwd(0):
                for _j in range(2):
                    next(cg, None)
            for _ in cg:
                pass
            sa_.__exit__(None, None, None)
            k.S.barrier()
            si = ExitStack()
            si.__enter__()
            ivt = [k.sb(si, "hivt%d" % i, [128, 32, 128], BF16) for i in range(2)]
            for o in range(2):
                if o == 1:
                    for _ in fwd(1):
                        pass
                for tt in range(16):
                    iv = ivt[tt % 2]
                    k.ld(iv[:], I["INV"][tt], iv)
                    pb = k.bank()
                    for m in range(32):
                        k.mm(pb[:], iv[:, m, :], yf[:, m, :], m == 0, m == 31, [iv, yf], [pb])
                    if o == 0:
                        k.tt(z[:, tt, :], pb[:], xg[0][:, tt, :], ALU.mult, [pb, xg[0]], [z])
                    else:
                        y_ = yo[tt % 2]
                        k.tt(y_[:], pb[:], xg[1][:, tt, :], ALU.mult, [pb, xg[1]], [y_])
                        k.sto(SC["YH"][tt * 128:(tt + 1) * 128, cb * 512:(cb + 1) * 512], y_[:], y_, [RS["YH"]])
            si.__exit__(None, None, None)
            sf.__exit__(None, None, None)
            k.S.barrier()
    k.S.barrier()
    with ExitStack() as st:
        bgh = k.sb(st, "bgh", [128, 1024])
        k.ld(bgh[:], I["rows1"][3:4, 1024:2048].broadcast_to([128, 1024]), bgh)
        tmps = norm_tmps(k, st, "hy")
        yv = [k.sb(st, "hyv%d" % i, [128, 1024]) for i in range(2)]
        mo = [k.sb(st, "hmo%d" % i, [128, 1024], BF16) for i in range(2)]
        for tt in range(16):
            y, m = yv[tt % 2], mo[tt % 2]
            k.ld(y[:], SC["YH"][tt * 128:(tt + 1) * 128, :], y, r=[RS["YH"]])
            rms_rows(k, tmps, y, 1024, bgh[:], m[:], m)
            k.sto(SC["MIX"][tt * 128:(tt + 1) * 128, 1024:2048], m[:], m, [RS["MIX"]])


def stage_out(k, env):
    I, SC, RS = env["I"], env["SC"], env["RS"]
    with ExitStack() as st:
        wo = k.sb(st, "wo", [128, NK, 2048], BF16)
        stg = [k.sb(st, "ostg%d" % i, [128, 4, 512]) for i in range(2)]
        load_cast_weight(k, I["w_out"], NK, wo, stg)
        g1row = k.sb(st, "g1row", [128, D])
        k.ld(g1row[:], SC["MOD"][0:1, 2 * D:3 * D].broadcast_to([128, D]), g1row, r=[RS["MOD"]])
        mx = [k.sb(st, "omx%d" % i, [128, D], BF16) for i in range(2)]
        mT = [k.sb(st, "omT%d" % i, [128, NK, 128], BF16) for i in range(2)]
        xt = [k.sb(st, "oxt%d" % i, [128, D]) for i in range(2)]
        x1 = [k.sb(st, "ox1%d" % i, [128, D]) for i in range(2)]
        srow2 = k.sb(st, "o_srow2", [128, D])
        shrow2 = k.sb(st, "o_shrow2", [128, D])
        n2 = k.sb(st, "o_n2", [128, D])
        k.ld(srow2[:], SC["MOD"][0:1, 4 * D:5 * D].broadcast_to([128, D]), srow2, r=[RS["MOD"]])
        k.ld(shrow2[:], SC["MOD"][0:1, 3 * D:4 * D].broadcast_to([128, D]), shrow2, r=[RS["MOD"]])
        k.ld(n2[:], I["rows1"][1:2, :].broadcast_to([128, D]), n2)
        k.stt(srow2[:], srow2[:], 1.0, n2[:], ALU.add, ALU.mult, [srow2, n2], [srow2])
        tmps2 = norm_tmps(k, st, "o2")
        hb2 = [k.sb(st, "o_hb%d" % i, [128, D], BF16) for i in range(2)]
        def prep(tt):
            m, mt, x = mx[tt % 2], mT[tt % 2], xt[tt % 2]
            k.ld(m[:], SC["MIX"][tt * 128:(tt + 1) * 128, :], m, r=[RS["MIX"]])
            k.ld(x[:], I["x"][tt * 128:(tt + 1) * 128, :], x)
            transpose_to(k, m, NK, mt, 0)
        prep(0)
        for tt in range(16):
            m, mt, x, o = mx[tt % 2], mT[tt % 2], xt[tt % 2], x1[tt % 2]
            if tt + 1 < 16:
                prep(tt + 1)
            for nb in range(4):
                pb = k.bank()
                for kc in range(NK):
                    k.mm(pb[:], mt[:, kc, :], wo[:, kc, nb * 512:(nb + 1) * 512], kc == 0, kc == NK - 1, [mt, wo], [pb])
                sl = slice(nb * 512, (nb + 1) * 512)
                k.tt(o[:, sl], pb[:], g1row[:, sl], ALU.mult, [pb, g1row], [o])
                k.tt(o[:, sl], o[:, sl], x[:, sl], ALU.add, [o, x], [o])
            k.sto(SC["X1"][tt * 128:(tt + 1) * 128, :], o[:], o, [RS["X1"]])
            h_ = hb2[tt % 2]
            norm_rows(k, tmps2, o, 128, srow2, shrow2, h_)
            k.sto(SC["H2"][tt * 128:(tt + 1) * 128, :], h_[:], h_, [RS["H2"]])


def stage_ffn(k, env):
    I, SC, RS, OUT = env["I"], env["SC"], env["RS"], env["OUT"]
    RO = Res("OUT")
    NT = T // 128
    sto_ = ExitStack()
    sto_.__enter__()
    wdb0 = k.sb(sto_, "f_wdb0", [128, NFC, 512], BF16)
    stg = [k.sb(sto_, "f_stg%d" % i, [128, 4, 512]) for i in range(2)]
    with ExitStack() as st:
        h2T = k.sb(st, "h2T", [128, NK, T], BF16)
        with ExitStack() as s0:
            hb = [k.sb(s0, "f_h%d" % i, [128, D], BF16) for i in range(2)]
            for tt in range(NT):
                h = hb[tt % 2]
                k.ld(h[:], SC["H2"][tt * 128:(tt + 1) * 128, :], h, r=[RS["H2"]])
                transpose_to(k, h, NK, h2T, tt * 128)
        k.S.barrier()
        with ExitStack() as sa:
            wgs = [k.sb(sa, "f_wgs%d" % i, [128, NK, 128]) for i in range(2)]
            wus = [k.sb(sa, "f_wus%d" % i, [128, NK, 128]) for i in range(2)]
            wgb = [k.sb(sa, "f_wgb%d" % i, [128, NK, 128], BF16) for i in range(2)]
            wub = [k.sb(sa, "f_wub%d" % i, [128, NK, 128], BF16) for i in range(2)]
            sil = [k.sb(sa, "f_sil%d" % i, [128, 512]) for i in range(2)]
            hc = [k.sb(sa, "f_hc%d" % i, [128, T], BF16) for i in range(2)]
            it = 0
            for fc in range(NFC):
                a, b, ab, bb, hcc = wgs[fc % 2], wus[fc % 2], wgb[fc % 2], wub[fc % 2], hc[fc % 2]
                k.ld(a[:], I["wg"][:, fc * 128:(fc + 1) * 128].rearrange("(kc p) n -> p kc n", p=128), a)
                k.ld(b[:], I["wu"][:, fc * 128:(fc + 1) * 128].rearrange("(kc p) n -> p kc n", p=128), b)
                k.cp(ab[:], a[:], [a], [ab], eng="scalar")
                k.cp(bb[:], b[:], [b], [bb], eng="vector")
                for tb in range(T // 512):
                    sl_ = sil[it % 2]
                    it += 1
                    pg = k.bank()
                    pu = k.bank()
                    ts_ = slice(tb * 512, (tb + 1) * 512)
                    for kc in range(NK):
                        k.mm(pg[:], ab[:, kc, :], h2T[:, kc, ts_], kc == 0, kc == NK - 1, [ab, h2T], [pg])
                    for kc in range(NK):
                        k.mm(pu[:], bb[:, kc, :], h2T[:, kc, ts_], kc == 0, kc == NK - 1, [bb, h2T], [pu])
                    k.act(sl_[:], pg[:], AF.Silu, [pg], [sl_])
                    k.tt(hcc[:, ts_], sl_[:], pu[:], ALU.mult, [sl_, pu], [hcc])
                k.sto(SC["HTS"][:, :, fc, :].rearrange("tt p t -> p tt t"), hcc[:].rearrange("p (tt t) -> p tt t", t=128), hcc, [RS["HTS"]])
                if fc == 16:
                    load_cast_weight(k, I["wd"], NFC, wdb0, stg, n0=0, ncols=512, c0=0)
    k.S.barrier()
    with ExitStack() as st:
        wdb = [wdb0, k.sb(st, "f_wdb1", [128, NFC, 512], BF16)]
        hts = [k.sb(st, "f_hts%d" % i, [128, NFC, 128], BF16) for i in range(2)]
        g2row = k.sb(st, "g2row", [128, D])
        k.ld(g2row[:], SC["MOD"][0:1, 5 * D:6 * D].broadcast_to([128, D]), g2row, r=[RS["MOD"]])
        x1p = [k.sb(st, "f_x1p%d" % i, [128, 512]) for i in range(2)]
        op_ = [k.sb(st, "f_op%d" % i, [128, 512]) for i in range(2)]
        finrow = k.sb(st, "finrow", [128, D])
        k.ld(finrow[:], I["rows1"][2:3, :].broadcast_to([128, D]), finrow)
        tmps = norm_tmps(k, st, "f1")
        xs = [k.sb(st, "f_x2%d" % i, [128, D]) for i in range(2)]
        os_ = [k.sb(st, "f_o%d" % i, [128, D]) for i in range(2)]
        it = 0
        for nb in range(4):
            wb_ = wdb[nb % 2]
            if nb + 1 < 4:
                load_cast_weight(k, I["wd"], NFC, wdb[(nb + 1) % 2], stg, n0=0, ncols=512, c0=(nb + 1) * 512)
            for tt in range(NT):
                h_, xp, o_ = hts[it % 2], x1p[it % 2], op_[it % 2]
                it += 1
                k.ld(h_[:], SC["HTS"][tt], h_, r=[RS["HTS"]])
                k.ld(xp[:], SC["X1"][tt * 128:(tt + 1) * 128, nb * 512:(nb + 1) * 512], xp, r=[RS["X1"]])
                pb = k.bank()
                for fc in range(NFC):
                    k.mm(pb[:], h_[:, fc, :], wb_[:, fc, :], fc == 0, fc == NFC - 1, [h_, wb_], [pb])
                k.tt(o_[:], pb[:], g2row[:, nb * 512:(nb + 1) * 512], ALU.mult, [pb, g2row], [o_])
                if nb < 3:
                    k.tt(o_[:], o_[:], xp[:], ALU.add, [o_, xp], [o_])
                    k.sto(SC["X2"][tt * 128:(tt + 1) * 128, nb * 512:(nb + 1) * 512], o_[:], o_, [RS["X2"]], tracked=True)
                else:
                    x, o = xs[tt % 2], os_[tt % 2]
                    k.ld(x[:, 0:1536], SC["X2"][tt * 128:(tt + 1) * 128, 0:1536], x, r=[RS["X2"]])
                    k.tt(x[:, 1536:2048], o_[:], xp[:], ALU.add, [o_, xp], [x])
                    rms_rows(k, tmps, x, D, finrow[:], o[:], o)
                    k.sto(OUT[tt * 128:(tt + 1) * 128, :], o[:], o, [RO])
    sto_.__exit__(None, None, None)


def kernel(**inputs):
    shared = prep_shared(inputs)
    x = np.asarray(inputs["x"], np.float32)
    c = np.asarray(inputs["c"], np.float32)
    ctx = np.asarray(inputs["ctx"], np.float32)
    c_ctx = np.asarray(inputs["c_ctx"], np.float32)
    ncores = 8
    nc = build_program()
    in_maps = []
    for b in range(ncores):
        m = dict(shared)
        m["x"] = np.ascontiguousarray(x[b])
        m["ctx"] = np.ascontiguousarray(ctx[b])
        cv = np.stack([c[b].reshape(NK, 128).T, c_ctx.reshape(NK, 128).T], axis=-1)
        m["cvec"] = np.ascontiguousarray(cv.astype(np.float32))
        in_maps.append(m)
    res = run_bass_kernel_spmd(nc, in_maps, core_ids=list(range(ncores)))
    out = np.stack([np.asarray(r["out"], np.float32) for r in res.results], axis=0)
    return out
```

```python
import os
import math
from contextlib import ExitStack

import numpy as np
import ml_dtypes

import concourse.bass as bass
import concourse.mybir as mybir
from concourse.bass_utils import run_bass_kernel_spmd

F32 = mybir.dt.float32
BF16 = mybir.dt.bfloat16
AF = mybir.ActivationFunctionType
ALU = mybir.AluOpType

D = 2048
T = 2048
CT = 256
NK = 16
S5W = 1024
HYW = 1024
G = 64
DFF = 5632
NFC = DFF // 128
EPS = 1e-6
MAGIC = 12582912.0
TWO_PI_LO = 6.28318
HALF_PI_LO = 1.570795

ENGS = ["tensor", "vector", "scalar", "gpsimd", "sync"]
CONV_MODE = "dram"
CONV_NBUF = 3


class Res:
    _n = 0

    def __init__(self, name):
        Res._n += 1
        self.id = Res._n
        self.name = name
        self.w = None
        self.r = []
        self.sem = None
        self.semid = None
        self.dtot = 0


class Tl:
    def __init__(self, t, R):
        self.t = t
        self.R = R

    def __getitem__(self, k):
        return self.t[k]


def _R(x):
    return x.R if isinstance(x, Tl) else x


class Sched:
    def __init__(self, nc, stack):
        self.nc = nc
        self.stack = stack
        self.ops = {e: [] for e in ENGS}
        self.cnt = {e: 0 for e in ENGS}
        self.esem = {e: stack.enter_context(nc.semaphore("es_" + e)) for e in ENGS if e != "sync"}
        self.waited = {e: {} for e in ENGS}
        self.dres = []
        self.pool = {"sync": [], "gpsimd": [], "scalar": []}
        self.nsem = 0
        self.epoch = 0
        self.bar_tile = stack.enter_context(nc.sbuf_tensor("bar_tile", [128, 1], F32))

    def _need(self, eng, ev):
        if ev is None:
            return
        if ev[0] == "E":
            q, v = ev[1], ev[2]
            if q == eng and (q == "tensor" or (q in ("vector", "scalar") and len(ev) > 3 and not ev[3])):
                return
            if self.waited[eng].get(q, 0) >= v:
                return
            self.waited[eng][q] = v
            sem = self.esem[q]
            self.ops[eng].append(lambda e, sem=sem, v=v: e.wait_ge(sem, v))
        else:
            r = ev[1]
            if ev[2] < self.epoch or r.sem is None:
                return
            v = ev[3] if len(ev) > 3 else r.dtot
            key = ("S", r.semid)
            if self.waited[eng].get(key, 0) >= v:
                return
            self.waited[eng][key] = v
            sem = r.sem
            self.ops[eng].append(lambda e, sem=sem, v=v: e.wait_ge(sem, v))

    def _deps(self, eng, reads, writes):
        for r in reads:
            self._need(eng, r.w)
        for r in writes:
            self._need(eng, r.w)
            for ev in r.r:
                self._need(eng, ev)

    def _commit(self, ev, reads, writes):
        for r in reads:
            r.r.append(ev)
            if len(r.r) > 16:
                best = {}
                keep = []
                for x in r.r:
                    if x[0] == "E":
                        if best.get(x[1], 0) < x[2]:
                            best[x[1]] = x[2]
                    elif x[2] >= self.epoch and x not in keep:
                        keep.append(x)
                r.r = keep + [("E", q, v, True) for q, v in best.items()]
        for r in writes:
            r.w = ev
            r.r = []

    def op(self, eng, fn, reads=(), writes=(), small=True):
        reads = [_R(x) for x in reads]
        writes = [_R(x) for x in writes]
        self._deps(eng, reads, writes)
        self.cnt[eng] += 1
        v = self.cnt[eng]
        sem = self.esem[eng]
        self.ops[eng].append(lambda e, fn=fn, sem=sem: fn(e).then_inc(sem, 1))
        self._commit(("E", eng, v, small), reads, writes)

    def dma(self, q, out, in_, reads=(), writes=(), owner=None, exact=False):
        reads = [_R(x) for x in reads]
        writes = [_R(x) for x in writes]
        if owner is None:
            owner = writes[0] if writes else reads[0]
        owner = _R(owner)
        if owner.sem is None:
            if self.pool[q]:
                owner.sem, owner.semid, owner.dtot = self.pool[q].pop()
            else:
                self.nsem += 1
                owner.sem = self.stack.enter_context(self.nc.semaphore("ds_%d" % self.nsem))
                owner.semid = self.nsem
                owner.dtot = 0
            owner.semq = q
            self.dres.append(owner)
        assert owner.semq == q, "DMA owner semaphore shared between queues"
        self._deps(q, reads, writes)
        owner.dtot += 16
        sem = owner.sem
        self.ops[q].append(lambda e, out=out, in_=in_, sem=sem: e.dma_start(out=out, in_=in_).then_inc(sem, 16))
        if exact:
            self._commit(("D", owner, self.epoch, owner.dtot), reads, writes)
        else:
            self._commit(("D", owner, self.epoch), reads, writes)

    def barrier(self):
        c = "gpsimd"
        for q in ENGS:
            if q not in ("sync", c) and self.cnt[q] > 0:
                self._need(c, ("E", q, self.cnt[q]))
        for r in self.dres:
            self._need(c, ("D", r, self.epoch))
        bt = self.bar_tile
        self.op(c, lambda e: e.memset(bt[:], 0.0), [], [])
        v = self.cnt[c]
        for e in ENGS:
            if e != c:
                self._need(e, ("E", c, v))
            for q in ENGS:
                if q != "sync":
                    self.waited[e][q] = max(self.waited[e].get(q, 0), self.cnt[q])
        for r in self.dres:
            self.pool[r.semq].append((r.sem, r.semid, r.dtot))
            r.sem = None
        self.dres = []
        self.epoch += 1

    def finish(self):
        self.barrier()
        nc = self.nc
        ops = self.ops
        with nc.Block() as block:
            @block.tensor
            def _(e):
                for f in ops["tensor"]:
                    f(e)

            @block.vector
            def _(e):
                for f in ops["vector"]:
                    f(e)

            @block.scalar
            def _(e):
                for f in ops["scalar"]:
                    f(e)

            @block.gpsimd
            def _(e):
                for f in ops["gpsimd"]:
                    f(e)

            @block.sync
            def _(e):
                for f in ops["sync"]:
                    f(e)


class K:
    def __init__(self, nc, st):
        self.nc = nc
        self.st = st
        self.S = Sched(nc, st)
        self.banks = []
        for i in range(8):
            t = st.enter_context(nc.psum_tensor("psb%d" % i, [128, 512], F32))
            self.banks.append(Tl(t, Res("psb%d" % i)))
        self.bi = 0
        self.brange = (0, 8)
        self.nuniq = 0

    def bank(self):
        lo, hi = self.brange
        b = self.banks[lo + self.bi % (hi - lo)]
        self.bi += 1
        return b

    def sb(self, st, name, shape, dt=F32):
        self.nuniq += 1
        nm = "%s_%d" % (name, self.nuniq)
        return Tl(st.enter_context(self.nc.sbuf_tensor(nm, list(shape), dt)), Res(nm))

    def ld(self, out, in_, w, r=()):
        self.S.dma("sync", out, in_, reads=r, writes=[w])

    def sto(self, out, in_, r, w=(), own_src=False, tracked=False):
        rl = [r] if not isinstance(r, (list, tuple)) else list(r)
        if not tracked and not own_src:
            if not hasattr(self, "_rot"):
                self._rot = [Res("rot%d" % i) for i in range(4)]
                self._nrot = 0
            owner = self._rot[self._nrot % 4]
            self._nrot += 1
            self.S.dma("gpsimd", out, in_, reads=rl, writes=[], owner=owner, exact=True)
            return
        rl = [r] if not isinstance(r, (list, tuple)) else list(r)
        owner = w[0] if w else None
        if own_src:
            src = _R(rl[0])
            if src.sem is None or getattr(src, "semq", None) == "gpsimd":
                owner = src
        self.S.dma("gpsimd", out, in_, reads=rl, writes=list(w), owner=owner)

    @staticmethod
    def _small(ap):
        n = 1
        for d in list(ap.shape)[1:]:
            n *= int(d)
        return n < 128

    def cp(self, out, in_, r, w, eng="vector"):
        if eng == "scalar":
            self.S.op(eng, lambda e: e.activation(out=out, in_=in_, func=AF.Copy), r, w, small=self._small(out))
        else:
            self.S.op(eng, lambda e: e.tensor_copy(out=out, in_=in_), r, w, small=self._small(out))

    def tt(self, out, in0, in1, op, r, w, eng="vector"):
        self.S.op(eng, lambda e: e.tensor_tensor(out=out, in0=in0, in1=in1, op=op), r, w, small=self._small(out))

    def ts(self, out, in0, s1, s2, op0, op1, r, w, eng="vector"):
        sm = self._small(out)
        if s2 is None:
            self.S.op(eng, lambda e: e.tensor_scalar(out=out, in0=in0, scalar1=s1, scalar2=None, op0=op0), r, w, small=sm)
        else:
            self.S.op(eng, lambda e: e.tensor_scalar(out=out, in0=in0, scalar1=s1, scalar2=s2, op0=op0, op1=op1), r, w, small=sm)

    def stt(self, out, in0, scalar, in1, op0, op1, r, w):
        self.S.op("vector", lambda e: e.scalar_tensor_tensor(out=out, in0=in0, scalar=scalar, in1=in1, op0=op0, op1=op1), r, w,
                  small=self._small(out))

    def act(self, out, in_, func, r, w, scale=1.0, bias=None, accum=None):
        def f(e):
            kw = {}
            if bias is not None:
                kw["bias"] = bias
            if accum is not None:
                kw["accum_out"] = accum
            return e.activation(out=out, in_=in_, func=func, scale=scale, **kw)
        self.S.op("scalar", f, r, w, small=(accum is not None) or self._small(out))

    def mm(self, out, lhsT, rhs, start, stop, r, w):
        self.S.op("tensor", lambda e: e.matmul(out, lhsT=lhsT, rhs=rhs, start=start, stop=stop), r, w)

    def tr(self, out, in_, ident, r, w):
        self.S.op("tensor", lambda e: e.transpose(out, in_, ident), r, w)

    def recip(self, out, in_, r, w):
        self.S.op("vector", lambda e: e.reciprocal(out=out, in_=in_), r, w)

    def memset(self, out, val, w, eng="vector"):
        self.S.op(eng, lambda e: e.memset(out, val), [], w)

    def scan(self, out, d0, d1, init, r, w):
        self.S.op("vector", lambda e: e.tensor_tensor_scan(out=out, data0=d0, data1=d1, initial=init, op0=ALU.mult, op1=ALU.add), r, w,
                  small=False)

    def rsum(self, out, in_, r, w):
        self.S.op("vector", lambda e: e.reduce_sum(out=out, in_=in_, axis=mybir.AxisListType.X), r, w)

    def frac_turns(self, y, rr, yT, rrT):
        self.ts(rr, y, MAGIC, None, ALU.add, None, [yT], [rrT])
        self.ts(rr, rr, MAGIC, None, ALU.subtract, None, [rrT], [rrT])
        self.tt(y, y, rr, ALU.subtract, [yT, rrT], [yT])

    def sincos(self, st, ang, shape, sin_out, cos_out, r, w_sin, w_cos, tag):
        y = self.sb(st, tag + "_y", shape)
        rr = self.sb(st, tag + "_r", shape)
        self.sincos2(ang, y, rr, sin_out, cos_out, r, w_sin, w_cos)

    def sincos2(self, ang, y, rr, sin_out, cos_out, r, w_sin, w_cos, turns=False):
        if turns:
            self.cp(y[:], ang, r, [y])
        else:
            self.ts(y[:], ang, 1.0 / (2 * math.pi), None, ALU.mult, None, r, [y])
        self._sincos_tail(y, rr, sin_out, cos_out, w_sin, w_cos)

    def _sincos_tail(self, y, rr, sin_out, cos_out, w_sin, w_cos):
        r = None
        self.ts(rr[:], y[:], MAGIC, None, ALU.add, None, [y], [rr])
        self.ts(rr[:], rr[:], MAGIC, None, ALU.subtract, None, [rr], [rr])
        self.tt(y[:], y[:], rr[:], ALU.subtract, [y, rr], [y])
        if sin_out is not None:
            self.act(sin_out, y[:], AF.Sin, [y], w_sin, scale=TWO_PI_LO)
        if cos_out is not None:
            self.stt(rr[:], y[:], -1.0, y[:], ALU.mult, ALU.max, [y], [rr])
            self.act(cos_out, rr[:], AF.Sin, [rr], w_cos, scale=-TWO_PI_LO, bias=self.halfpi[:, 0:1])


_CONST_CACHE = {}


def host_consts():
    if _CONST_CACHE:
        return _CONST_CACHE
    N = 4096
    n = 2048
    idx = np.arange(n, dtype=np.int64)
    ang = 2.0 * np.pi * ((idx[:, None] * idx[None, :]) % N).astype(np.float64) / N
    cs = np.cos(ang)
    sn = np.sin(ang)
    alt = (-1.0) ** idx
    fwd = np.empty((n, N), np.float64)
    fwd[:, :n] = cs
    fwd[:, n:] = -sn
    fwd[:, n] = alt
    inv = np.empty((N, n), np.float64)
    inv[:n] = (2.0 / N) * cs
    inv[0] = 1.0 / N
    inv[n:] = -(2.0 / N) * sn
    inv[n] = alt / N
    c = _CONST_CACHE
    fw16 = fwd.astype(np.float32).astype(ml_dtypes.bfloat16)
    iv16 = inv.astype(np.float32).astype(ml_dtypes.bfloat16)
    c["FWD"] = np.ascontiguousarray(fw16.reshape(16, 128, 32, 128).transpose(2, 1, 0, 3))
    c["INV"] = np.ascontiguousarray(iv16.reshape(32, 128, 16, 128).transpose(2, 1, 0, 3))
    pos = np.arange(n, dtype=np.float32)
    tt = pos[:, None] / np.float32(n)
    bands = np.linspace(1e-4, 15, 16, dtype=np.float32)
    a2 = (np.float32(2.0 * math.pi) * pos[:, None] * bands[None, :] / np.float32(n)).astype(np.float32)
    z = np.concatenate([tt, np.cos(a2), -np.sin(a2)], axis=-1).astype(np.float32)
    c["ZT"] = np.ascontiguousarray(z.T)
    c["IDENT"] = np.eye(128, dtype=np.float32)
    c["IDENTB"] = np.eye(128, dtype=np.float32).astype(ml_dtypes.bfloat16)
    sig = np.arange(128) // 16
    c["MASKF"] = (sig[None, :] >= sig[:, None]).astype(np.float32)
    c["MASKB"] = (sig[:, None] >= sig[None, :]).astype(np.float32)
    ex = np.zeros((3, 128, 8), np.float32)
    i8 = np.arange(8, dtype=np.float32)
    ex[0, :64] = 7 - i8
    ex[0, 64:] = i8
    ex[1, :64] = i8 - 7
    ex[1, 64:] = -i8
    ex[2, :64] = i8 + 1
    ex[2, 64:] = 8 - i8
    c["EXPO"] = np.ascontiguousarray(np.broadcast_to(ex[None], (128, 3, 128, 8))).astype(np.float32)
    cols = np.zeros((128, 16), np.float32)
    p = np.arange(128)
    cols[:, 0] = 1.0
    cols[:, 1] = np.where(p < 64, -1.0, 1.0)
    cols[:, 2] = np.where(p < 64, 1.0, -1.0)
    cols[:, 3] = -1.0
    cols[:, 4] = (p % 64 != 0)
    cols[:, 5] = (p % 64 != 63)
    cols[:, 6] = HALF_PI_LO
    cols[:, 7] = (p != 0)
    cols[:, 8] = alt[:128]
    for q in range(16):
        pass
    c["COLS"] = cols
    shp = np.zeros((128, 128), np.float32)
    shn = np.zeros((128, 128), np.float32)
    for m in range(128):
        if m % 64 != 0:
            shp[m - 1, m] = 1.0
        if m % 64 != 63:
            shn[m + 1, m] = 1.0
    c["SHP"] = shp
    c["SHN"] = shn
    c["POSF"] = (-(np.arange(16)[None, :] * 128 + p[:, None]) / 2048.0).astype(np.float32)
    c["IOTA"] = np.ascontiguousarray(np.broadcast_to(np.arange(1, 257, dtype=np.float32)[None], (128, 256)))
    kk = np.zeros((128, 128, 32), np.float32)
    kk[:, :64] = 31 - np.arange(32)
    kk[:, 64:] = np.arange(32)
    c["KCTX"] = kk
    return c


def prep_shared(inp):
    f = lambda k: np.ascontiguousarray(np.asarray(inp[k], np.float32))
    o = {}
    o["ada_w"] = f("ada_w")[0]
    o["ada_b"] = f("ada_b")
    o["w_in"] = f("w_in")[0]
    o["rows1"] = np.ascontiguousarray(np.stack([f("norm1_g")[0], f("norm2_g")[0], f("final_g"),
                                                np.concatenate([f("branch_g_s5")[0], f("branch_g_hy")[0]])]))
    o["conv_w"] = f("conv_w")[0]
    o["conv_b"] = f("conv_b")
    o["hy_w1"] = f("hy_w1")[0]
    o["hy_w2"] = f("hy_w2")[0]
    o["hy_w3"] = f("hy_w3")[0]
    o["hy_cols"] = np.ascontiguousarray(np.stack([f("hy_b1")[0], f("hy_b2")[0], f("hy_sin_freq")[0, 0], f("hy_sin_freq")[0, 1]], axis=1))
    o["hy_decay"] = f("hy_decay").reshape(1, 4096)
    o["hy_bias"] = f("hy_bias").reshape(1, 2048)
    lr = f("s5_lam_re")[0].reshape(128, 64).T
    li = f("s5_lam_im")[0].reshape(128, 64).T
    o["LAMR2"] = np.ascontiguousarray(np.concatenate([lr, lr], 0))
    o["LAMI2"] = np.ascontiguousarray(np.concatenate([li, li], 0))
    o["LDT"] = np.ascontiguousarray(np.broadcast_to(f("s5_log_dt")[0].reshape(1, 128), (128, 128)))
    br = f("s5_b_re")[0].reshape(128, 64, 16).transpose(1, 0, 2)
    bi = f("s5_b_im")[0].reshape(128, 64, 16).transpose(1, 0, 2)
    o["XB1"] = np.ascontiguousarray(np.concatenate([br, bi], 0))
    o["XB2"] = np.ascontiguousarray(np.concatenate([bi, br], 0))
    cr = f("s5_c_re")[0].reshape(128, 16, 64).transpose(2, 0, 1)
    ci = f("s5_c_im")[0].reshape(128, 16, 64).transpose(2, 0, 1)
    o["XC1"] = np.ascontiguousarray(np.concatenate([cr, ci], 0))
    o["XC2"] = np.ascontiguousarray(np.concatenate([ci, cr], 0))
    dd = f("s5_d")[0]
    o["DCOL"] = np.ascontiguousarray(np.tile(dd.T, (8, 1)))
    o["glu_w"] = f("s5_glu_w")[0]
    o["glu_b"] = f("s5_glu_b")
    o["w_out"] = f("w_out")[0]
    o["wg"] = f("ffn_w_gate")[0]
    o["wu"] = f("ffn_w_up")[0]
    o["wd"] = f("ffn_w_down")[0]
    o.update(host_consts())
    return o


IN_SPECS = None


def in_specs():
    return [
        ("x", [T, D], F32), ("ctx", [CT, D], F32), ("cvec", [128, NK, 2], F32),
        ("ada_w", [D, 6 * D], F32), ("ada_b", [1, 6 * D], F32), ("w_in", [D, 4096], F32),
        ("rows1", [4, D], F32), ("conv_w", [3, 3072], F32), ("conv_b", [1, 3072], F32),
        ("hy_w1", [33, 64], F32), ("hy_w2", [64, 64], F32), ("hy_w3", [64, 4096], F32), ("hy_cols", [64, 4], F32),
        ("hy_decay", [1, 4096], F32), ("hy_bias", [1, 2048], F32),
        ("LAMR2", [128, 128], F32), ("LAMI2", [128, 128], F32), ("LDT", [128, 128], F32),
        ("XB1", [128, 128, 16], F32), ("XB2", [128, 128, 16], F32), ("XC1", [128, 128, 16], F32), ("XC2", [128, 128, 16], F32),
        ("DCOL", [128, 64], F32), ("glu_w", [S5W, 2048], F32), ("glu_b", [1, 2048], F32), ("w_out", [D, D], F32),
        ("wg", [D, DFF], F32), ("wu", [D, DFF], F32), ("wd", [DFF, D], F32),
        ("FWD", [32, 128, 16, 128], BF16), ("INV", [16, 128, 32, 128], BF16), ("ZT", [33, 2048], F32),
        ("IDENT", [128, 128], F32), ("IDENTB", [128, 128], BF16), ("MASKF", [128, 128], F32), ("MASKB", [128, 128], F32),
        ("EXPO", [128, 3, 128, 8], F32), ("COLS", [128, 16], F32), ("POSF", [128, 16], F32), ("IOTA", [128, 256], F32),
        ("KCTX", [128, 128, 32], F32), ("SHP", [128, 128], F32), ("SHN", [128, 128], F32),
    ]


ALL_STAGES = ["mod", "proj", "filt", "s5prep", "s5", "glu", "hyena", "out", "ffn"]


def build_program(dbg=(), stages=None):
    stages = ALL_STAGES if stages is None else stages
    nc = bass.Bass("TRN2", target_bir_lowering=False)
    I = {}
    for name, shape, dt in in_specs():
        I[name] = nc.dram_tensor(name, list(shape), dt, kind="ExternalInput").ap()
    OUT = nc.dram_tensor("out", [T, D], F32, kind="ExternalOutput").ap()

    def scr(name, shape, dt=F32):
        if name in dbg:
            return nc.dram_tensor(name, list(shape), dt, kind="ExternalOutput").ap()
        return nc.dram_tensor(name, list(shape), dt).ap()

    SC = {}
    SC["MOD"] = scr("MOD", [2, 6 * D])
    SC["U"] = scr("U", [T, S5W], BF16)
    SC["UC"] = scr("UC", [CT, S5W], BF16)
    SC["PH"] = scr("PH", [T, 3 * HYW])
    SC["HT"] = scr("HT", [2048, 4096])
    SC["HF"] = scr("HF", [4096, 2048])
    SC["S5WT"] = scr("S5WT", [G, 128, 9 * 128], BF16)
    SC["S5P"] = scr("S5P", [128, 2, 128])
    SC["YS5"] = scr("YS5", [T, S5W], BF16)
    SC["YH"] = scr("YH", [T, HYW])
    SC["MIX"] = scr("MIX", [T, D], BF16)
    SC["X1"] = scr("X1", [T, D])
    SC["X0"] = scr("X0", [128, 128])
    SC["HTS"] = scr("HTS", [T // 128, 128, NFC, 128], BF16)
    SC["X2"] = scr("X2", [T, D])
    SC["H2"] = scr("H2", [T, D], BF16)
    RS = {k: Res("scr_" + k) for k in SC}

    with ExitStack() as st0:
        k = K(nc, st0)
        S = k.S
        cols = k.sb(st0, "cols", [128, 16])
        k.ld(cols[:], I["COLS"], cols)
        k.halfpi = cols[:, 6:7]
        k.cols = cols
        ident = k.sb(st0, "ident", [128, 128])
        k.ld(ident[:], I["IDENT"], ident)
        identb = k.sb(st0, "identb", [128, 128], BF16)
        k.ld(identb[:], I["IDENTB"], identb)
        k.ident = ident
        k.identb = identb
        env = dict(I=I, SC=SC, RS=RS, OUT=OUT, dbg=dbg)
        if "mod" in stages and "s5prep" in stages:
            with ExitStack() as stm:
                stage_s5prep(k, env, side=gen_mod(k, env, stm))
            S.barrier()
        elif "mod" in stages:
            stage_mod(k, env)
            S.barrier()
        if "proj" in stages:
            stage_proj_all(k, env)
            S.barrier()
        if "filt" in stages:
            stage_filt(k, env)
            S.barrier()
        if "s5prep" in stages and "mod" not in stages:
            stage_s5prep(k, env)
            S.barrier()
        if "s5" in stages:
            stage_s5(k, env)
            S.barrier()
        if "glu" in stages:
            stage_glu(k, env)
            S.barrier()
        if "hyena" in stages:
            stage_hyena(k, env)
            S.barrier()
        if "out" in stages:
            stage_out(k, env)
            S.barrier()
        if "ffn" in stages:
            stage_ffn(k, env)
        S.finish()
    return nc


def gen_mod(k, env, st):
    I, SC, RS = env["I"], env["SC"], env["RS"]
    NB, W = 48, 256
    cv = k.sb(st, "cv", [128, NK, 2])
    k.ld(cv[:], I["cvec"], cv)
    sl = k.sb(st, "sl", [128, NK, 2])
    k.act(sl[:], cv[:], AF.Silu, [cv], [sl])
    wts = [k.sb(st, "adaw%d" % i, [128, NK, W]) for i in range(3)]
    adab = [k.sb(st, "adab%d" % i, [2, W]) for i in range(3)]
    mods = [k.sb(st, "modsb%d" % i, [2, W]) for i in range(3)]

    def load(nb):
        k.ld(wts[nb % 3][:], I["ada_w"][:, nb * W:(nb + 1) * W].rearrange("(kc p) n -> p kc n", p=128), wts[nb % 3])
        k.ld(adab[nb % 3][:], I["ada_b"][0:1, nb * W:(nb + 1) * W].broadcast_to([2, W]), adab[nb % 3])
    load(0)
    load(1)
    for nb in range(NB):
        if nb + 2 < NB:
            load(nb + 2)
        yield
        wt, ab, mo = wts[nb % 3], adab[nb % 3], mods[nb % 3]
        pb = k.bank()
        for kc in range(NK):
            k.mm(pb[0:2, 0:W], sl[:, kc, :], wt[:, kc, :], kc == 0, kc == NK - 1, [sl, wt], [pb])
        k.tt(mo[:], pb[0:2, 0:W], ab[:], ALU.add, [pb, ab], [mo])
        k.sto(SC["MOD"][:, nb * W:(nb + 1) * W], mo[:], mo, [RS["MOD"]])


def stage_mod(k, env):
    with ExitStack() as st:
        for _ in gen_mod(k, env, st):
            pass


def run_with_side(main_gen, side_gen, every=2):
    n = 0
    side_live = True
    for _ in main_gen:
        n += 1
        if side_live and n % every == 0:
            try:
                next(side_gen)
            except StopIteration:
                side_live = False
    if side_live:
        for _ in side_gen:
            pass


def norm_tmps(k, st, tag):
    return (k.sb(st, tag + "junk", [128, D], BF16), k.sb(st, tag + "ssq", [128, 1]), k.sb(st, tag + "tmp", [128, D]))


def norm_rows(k, tmps, xt, n_part, srow, shrow, out_bf):
    junk, ssq, tmp = tmps
    k.act(junk[0:n_part, :], xt[0:n_part, :], AF.Square, [xt], [junk, ssq], accum=ssq[0:n_part, :])
    k.ts(ssq[0:n_part, :], ssq[0:n_part, :], 1.0 / D, EPS, ALU.mult, ALU.add, [ssq], [ssq])
    k.act(ssq[0:n_part, :], ssq[0:n_part, :], AF.Sqrt, [ssq], [ssq])
    k.recip(ssq[0:n_part, :], ssq[0:n_part, :], [ssq], [ssq])
    k.stt(tmp[0:n_part, :], xt[0:n_part, :], ssq[0:n_part, 0:1], srow[0:n_part, :], ALU.mult, ALU.mult, [xt, ssq, srow], [tmp])
    if shrow is not None:
        k.tt(out_bf[0:n_part, :], tmp[0:n_part, :], shrow[0:n_part, :], ALU.add, [tmp, shrow], [out_bf])
    else:
        k.cp(out_bf[0:n_part, :], tmp[0:n_part, :], [tmp], [out_bf])


def transpose_to(k, src_bf, nkc, dstT, col0, ncols=128):
    for h in range(0, nkc, 8):
        pb = k.bank()
        pbb = pb[:].bitcast(BF16)
        n = min(8, nkc - h)
        for j in range(n):
            k.tr(pbb[:, j * 128:j * 128 + ncols], src_bf[0:ncols, (h + j) * 128:(h + j + 1) * 128], k.identb[0:ncols, 0:ncols],
                 [src_bf, k.identb], [pb])
        src = pbb[:, 0:n * 128].rearrange("p (j c) -> p j c", c=128)[:, :, 0:ncols]
        k.cp(dstT[:, h:h + n, col0:col0 + ncols], src, [pb], [dstT], eng="scalar" if (h // 8) % 2 else "vector")


def stage_proj(k, env, ctx):
    I, SC, RS = env["I"], env["SC"], env["RS"]
    ntok = CT if ctx else T
    ntile = ntok // 128
    xin = I["ctx"] if ctx else I["x"]
    mrow = 1 if ctx else 0
    nbs = 2 if ctx else 8
    with ExitStack() as st:
        srow = k.sb(st, "srow", [128, D])
        shrow = k.sb(st, "shrow", [128, D])
        g1r = k.sb(st, "g1r", [128, D])
        k.ld(srow[:], SC["MOD"][mrow:mrow + 1, D:2 * D].broadcast_to([128, D]), srow, r=[RS["MOD"]])
        k.ld(shrow[:], SC["MOD"][mrow:mrow + 1, 0:D].broadcast_to([128, D]), shrow, r=[RS["MOD"]])
        k.ld(g1r[:], I["rows1"][0:1, :].broadcast_to([128, D]), g1r)
        k.stt(srow[:], srow[:], 1.0, g1r[:], ALU.add, ALU.mult, [srow, g1r], [srow])
        xnT = [k.sb(st, "xnT%d" % i, [128, NK, 128], BF16) for i in range(ntile)]
        stg = [k.sb(st, "wstg%d" % i, [128, 4, 512]) for i in range(2)]
        wbs = [k.sb(st, "wb%d" % i, [128, NK, 512], BF16) for i in range(2)]
        ob = [k.sb(st, "ob%d" % i, [128, 512], BF16) for i in range(4)]
        of = [k.sb(st, "of%d" % i, [128, 512]) for i in range(2)]
        xts = [k.sb(st, "xt%d" % i, [128, D]) for i in range(2)]
        xbs = [k.sb(st, "xb%d" % i, [128, D], BF16) for i in range(2)]
        ntm = norm_tmps(k, st, "n1")
        qq = [0]

        def load_w(nb):
            wb = wbs[nb % 2]
            for kq in range(4):
                sg = stg[qq[0] % 2]
                qq[0] += 1
                k.ld(sg[:], I["w_in"][kq * 512:(kq + 1) * 512, nb * 512:(nb + 1) * 512].rearrange("(kc p) n -> p kc n", p=128), sg)
                k.cp(wb[:, kq * 4:(kq + 1) * 4, :], sg[:], [sg], [wb], eng="scalar" if kq % 2 else "vector")

        def mm_tile(nb, tt):
            wb = wbs[nb % 2]
            pb = k.bank()
            for kc in range(NK):
                k.mm(pb[:], xnT[tt][:, kc, :], wb[:, kc, :], kc == 0, kc == NK - 1, [xnT[tt], wb], [pb])
            if nb < 2:
                o = ob[(2 * tt + nb) % 4]
                k.cp(o[:], pb[:], [pb], [o], eng="scalar" if tt % 2 else "vector")
                dst = SC["UC"] if ctx else SC["U"]
                k.sto(dst[tt * 128:(tt + 1) * 128, nb * 512:(nb + 1) * 512], o[:], o, [RS["UC" if ctx else "U"]])
            else:
                o = of[tt % 2]
                k.cp(o[:], pb[:], [pb], [o], eng="scalar" if tt % 2 else "vector")
                k.sto(SC["PH"][tt * 128:(tt + 1) * 128, (nb - 2) * 512:(nb - 1) * 512], o[:], o, [RS["PH"]])

        load_w(0)
        load_w(1)
        for tt in range(ntile):
            xt = xts[tt % 2]
            xb = xbs[tt % 2]
            k.ld(xt[:], xin[tt * 128:(tt + 1) * 128, :], xt)
            norm_rows(k, ntm, xt, 128, srow, shrow, xb)
            transpose_to(k, xb, NK, xnT[tt], 0)
            mm_tile(0, tt)
            mm_tile(1, tt)
        if nbs > 2:
            load_w(2)
        for nb in range(2, nbs):
            if nb + 1 < nbs:
                load_w(nb + 1)
            for tt in range(ntile):
                mm_tile(nb, tt)


def stage_proj_all(k, env):
    I, SC, RS = env["I"], env["SC"], env["RS"]
    ntile = T // 128
    with ExitStack() as st:
        srow = k.sb(st, "srow", [128, D])
        shrow = k.sb(st, "shrow", [128, D])
        g1r = k.sb(st, "g1r", [128, D])
        k.ld(g1r[:], I["rows1"][0:1, :].broadcast_to([128, D]), g1r)

        def set_rows(mrow):
            k.ld(srow[:], SC["MOD"][mrow:mrow + 1, D:2 * D].broadcast_to([128, D]), srow, r=[RS["MOD"]])
            k.ld(shrow[:], SC["MOD"][mrow:mrow + 1, 0:D].broadcast_to([128, D]), shrow, r=[RS["MOD"]])
            k.stt(srow[:], srow[:], 1.0, g1r[:], ALU.add, ALU.mult, [srow, g1r], [srow])
        xnT = [k.sb(st, "xnT%d" % i, [128, NK, 128], BF16) for i in range(ntile)]
        stg = [k.sb(st, "wstg%d" % i, [128, 4, 512]) for i in range(2)]
        wbs = [k.sb(st, "wb%d" % i, [128, NK, 512], BF16) for i in range(2)]
        ob = [k.sb(st, "ob%d" % i, [128, 512], BF16) for i in range(4)]
        of = [k.sb(st, "of%d" % i, [128, 512]) for i in range(2)]
        xts = [k.sb(st, "xt%d" % i, [128, D]) for i in range(2)]
        xbs = [k.sb(st, "xb%d" % i, [128, D], BF16) for i in range(2)]
        ntm = norm_tmps(k, st, "n1")
        qq = [0]
        cnt = [0]

        def load_w(nb):
            wb = wbs[nb % 2]
            for kq in range(4):
                sg = stg[qq[0] % 2]
                qq[0] += 1
                k.ld(sg[:], I["w_in"][kq * 512:(kq + 1) * 512, nb * 512:(nb + 1) * 512].rearrange("(kc p) n -> p kc n", p=128), sg)
                k.cp(wb[:, kq * 4:(kq + 1) * 4, :], sg[:], [sg], [wb], eng="scalar" if kq % 2 else "vector")

        def mm_tile(nb, tt, is_ctx=False):
            wb = wbs[nb % 2]
            pb = k.bank()
            for kc in range(NK):
                k.mm(pb[:], xnT[tt][:, kc, :], wb[:, kc, :], kc == 0, kc == NK - 1, [xnT[tt], wb], [pb])
            cnt[0] += 1
            eng = "scalar" if cnt[0] % 2 else "vector"
            if nb < 2:
                o = ob[cnt[0] % 4]
                k.cp(o[:], pb[:], [pb], [o], eng=eng)
                dst = SC["UC"] if is_ctx else SC["U"]
                k.sto(dst[tt * 128:(tt + 1) * 128, nb * 512:(nb + 1) * 512], o[:], o, [RS["UC" if is_ctx else "U"]])
            else:
                o = of[cnt[0] % 2]
                k.cp(o[:], pb[:], [pb], [o], eng=eng)
                k.sto(SC["PH"][tt * 128:(tt + 1) * 128, (nb - 2) * 512:(nb - 1) * 512], o[:], o, [RS["PH"]])

        def prep(xin, tt, it):
            xt = xts[it % 2]
            xb = xbs[it % 2]
            k.ld(xt[:], xin[tt * 128:(tt + 1) * 128, :], xt)
            norm_rows(k, ntm, xt, 128, srow, shrow, xb)
            transpose_to(k, xb, NK, xnT[tt], 0)

        load_w(0)
        load_w(1)
        set_rows(1)
        for c in range(CT // 128):
            prep(I["ctx"], c, c)
            mm_tile(0, c, True)
            mm_tile(1, c, True)
        set_rows(0)
        for tt in range(ntile):
            prep(I["x"], tt, tt)
            mm_tile(0, tt)
            mm_tile(1, tt)
        load_w(2)
        for nb in range(2, 8):
            if nb + 1 < 8:
                load_w(nb + 1)
            for tt in range(ntile):
                mm_tile(nb, tt)


def filt_pers(k, st):
    return (k.sb(st, "rn", [128, 2048]), k.sb(st, "brow", [128, 2048]), k.sb(st, "altb", [128, 1], BF16))


def gen_filt1(k, env, st, pers):
    I, SC, RS = env["I"], env["SC"], env["RS"]
    if True:
        zt = k.sb(st, "zt", [33, 2048])
        k.ld(zt[:], I["ZT"], zt)
        w1 = k.sb(st, "w1", [33, 64])
        k.ld(w1[:], I["hy_w1"], w1)
        w2 = k.sb(st, "w2", [64, 64])
        k.ld(w2[:], I["hy_w2"], w2)
        w3 = k.sb(st, "w3", [64, 4096])
        k.ld(w3[:], I["hy_w3"], w3)
        hc = k.sb(st, "hc", [64, 4])
        k.ld(hc[:], I["hy_cols"], hc)
        fs = k.sb(st, "fs", [64, 2])
        k.ts(fs[:], hc[:, 2:4], 1.0 / (2 * math.pi), None, ALU.mult, None, [hc], [fs])
        hh = [k.sb(st, "hmlp%d" % i, [64, 2048]) for i in range(2)]
        rr = k.sb(st, "hrr", [64, 2048])
        for layer in range(2):
            src = zt if layer == 0 else hh[0]
            kk = 33 if layer == 0 else 64
            w = w1 if layer == 0 else w2
            dst = hh[layer]
            for nb in range(4):
                pb = k.bank()
                k.mm(pb[0:64, :], w[0:kk, :], src[0:kk, nb * 512:(nb + 1) * 512], True, True, [w, src], [pb])
                k.ts(dst[:, nb * 512:(nb + 1) * 512], pb[0:64, :], hc[:, layer:layer + 1], fs[:, layer:layer + 1], ALU.add, ALU.mult,
                     [pb, hc, fs], [dst])
            k.frac_turns(dst[:], rr[:], dst, rr)
            k.act(dst[:], dst[:], AF.Sin, [dst], [dst], scale=TWO_PI_LO)
        h2 = k.sb(st, "h2b", [64, 2048], BF16)
        k.cp(h2[:], hh[1][:], [hh[1]], [h2])
        w3b = k.sb(st, "w3b", [64, 4096], BF16)
        k.cp(w3b[:], w3[:], [w3], [w3b], eng="scalar")
        w3 = w3b
        dec = k.sb(st, "dec", [128, 4096])
        k.ld(dec[:], I["hy_decay"].broadcast_to([128, 4096]), dec)
        k.act(dec[:], dec[:], AF.Abs, [dec], [dec])
        posf = k.sb(st, "posf", [128, 16])
        k.ld(posf[:], I["POSF"], posf)
        ones = k.sb(st, "ones", [128, 128], BF16)
        k.memset(ones[:], 1.0, [ones])
        l1 = k.sb(st, "l1", [128, 4096])
        wt = [k.sb(st, "fwin%d" % i, [128, 512]) for i in range(2)]
        ht = [k.sb(st, "fh%d" % i, [128, 512]) for i in range(3)]
        ab = [k.sb(st, "fab%d" % i, [128, 512], BF16) for i in range(2)]
        items = [(nb, pc) for nb in range(8) for pc in range(16)]
        pbs_ = {}

        def h3mm(i):
            nb, pc = items[i]
            pb = k.bank()
            k.mm(pb[:], h2[:, pc * 128:(pc + 1) * 128], w3[:, nb * 512:(nb + 1) * 512], True, True, [h2, w3], [pb])
            pbs_[i] = pb
        yield
        h3mm(0)
        for i, (nb, pc) in enumerate(items):
            if i + 1 < len(items):
                h3mm(i + 1)
            s_is_bwd = (nb // 2) % 2 == 1
            pl = k.banks[6 + nb % 2]
            pb = pbs_.pop(i)
            wti = wt[i % 2]
            hti = ht[i % 3]
            abi = ab[i % 2]
            k.act(wti[:], dec[:, nb * 512:(nb + 1) * 512], AF.Exp, [dec, posf], [wti], scale=posf[:, pc:pc + 1])
            k.tt(hti[:], pb[:], wti[:], ALU.mult, [pb, wti], [hti])
            if s_is_bwd and pc == 0:
                k.ts(hti[:], hti[:], k.cols[:, 7:8], None, ALU.mult, None, [hti, k.cols], [hti])
            k.sto(SC["HT"][pc * 128:(pc + 1) * 128, nb * 512:(nb + 1) * 512], hti[:], hti, [RS["HT"]])
            k.stt(abi[:], hti[:], -1.0, hti[:], ALU.mult, ALU.max, [hti], [abi])
            k.mm(pl[:], ones[:], abi[:], pc == 0, pc == 15, [ones, abi], [pl])
            if pc == 15:
                k.cp(l1[:, nb * 512:(nb + 1) * 512], pl[:], [pl], [l1], eng="scalar")
            yield
        rn, brow, altb = pers
        for o in range(2):
            k.tt(rn[:, o * 1024:(o + 1) * 1024], l1[:, o * 2048:o * 2048 + 1024], l1[:, o * 2048 + 1024:(o + 1) * 2048], ALU.add, [l1], [rn])
        k.ts(rn[:], rn[:], EPS, None, ALU.add, None, [rn], [rn])
        k.recip(rn[:], rn[:], [rn], [rn])
        k.ld(brow[:], I["hy_bias"].broadcast_to([128, 2048]), brow)
        k.cp(altb[:], k.cols[:, 8:9], [k.cols], [altb])


def gen_filt2(k, env, st, pers):
    I, SC, RS = env["I"], env["SC"], env["RS"]
    if True:
        e_ts = [k.sb(st, "e_t%d" % i, [128, 16, 512], BF16) for i in range(2)]
        d_ts = [k.sb(st, "d_t%d" % i, [128, 16, 512], BF16) for i in range(2)]
        ff = [k.sb(st, "ff%d" % i, [128, 512]) for i in range(2)]
        fb = [k.sb(st, "fb%d" % i, [128, 512]) for i in range(2)]
        fw = [k.sb(st, "fwt%d" % i, [128, 16, 128], BF16) for i in range(4)]
        ot = [k.sb(st, "fo%d" % i, [128, 512]) for i in range(2)]
        nyqs = [k.sb(st, "nyq%d" % i, [1, 512]) for i in range(2)]
        rn, brow, altb = pers
        yield

        def cols_of(q):
            o, half = q // 2, q % 2
            return o * 2048 + half * 512, o * 2048 + 1024 + half * 512, o * 1024 + half * 512

        def build(q):
            cf, cb, co = cols_of(q)
            e_t, d_t, nyq = e_ts[q % 2], d_ts[q % 2], nyqs[q % 2]
            pn = k.banks[7]
            for lc in range(16):
                a = ff[lc % 2]
                b = fb[lc % 2]
                k.ld(a[:], SC["HT"][lc * 128:(lc + 1) * 128, cf:cf + 512], a, r=[RS["HT"]])
                k.ld(b[:], SC["HT"][lc * 128:(lc + 1) * 128, cb:cb + 512], b, r=[RS["HT"]])
                k.tt(e_t[:, lc, :], a[:], b[:], ALU.add, [a, b], [e_t])
                k.tt(d_t[:, lc, :], a[:], b[:], ALU.subtract, [a, b], [d_t])
                k.mm(pn[0:1, :], altb[:, 0:1], e_t[:, lc, :], lc == 0, lc == 15, [altb, e_t], [pn])
            k.tt(nyq[:], pn[0:1, :], rn[0:1, co:co + 512], ALU.mult, [pn, rn], [nyq])
            k.tt(nyq[:], nyq[:], brow[0:1, co:co + 512], ALU.add, [nyq, brow], [nyq])

        build(0)
        mi = 0
        for q in range(4):
            cf, cb, co = cols_of(q)
            e_t, d_t, nyq = e_ts[q % 2], d_ts[q % 2], nyqs[q % 2]
            if q + 1 < 4:
                build(q + 1)
            for m in range(32):
                fwt = fw[mi % 4]
                mi += 1
                k.ld(fwt[:], I["FWD"][m], fwt)
                pb = k.bank()
                src = e_t if m < 16 else d_t
                for lc in range(16):
                    k.mm(pb[:], fwt[:, lc, :], src[:, lc, :], lc == 0, lc == 15, [fwt, src], [pb])
                o_t = ot[m % 2]
                k.tt(o_t[:], pb[:], rn[:, co:co + 512], ALU.mult, [pb, rn], [o_t])
                if m < 16:
                    k.tt(o_t[:], o_t[:], brow[:, co:co + 512], ALU.add, [o_t, brow], [o_t])
                if m == 16:
                    k.cp(o_t[0:1, :], nyq[:], [nyq, o_t], [o_t])
                k.sto(SC["HF"][m * 128:(m + 1) * 128, co:co + 512], o_t[:], o_t, [RS["HF"]])
                yield


def stage_filt(k, env):
    with ExitStack() as stp:
        pers = filt_pers(k, stp)
        k.brange = (0, 6)
        with ExitStack() as sa:
            for _ in gen_filt1(k, env, sa, pers):
                pass
        k.S.barrier()
        k.brange = (0, 7)
        with ExitStack() as sb_:
            for _ in gen_filt2(k, env, sb_, pers):
                pass
        k.brange = (0, 8)


def bc_last(ap, n):
    shp = list(ap.shape)
    return ap.unsqueeze(len(shp)).broadcast_to(shp + [n])


def gen_s5prep(k, env):
    I, SC, RS = env["I"], env["SC"], env["RS"]
    cols = k.cols
    sA, sB, sC, sD = (cols[:, i:i + 1] for i in range(4))
    with ExitStack() as st:
        def ldt(name, shape, stx=st):
            t = k.sb(stx, name, shape)
            k.ld(t[:], I[name], t)
            return t
        xc1 = ldt("XC1", [128, 128, 16])
        xc2 = ldt("XC2", [128, 128, 16])
        dcol = ldt("DCOL", [128, 64])
        maskf = ldt("MASKF", [128, 128])
        maskb = ldt("MASKB", [128, 128])
        lr = k.sb(st, "lr", [128, 128])
        li = k.sb(st, "li", [128, 128])
        xbb1 = k.sb(st, "xbb1", [128, 128, 16])
        xbb2 = k.sb(st, "xbb2", [128, 128, 16])
        with ExitStack() as sta:
            lamr = ldt("LAMR2", [128, 128], sta)
            lami = ldt("LAMI2", [128, 128], sta)
            ldtt = ldt("LDT", [128, 128], sta)
            xb1 = ldt("XB1", [128, 128, 16], sta)
            xb2 = ldt("XB2", [128, 128, 16], sta)
            sm = lambda n: k.sb(sta, n, [128, 128])
            dt = sm("dt"); ea = sm("ea"); sn = sm("sn"); cs = sm("cs")
            ar = sm("ar"); ai = sm("ai"); den = sm("den"); t1 = sm("t1"); t2 = sm("t2"); cr = sm("cr"); ci = sm("ci")
            cib = sm("cib"); cic = sm("cic")
            k.act(dt[:], ldtt[:], AF.Exp, [ldtt], [dt])
            k.tt(lr[:], lamr[:], dt[:], ALU.mult, [lamr, dt], [lr])
            k.tt(li[:], lami[:], dt[:], ALU.mult, [lami, dt], [li])
            k.act(ea[:], lr[:], AF.Exp, [lr], [ea])
            k.sincos(sta, li[:], [128, 128], sn[:], cs[:], [li], [sn], [cs], "sc0")
            yield
            k.tt(ar[:], ea[:], cs[:], ALU.mult, [ea, cs], [ar])
            k.tt(ai[:], ea[:], sn[:], ALU.mult, [ea, sn], [ai])
            k.ts(ar[:], ar[:], -1.0, None, ALU.add, None, [ar], [ar])
            k.tt(den[:], lamr[:], lamr[:], ALU.mult, [lamr], [den])
            k.tt(t1[:], lami[:], lami[:], ALU.mult, [lami], [t1])
            k.tt(den[:], den[:], t1[:], ALU.add, [den, t1], [den])
            k.recip(den[:], den[:], [den], [den])
            k.tt(t1[:], ar[:], lamr[:], ALU.mult, [ar, lamr], [t1])
            k.tt(t2[:], ai[:], lami[:], ALU.mult, [ai, lami], [t2])
            k.tt(t1[:], t1[:], t2[:], ALU.add, [t1, t2], [t1])
            k.tt(cr[:], t1[:], den[:], ALU.mult, [t1, den], [cr])
            k.tt(t1[:], ai[:], lamr[:], ALU.mult, [ai, lamr], [t1])
            k.tt(t2[:], ar[:], lami[:], ALU.mult, [ar, lami], [t2])
            k.tt(t1[:], t1[:], t2[:], ALU.subtract, [t1, t2], [t1])
            k.tt(ci[:], t1[:], den[:], ALU.mult, [t1, den], [ci])
            k.ts(cib[:], ci[:], sB, None, ALU.mult, None, [ci, cols], [cib])
            k.ts(cic[:], ci[:], sC, None, ALU.mult, None, [ci, cols], [cic])
            yield
            s5p = k.sb(sta, "s5p", [128, 2, 128])
            k.ts(s5p[:, 0, :], lr[:], 8.0, None, ALU.mult, None, [lr], [s5p])
            k.ts(s5p[:, 1, :], li[:], 8.0 / (2 * math.pi), None, ALU.mult, None, [li], [s5p])
            k.sto(SC["S5P"], s5p[:], s5p, [RS["S5P"]])
            tmpa = k.sb(sta, "tmpa", [128, 128, 16])
            k.tt(xbb1[:], bc_last(cr[:], 16), xb1[:], ALU.mult, [cr, xb1], [xbb1])
            k.tt(tmpa[:], bc_last(cib[:], 16), xb2[:], ALU.mult, [cib, xb2], [tmpa])
            k.tt(xbb1[:], xbb1[:], tmpa[:], ALU.add, [xbb1, tmpa], [xbb1])
            k.tt(xbb2[:], bc_last(cr[:], 16), xb2[:], ALU.mult, [cr, xb2], [xbb2])
            k.tt(tmpa[:], bc_last(cic[:], 16), xb1[:], ALU.mult, [cic, xb1], [tmpa])
            k.tt(xbb2[:], xbb2[:], tmpa[:], ALU.add, [xbb2, tmpa], [xbb2])
            yield
        yield "barrier"
        pw = {}
        for fam in range(3):
            pw[(fam, "r")] = k.sb(st, "pr%d" % fam, [128, 128, 8])
            pw[(fam, "i")] = k.sb(st, "pi%d" % fam, [128, 128, 8])
        with ExitStack() as stb:
            expo = k.sb(stb, "expo", [128, 3, 128, 8])
            k.ld(expo[:], I["EXPO"], expo)
            marg = k.sb(stb, "marg", [128, 128, 8])
            ang = k.sb(stb, "ang", [128, 128, 8])
            ytm = k.sb(stb, "scf_y", [128, 128, 8])
            rtm = k.sb(stb, "scf_r", [128, 128, 8])
            for fam in range(3):
                pr, pi = pw[(fam, "r")], pw[(fam, "i")]
                k.tt(marg[:], expo[:, fam, :, :], bc_last(lr[:], 8), ALU.mult, [expo, lr], [marg])
                k.act(marg[:], marg[:], AF.Exp, [marg], [marg])
                k.tt(ang[:], expo[:, fam, :, :], bc_last(li[:], 8), ALU.mult, [expo, li], [ang])
                k.sincos2(ang[:], ytm, rtm, pi[:], pr[:], [ang], [pi], [pr])
                k.tt(pr[:], pr[:], marg[:], ALU.mult, [pr, marg], [pr])
                k.tt(pi[:], pi[:], marg[:], ALU.mult, [pi, marg], [pi])
                yield
        yield "barrier"

        def signed(src, col, name):
            t = k.sb(st, name, [128, 128, 8])
            k.ts(t[:], src[:], col, None, ALU.mult, None, [src, cols], [t])
            return t
        p0r, p0i = pw[(0, "r")], pw[(0, "i")]
        p0iB = signed(p0i, sB, "p0iB")
        p0rC = signed(p0r, sC, "p0rC")
        p1rC = signed(pw[(1, "r")], sC, "p1rC")
        p1iD = signed(pw[(1, "i")], sD, "p1iD")
        p2rC = signed(pw[(2, "r")], sC, "p2rC")
        p2iD = signed(pw[(2, "i")], sD, "p2iD")
        p2rD = signed(pw[(2, "r")], sD, "p2rD")
        p2iB = signed(pw[(2, "i")], sB, "p2iB")
        fams = {
            "T1": (p0r, xbb1, p0iB, xbb2),
            "T1sw": (p0rC, xbb2, p0i, xbb1),
            "R": (p1rC, xc1, p1iD, xc2),
            "G1": (p2rC, xc1, p2iD, xc2),
            "G2": (p2rD, xc2, p2iB, xc1),
        }
        stg = [k.sb(st, "s5stg%d" % i, [128, 8, 9, 128], BF16) for i in range(1)]
        mtmp = [k.sb(st, "mtmp%d" % i, [128, 8, 128]) for i in range(2)]
        fa = k.sb(st, "fam_a", [128, 8, 8, 16])
        fbt = k.sb(st, "fam_b", [128, 8, 8, 16])
        fa2 = k.sb(st, "fam_a2", [128, 8, 8, 16])
        fbt2 = k.sb(st, "fam_b2", [128, 8, 8, 16])
        t1b = [k.sb(st, "T1b%d" % i, [128, 8, 128], BF16) for i in range(2)]
        t1s = [k.sb(st, "T1s%d" % i, [128, 8, 128], BF16) for i in range(2)]
        rmb = [k.sb(st, "Rmb%d" % i, [128, 8, 128], BF16) for i in range(2)]
        mt2 = k.sb(st, "mt2", [128, 4, 128])
        v4 = lambda ap: ap.rearrange("p g (i c) -> p g i c", c=16)

        def fam(name, gd0, out_ap, outT):
            pa, xa, pb_, xb_ = fams[name]
            pa_bc = bc_last(pa[:, gd0:gd0 + 8, :], 16)
            pb_bc = bc_last(pb_[:, gd0:gd0 + 8, :], 16)
            xa_bc = xa[:, gd0:gd0 + 8, :].unsqueeze(2).broadcast_to([128, 8, 8, 16])
            xb_bc = xb_[:, gd0:gd0 + 8, :].unsqueeze(2).broadcast_to([128, 8, 8, 16])
            k.tt(fa[:], pa_bc, xa_bc, ALU.mult, [pa, xa], [fa])
            k.tt(fbt[:], pb_bc, xb_bc, ALU.mult, [pb_, xb_], [fbt])
            k.tt(out_ap, fa[:], fbt[:], ALU.add, [fa, fbt], [outT])

        for gb in range(8):
            g0 = gb * 8
            sg = stg[0]
            mt = mtmp[0]
            for d in range(2):
                gd0 = d * 64 + g0
                fam("T1", gd0, v4(t1b[d][:]), t1b[d])
                yield
                fam("T1sw", gd0, v4(t1s[d][:]), t1s[d])
                fam("R", gd0, v4(rmb[d][:]), rmb[d])
                yield
                fam("G1", gd0, v4(sg[:, :, 5 + 2 * d, :]), sg)
                fam("G2", gd0, v4(sg[:, :, 6 + 2 * d, :]), sg)
                yield
            for d in range(2):
                T1b, T1s, Rmb = t1b[d], t1s[d], rmb[d]
                for h in range(2):
                    pbw = k.bank()
                    pbwb = pbw[:].bitcast(BF16)
                    pbm = k.bank()
                    for gi in range(4):
                        g = h * 4 + gi
                        k.tr(pbwb[:, (2 * gi) * 128:(2 * gi + 1) * 128], T1b[:, g, :], k.identb[:], [T1b, k.identb], [pbw])
                        k.tr(pbwb[:, (2 * gi + 1) * 128:(2 * gi + 2) * 128], T1s[:, g, :], k.identb[:], [T1s, k.identb], [pbw])
                        k.mm(pbm[:, gi * 128:(gi + 1) * 128], T1b[:, g, :], Rmb[:, g, :], True, True, [T1b, Rmb], [pbm])
                    k.cp(sg[:, h * 4:(h + 1) * 4, 1 + 2 * d:3 + 2 * d, :], pbwb.rearrange("p (g w c) -> p g w c", g=4, w=2), [pbw], [sg], eng="scalar")
                    msk = (maskf if d == 0 else maskb)
                    mbc = msk[:].unsqueeze(1).broadcast_to([128, 4, 128])
                    pm3 = pbm[:].rearrange("p (g c) -> p g c", g=4)
                    if d == 0:
                        k.tt(mt[:, h * 4:(h + 1) * 4, :], pm3, mbc, ALU.mult, [pbm, msk], [mt])
                    else:
                        k.tt(mt2[:], pm3, mbc, ALU.mult, [pbm, msk], [mt2])
                        k.tt(mt[:, h * 4:(h + 1) * 4, :], mt[:, h * 4:(h + 1) * 4, :], mt2[:], ALU.add, [mt, mt2], [mt])
                    yield
            for gi in range(8):
                k.stt(sg[:, gi, 0, :], k.ident[:], dcol[:, g0 + gi:g0 + gi + 1], mt[:, gi, :], ALU.mult, ALU.add, [k.ident, dcol, mt], [sg])
            k.sto(SC["S5WT"][g0:g0 + 8].rearrange("g p n -> p g n"), sg[:].rearrange("p g w c -> p g (w c)"), sg, [RS["S5WT"]])


def stage_s5prep(k, env, side=None):
    def main():
        for y in gen_s5prep(k, env):
            if y == "barrier":
                k.S.barrier()
            yield
    if side is None:
        for _ in main():
            pass
    else:
        next(side)
        run_with_side(main(), side, every=2)


def stage_s5(k, env):
    I, SC, RS = env["I"], env["SC"], env["RS"]
    idb = k.identb
    with ExitStack() as st:
        s5p = k.sb(st, "s5p", [128, 2, 128])
        k.ld(s5p[:], SC["S5P"], s5p, r=[RS["S5P"]])
        r8 = k.sb(st, "r8", [128, 128])
        k.act(r8[:], s5p[:, 0, :], AF.Exp, [s5p], [r8])
        x0 = k.sb(st, "x0", [128, 128])
        wts = [k.sb(st, "s5w%d" % i, [128, 9, 128], BF16) for i in range(4)]
        with ExitStack() as sc:
            tmc = k.sb(sc, "tmc", [32, 8, 1024], BF16)
            k.ld(tmc[:], SC["UC"].rearrange("(jj s) n -> jj s n", s=8), tmc, r=[RS["UC"]])
            ucs = k.sb(sc, "ucs", [128, 64, 32], BF16)
            tmcg = k.sb(sc, "tmcg", [32, 64, 128], BF16)
            k.cp(tmcg[:].rearrange("p g (s c) -> p s g c", c=16), tmc[:].rearrange("p s (g c) -> p s g c", c=16), [tmc], [tmcg])
            for gq in range(4):
                pb = k.bank()
                pbb = pb[:].bitcast(BF16)
                for gi in range(16):
                    g = gq * 16 + gi
                    k.tr(pbb[:, gi * 32:(gi + 1) * 32], tmcg[0:32, g, :], idb[0:32, 0:32], [tmcg, idb], [pb])
                k.cp(ucs[:, gq * 16:(gq + 1) * 16, :], pbb[:, 0:512].rearrange("p (g j) -> p g j", j=32), [pb], [ucs])
            kc = k.sb(sc, "kc", [128, 128, 32])
            k.ld(kc[:], I["KCTX"], kc)
            ppr = k.sb(sc, "ppr", [128, 128, 32])
            ppi = k.sb(sc, "ppi", [128, 128, 32])
            marg = k.sb(sc, "cmarg", [128, 128, 32])
            ytm_ = k.sb(sc, "cy", [128, 128, 32])
            rtm_ = k.sb(sc, "cr", [128, 128, 32])
            k.tt(marg[:], kc[:], bc_last(s5p[:, 0, :], 32), ALU.mult, [kc, s5p], [marg])
            k.act(marg[:], marg[:], AF.Exp, [marg], [marg])
            k.tt(ytm_[:], kc[:], bc_last(s5p[:, 1, :], 32), ALU.mult, [kc, s5p], [ytm_])
            k._sincos_tail(ytm_, rtm_, ppi[:], ppr[:], [ppi], [ppr])
            k.tt(ppr[:], ppr[:], marg[:], ALU.mult, [ppr, marg], [ppr])
            k.tt(ppi[:], ppi[:], marg[:], ALU.mult, [ppi, marg], [ppi])
            ct1 = [k.sb(sc, "ct1_%d" % i, [128, 2, 4, 32]) for i in range(2)]
            ct2 = [k.sb(sc, "ct2_%d" % i, [128, 2, 4, 32]) for i in range(2)]
            wc = [k.sb(sc, "s5wc%d" % i, [128, 9, 128], BF16) for i in range(8)]
            ppr4 = ppr[:].rearrange("p (d g) j -> p d g j", d=2)
            ppi4 = ppi[:].rearrange("p (d g) j -> p d g j", d=2)
            x04 = x0[:].rearrange("p (d g) -> p d g", d=2)
            for g4 in range(16):
                pb = k.bank()
                for gi in range(4):
                    g = g4 * 4 + gi
                    w = wc[g % 8]
                    k.ld(w[:], SC["S5WT"][g].rearrange("p (w c) -> p w c", c=128), w, r=[RS["S5WT"]])
                    for q in range(4):
                        k.mm(pb[:, gi * 128 + q * 32:gi * 128 + (q + 1) * 32], w[:, 1 + q, :], ucs[:, g, :], True, True, [w, ucs], [pb])
                v = pb[:].rearrange("p (gi d s j) -> p d gi s j", gi=4, d=2, s=2)
                a1, a2 = ct1[g4 % 2], ct2[g4 % 2]
                gs = slice(g4 * 4, g4 * 4 + 4)
                k.tt(a1[:], ppr4[:, :, gs, :], v[:, :, :, 0, :], ALU.mult, [ppr, pb], [a1])
                k.tt(a2[:], ppi4[:, :, gs, :], v[:, :, :, 1, :], ALU.mult, [ppi, pb], [a2])
                k.tt(a1[:], a1[:], a2[:], ALU.subtract, [a1, a2], [a1])
                k.rsum(x04[:, :, gs], a1[:], [a1], [x0])
        if "X0" in env.get("dbg", ()):
            k.sto(SC["X0"], x0[:], x0, [RS["X0"]])
        k.S.barrier()
        with ExitStack() as sm:
            tm = [k.sb(sm, "tm%d" % i, [128, 8, 1024], BF16) for i in range(2)]
            yt = [k.sb(sm, "ytm%d" % i, [128, 8, 1024], BF16) for i in range(2)]
            for jb in range(2):
                k.ld(tm[jb][:], SC["U"][jb * 1024:(jb + 1) * 1024, :].rearrange("(jj s) n -> jj s n", s=8), tm[jb], r=[RS["U"]])
            tmg = [k.sb(sm, "tmg%d" % i, [128, 64, 128], BF16) for i in range(2)]
            for jb in range(2):
                k.cp(tmg[jb][:].rearrange("p g (s c) -> p s g c", c=16), tm[jb][:].rearrange("p s (g c) -> p s g c", c=16), [tm[jb]], [tmg[jb]],
                     eng="scalar" if jb else "vector")
            iota = k.sb(sm, "iota", [128, 256])
            k.ld(iota[:], I["IOTA"], iota)
            tabc = [k.sb(sm, "tabc%d" % i, [128, 2, 4, 256]) for i in range(2)]
            tabs = [k.sb(sm, "tabs%d" % i, [128, 2, 4, 256]) for i in range(2)]
            ty = k.sb(sm, "ty", [128, 2, 4, 256])
            tr_ = k.sb(sm, "trr", [128, 2, 4, 256])
            ust = [k.sb(sm, "ust%d" % i, [128, 256], BF16) for i in range(2)]
            m1 = [k.sb(sm, "m1_%d" % i, [128, 256]) for i in range(2)]
            m2 = [k.sb(sm, "m2_%d" % i, [128, 256]) for i in range(2)]
            zz = [k.sb(sm, "zz%d" % i, [128, 256]) for i in range(2)]
            zc = [[k.sb(sm, "zc%d_%d" % (d, i), [128, 257], BF16) for i in range(2)] for d in range(2)]
            zs = [[k.sb(sm, "zs%d_%d" % (d, i), [128, 257], BF16) for i in range(2)] for d in range(2)]
            for d in range(2):
                for i in range(2):
                    k.memset(zs[d][i][:], 0.0, [zs[d][i]])
            k.brange = (0, 6)
            zz4 = [[k.sb(sm, "zz%d_%d" % (d, i), [128, 256]) for i in range(2)] for d in range(2)]
            yb = [k.banks[6], k.banks[7]]
            state = {}

            def tables(g4):
                tc_, tsn_ = tabc[g4 % 2], tabs[g4 % 2]
                phi4 = s5p[:, 1, :].rearrange("p (d g) -> p d g", d=2)[:, :, g4 * 4:(g4 + 1) * 4]
                k.tt(ty[:], iota[:].unsqueeze(1).unsqueeze(1).broadcast_to([128, 2, 4, 256]), bc_last(phi4, 256), ALU.mult, [iota, s5p], [ty])
                k._sincos_tail(ty, tr_, tsn_[:], tc_[:], [tsn_], [tc_])

            def front(g):
                w = wts[g % 4]
                k.ld(w[:], SC["S5WT"][g].rearrange("p (w c) -> p w c", c=128), w, r=[RS["S5WT"]])
                pbu = k.bank()
                pbub = pbu[:].bitcast(BF16)
                for jb in range(2):
                    k.tr(pbub[:, jb * 128:(jb + 1) * 128], tmg[jb][:, g, :], idb[:], [tmg[jb], idb], [pbu])
                us = ust[g % 2]
                k.cp(us[:], pbub[:, 0:256], [pbu], [us], eng="scalar")
                pbs2 = []
                for d in range(2):
                    pbs = k.bank()
                    k.mm(pbs[:, 0:256], w[:, 1 + 2 * d, :], us[:], True, True, [w, us], [pbs])
                    k.mm(pbs[:, 256:512], w[:, 2 + 2 * d, :], us[:], True, True, [w, us], [pbs])
                    pbs2.append(pbs)
                state[g] = (w, us, pbs2)

            def mid(g):
                w, us, pbs2 = state[g]
                g4, gi = g // 4, g % 4
                tc_, tsn_ = tabc[g4 % 2], tabs[g4 % 2]
                tcs = [tc_[:, 0, gi, :], tc_[:, 1, gi, ::-1]]
                tsns = [tsn_[:, 0, gi, :], tsn_[:, 1, gi, ::-1]]
                zcd = [zc[0][g % 2], zc[1][g % 2]]
                zsd = [zs[0][g % 2], zs[1][g % 2]]
                abz = [(m1[d], m2[d], zz4[d][g % 2]) for d in range(2)]
                gds = [g, 64 + g]
                for d in range(2):
                    k.tt(abz[d][0][:], tcs[d], pbs2[d][:, 0:256], ALU.mult, [tc_, pbs2[d]], [abz[d][0]])
                for d in range(2):
                    k.tt(abz[d][1][:], tsns[d], pbs2[d][:, 256:512], ALU.mult, [tsn_, pbs2[d]], [abz[d][1]])
                for d in range(2):
                    k.tt(abz[d][0][:], abz[d][0][:], abz[d][1][:], ALU.add, [abz[d][0], abz[d][1]], [abz[d][0]])
                for d in range(2):
                    a, z, gd = abz[d][0], abz[d][2], gds[d]
                    r8b = r8[:, gd:gd + 1].broadcast_to([128, 256])
                    if d == 0:
                        k.scan(z[:], r8b, a[:], x0[:, gd:gd + 1], [r8, a, x0], [z])
                    else:
                        k.scan(z[:, ::-1], r8b, a[:, ::-1], x0[:, gd:gd + 1], [r8, a, x0], [z])
                for d in range(2):
                    z = abz[d][2]
                    dst = zcd[d][:, 1:257] if d == 0 else zcd[d][:, 0:256]
                    k.tt(dst, tcs[d], z[:], ALU.mult, [tc_, z], [zcd[d]], eng="gpsimd" if d else "vector")
                for d in range(2):
                    z = abz[d][2]
                    dst = zsd[d][:, 1:257] if d == 0 else zsd[d][:, 0:256]
                    k.tt(dst, tsns[d], z[:], ALU.mult, [tsn_, z], [zsd[d]], eng="gpsimd" if d else "vector")
                k.cp(zcd[0][:, 0:1], x0[:, g:g + 1], [x0], [zcd[0]], eng="scalar")
                k.cp(zcd[1][:, 256:257], x0[:, 64 + g:65 + g], [x0], [zcd[1]], eng="scalar")
                state[g] = (w, us, zcd, zsd)

            def back(g):
                w, us, zcd, zsd = state.pop(g)
                g4, gi = g // 4, g % 4
                for jb in range(2):
                    o = yb[jb][:, gi * 128:(gi + 1) * 128]
                    c0 = jb * 128
                    k.mm(o, us[:, c0:c0 + 128], w[:, 0, :], True, False, [us, w], [yb[jb]])
                    k.mm(o, zcd[0][:, c0:c0 + 128], w[:, 5, :], False, False, [zcd[0], w], [yb[jb]])
                    k.mm(o, zsd[0][:, c0:c0 + 128], w[:, 6, :], False, False, [zsd[0], w], [yb[jb]])
                    k.mm(o, zcd[1][:, c0 + 1:c0 + 129], w[:, 7, :], False, False, [zcd[1], w], [yb[jb]])
                    k.mm(o, zsd[1][:, c0 + 1:c0 + 129], w[:, 8, :], False, True, [zsd[1], w], [yb[jb]])
                if gi == 3:
                    for jb in range(2):
                        src = yb[jb][:].rearrange("p (g t c) -> p t g c", g=4, t=8)
                        dst = yt[jb][:, :, g4 * 64:(g4 + 1) * 64].rearrange("p t (g c) -> p t g c", c=16)
                        k.act(dst, src, AF.Gelu, [yb[jb]], [yt[jb]])

            tables(0)
            front(0)
            for g in range(G):
                if g + 1 < G:
                    front(g + 1)
                if g % 4 == 0 and g // 4 + 1 < 16:
                    tables(g // 4 + 1)
                mid(g)
                back(g)
            k.brange = (0, 8)
            for jb in range(2):
                k.sto(SC["YS5"][jb * 1024:(jb + 1) * 1024, :].rearrange("(jj s) n -> jj s n", s=8), yt[jb][:], yt[jb], [RS["YS5"]])


def rms_rows(k, tmps, yt, width, grow, out_ap, outT):
    junk, ssq, _ = tmps
    k.act(junk[:, 0:width], yt[:, 0:width], AF.Square, [yt], [junk, ssq], accum=ssq[:])
    k.ts(ssq[:], ssq[:], 1.0 / width, EPS, ALU.mult, ALU.add, [ssq], [ssq])
    k.act(ssq[:], ssq[:], AF.Sqrt, [ssq], [ssq])
    k.recip(ssq[:], ssq[:], [ssq], [ssq])
    k.stt(out_ap, yt[:, 0:width], ssq[:, 0:1], grow, ALU.mult, ALU.mult, [yt, ssq], [outT])


def load_cast_weight(k, wdram, nkc, wb, stg, n0=0, ncols=None, c0=0):
    ncols = wdram.shape[1] if ncols is None else ncols
    q = 0
    for nb in range(ncols // 512):
        for kq in range(0, nkc, 4):
            sg = stg[q % len(stg)]
            q += 1
            k.ld(sg[:], wdram[kq * 128:(kq + 4) * 128, c0 + nb * 512:c0 + (nb + 1) * 512].rearrange("(kc p) n -> p kc n", p=128), sg)
            k.cp(wb[:, kq:kq + 4, n0 + nb * 512:n0 + (nb + 1) * 512], sg[:], [sg], [wb], eng="scalar" if q % 2 else "vector")


def stage_glu(k, env):
    I, SC, RS = env["I"], env["SC"], env["RS"]
    with ExitStack() as st:
        wgl = k.sb(st, "wgl", [128, 8, 2048], BF16)
        stg = [k.sb(st, "gstg%d" % i, [128, 4, 512]) for i in range(2)]
        load_cast_weight(k, I["glu_w"], 8, wgl, stg)
        glub = k.sb(st, "glub", [128, 2048])
        k.ld(glub[:], I["glu_b"].broadcast_to([128, 2048]), glub)
        bgs = k.sb(st, "bgs", [128, 1024])
        k.ld(bgs[:], I["rows1"][3:4, 0:1024].broadcast_to([128, 1024]), bgs)
        ytl = [k.sb(st, "gy%d" % i, [128, 1024], BF16) for i in range(2)]
        yT = [k.sb(st, "gyT%d" % i, [128, 8, 128], BF16) for i in range(2)]
        av = [k.sb(st, "ga%d" % i, [128, 1024]) for i in range(2)]
        sv = [k.sb(st, "gs%d" % i, [128, 1024]) for i in range(2)]
        mo = [k.sb(st, "gm%d" % i, [128, 1024], BF16) for i in range(2)]
        tmps = norm_tmps(k, st, "gl")
        def prep(tt):
            y, yt_ = ytl[tt % 2], yT[tt % 2]
            k.ld(y[:], SC["YS5"][tt * 128:(tt + 1) * 128, :], y, r=[RS["YS5"]])
            transpose_to(k, y, 8, yt_, 0)
        prep(0)
        for tt in range(16):
            y, yt_, a, sg_, m = ytl[tt % 2], yT[tt % 2], av[tt % 2], sv[tt % 2], mo[tt % 2]
            if tt + 1 < 16:
                prep(tt + 1)
            for nb in range(4):
                pb = k.bank()
                for kc in range(8):
                    k.mm(pb[:], yt_[:, kc, :], wgl[:, kc, nb * 512:(nb + 1) * 512], kc == 0, kc == 7, [yt_, wgl], [pb])
                if nb < 2:
                    k.tt(a[:, nb * 512:(nb + 1) * 512], pb[:], glub[:, nb * 512:(nb + 1) * 512], ALU.add, [pb, glub], [a])
                else:
                    k.tt(sg_[:, (nb - 2) * 512:(nb - 1) * 512], pb[:], glub[:, nb * 512:(nb + 1) * 512], ALU.add, [pb, glub], [sg_])
            k.act(sg_[:], sg_[:], AF.Sigmoid, [sg_], [sg_])
            k.tt(a[:], a[:], sg_[:], ALU.mult, [a, sg_], [a])
            rms_rows(k, tmps, a, 1024, bgs[:], m[:], m)
            k.sto(SC["MIX"][tt * 128:(tt + 1) * 128, 0:1024], m[:], m, [RS["MIX"]])


def stage_hyena(k, env):
    I, SC, RS = env["I"], env["SC"], env["RS"]
    cols = k.cols
    m0, m2 = cols[:, 4:5], cols[:, 5:6]
    with ExitStack() as st:
        z = k.sb(st, "hz", [128, 16, 512], BF16)
        xg = [k.sb(st, "hx%d" % i, [128, 16, 512]) for i in range(2)]
        yf = k.sb(st, "hyf", [128, 32, 512], BF16)
        for cb in range(2):
            sf = ExitStack()
            sf.__enter__()
            fwr = [k.sb(sf, "hfwr%d" % i, [128, 16, 128], BF16) for i in range(2)]
            fwi = [k.sb(sf, "hfwi%d" % i, [128, 16, 128], BF16) for i in range(2)]
            hre = [k.sb(sf, "hre%d" % i, [128, 512]) for i in range(2)]
            him = [k.sb(sf, "him%d" % i, [128, 512]) for i in range(2)]
            ya = [k.sb(sf, "hya%d" % i, [128, 512]) for i in range(2)]
            yb_ = [k.sb(sf, "hyb%d" % i, [128, 512]) for i in range(2)]
            yo = [k.sb(sf, "hyo%d" % i, [128, 512]) for i in range(2)]
            sa_ = ExitStack()
            sa_.__enter__()
            NB_ = 2
            cw = k.sb(sa_, "hcw", [128, 3, 3, 512])
            cbv = k.sb(sa_, "hcb", [128, 3, 512])
            cu = [k.sb(sa_, "hcu%d" % i, [128, 512]) for i in range(NB_)]
            t1 = [k.sb(sa_, "ht1%d" % i, [128, 512]) for i in range(NB_)]
            t2 = [k.sb(sa_, "ht2%d" % i, [128, 512]) for i in range(NB_)]
            t3 = [k.sb(sa_, "ht3%d" % i, [128, 512]) for i in range(NB_)]
            for sg in range(3):
                c0 = sg * 1024 + cb * 512
                for tap in range(3):
                    k.ld(cw[:, tap, sg, :], I["conv_w"][tap:tap + 1, c0:c0 + 512].broadcast_to([128, 512]), cw)
                k.ld(cbv[:, sg, :], I["conv_b"][0:1, c0:c0 + 512].broadcast_to([128, 512]), cbv)
            shp = k.sb(sa_, "shp", [128, 128])
            shn = k.sb(sa_, "shn", [128, 128])
            k.ld(shp[:], I["SHP"], shp)
            k.ld(shn[:], I["SHN"], shn)
            itc = [0]

            def conv_tile(sg, tt):
                c0 = sg * 1024 + cb * 512
                c_, a_, b_, e_ = (x[itc[0] % NB_] for x in (cu, t1, t2, t3))
                itc[0] += 1
                r0 = tt * 128
                k.ld(c_[:], SC["PH"][r0:r0 + 128, c0:c0 + 512], c_, r=[RS["PH"]])
                pp = k.bank()
                pn = k.bank()
                k.mm(pp[:], shp[:], c_[:], True, True, [shp, c_], [pp])
                k.mm(pn[:], shn[:], c_[:], True, True, [shn, c_], [pn])
                k.tt(b_[:], c_[:], cw[:, 1, sg, :], ALU.mult, [c_, cw], [b_], eng="gpsimd")
                k.tt(a_[:], pp[:], cw[:, 0, sg, :], ALU.mult, [pp, cw], [a_])
                k.tt(e_[:], pn[:], cw[:, 2, sg, :], ALU.mult, [pn, cw], [e_])
                k.tt(a_[:], a_[:], b_[:], ALU.add, [a_, b_], [a_])
                k.tt(e_[:], e_[:], cbv[:, sg, :], ALU.add, [e_, cbv], [e_])
                if sg == 0:
                    k.tt(z[:, tt, :], a_[:], e_[:], ALU.add, [a_, e_], [z])
                else:
                    k.tt(xg[sg - 1][:, tt, :], a_[:], e_[:], ALU.add, [a_, e_], [xg[sg - 1]])

            def conv_rest():
                for sg in (1, 2):
                    for tt in range(16):
                        conv_tile(sg, tt)
                        yield

            def fwd(o):
                hc0 = o * 1024 + cb * 512
                for kk in range(16):
                    fr, fi = fwr[kk % 2], fwi[kk % 2]
                    k.ld(fr[:], I["FWD"][kk], fr)
                    k.ld(fi[:], I["FWD"][16 + kk], fi)
                    hr, hi = hre[kk % 2], him[kk % 2]
                    k.ld(hr[:], SC["HF"][kk * 128:(kk + 1) * 128, hc0:hc0 + 512], hr, r=[RS["HF"]])
                    k.ld(hi[:], SC["HF"][2048 + kk * 128:2048 + (kk + 1) * 128, hc0:hc0 + 512], hi, r=[RS["HF"]])
                    pre = k.bank()
                    pim = k.bank()
                    for tt in range(16):
                        k.mm(pre[:], fr[:, tt, :], z[:, tt, :], tt == 0, tt == 15, [fr, z], [pre])
                    for tt in range(16):
                        k.mm(pim[:], fi[:, tt, :], z[:, tt, :], tt == 0, tt == 15, [fi, z], [pim])
                    a_, b_ = ya[kk % 2], yb_[kk % 2]
                    k.tt(a_[:], pre[:], hr[:], ALU.mult, [pre, hr], [a_])
                    k.tt(b_[:], pim[:], hi[:], ALU.mult, [pim, hi], [b_])
                    k.tt(yf[:, kk, :], a_[:], b_[:], ALU.subtract, [a_, b_], [yf])
                    c_, d_ = (yo[0], yo[1]) if kk == 0 else (a_, b_)
                    k.tt(c_[:], pre[:], hi[:], ALU.mult, [pre, hi], [c_])
                    k.tt(d_[:], pim[:], hr[:], ALU.mult, [pim, hr], [d_])
                    k.tt(yf[:, 16 + kk, :], c_[:], d_[:], ALU.add, [c_, d_], [yf])
                    if kk == 0:
                        k.cp(yf[0:1, 0, :], a_[0:1, :], [a_, yf], [yf])
                        k.cp(yf[0:1, 16, :], b_[0:1, :], [b_, yf], [yf])
                    yield

            for tt in range(16):
                conv_tile(0, tt)
            cg = conv_rest()
            for _ in fwd(0):
                for _j in range(2):
                    next(cg, None)
            for _ in cg:
                pass
            sa_.__exit__(None, None, None)
            k.S.barrier()
            si = ExitStack()
            si.__enter__()
            ivt = [k.sb(si, "hivt%d" % i, [128, 32, 128], BF16) for i in range(2)]
            for o in range(2):
                if o == 1:
                    for _ in fwd(1):
                        pass
                for tt in range(16):
                    iv = ivt[tt % 2]
                    k.ld(iv[:], I["INV"][tt], iv)
                    pb = k.bank()
                    for m in range(32):
                        k.mm(pb[:], iv[:, m, :], yf[:, m, :], m == 0, m == 31, [iv, yf], [pb])
                    if o == 0:
                        k.tt(z[:, tt, :], pb[:], xg[0][:, tt, :], ALU.mult, [pb, xg[0]], [z])
                    else:
                        y_ = yo[tt % 2]
                        k.tt(y_[:], pb[:], xg[1][:, tt, :], ALU.mult, [pb, xg[1]], [y_])
                        k.sto(SC["YH"][tt * 128:(tt + 1) * 128, cb * 512:(cb + 1) * 512], y_[:], y_, [RS["YH"]])
            si.__exit__(None, None, None)
            sf.__exit__(None, None, None)
            k.S.barrier()
    k.S.barrier()
    with ExitStack() as st:
        bgh = k.sb(st, "bgh", [128, 1024])
        k.ld(bgh[:], I["rows1"][3:4, 1024:2048].broadcast_to([128, 1024]), bgh)
        tmps = norm_tmps(k, st, "hy")
        yv = [k.sb(st, "hyv%d" % i, [128, 1024]) for i in range(2)]
        mo = [k.sb(st, "hmo%d" % i, [128, 1024], BF16) for i in range(2)]
        for tt in range(16):
            y, m = yv[tt % 2], mo[tt % 2]
            k.ld(y[:], SC["YH"][tt * 128:(tt + 1) * 128, :], y, r=[RS["YH"]])
            rms_rows(k, tmps, y, 1024, bgh[:], m[:], m)
            k.sto(SC["MIX"][tt * 128:(tt + 1) * 128, 1024:2048], m[:], m, [RS["MIX"]])


def stage_out(k, env):
    I, SC, RS = env["I"], env["SC"], env["RS"]
    with ExitStack() as st:
        wo = k.sb(st, "wo", [128, NK, 2048], BF16)
        stg = [k.sb(st, "ostg%d" % i, [128, 4, 512]) for i in range(2)]
        load_cast_weight(k, I["w_out"], NK, wo, stg)
        g1row = k.sb(st, "g1row", [128, D])
        k.ld(g1row[:], SC["MOD"][0:1, 2 * D:3 * D].broadcast_to([128, D]), g1row, r=[RS["MOD"]])
        mx = [k.sb(st, "omx%d" % i, [128, D], BF16) for i in range(2)]
        mT = [k.sb(st, "omT%d" % i, [128, NK, 128], BF16) for i in range(2)]
        xt = [k.sb(st, "oxt%d" % i, [128, D]) for i in range(2)]
        x1 = [k.sb(st, "ox1%d" % i, [128, D]) for i in range(2)]
        srow2 = k.sb(st, "o_srow2", [128, D])
        shrow2 = k.sb(st, "o_shrow2", [128, D])
        n2 = k.sb(st, "o_n2", [128, D])
        k.ld(srow2[:], SC["MOD"][0:1, 4 * D:5 * D].broadcast_to([128, D]), srow2, r=[RS["MOD"]])
        k.ld(shrow2[:], SC["MOD"][0:1, 3 * D:4 * D].broadcast_to([128, D]), shrow2, r=[RS["MOD"]])
        k.ld(n2[:], I["rows1"][1:2, :].broadcast_to([128, D]), n2)
        k.stt(srow2[:], srow2[:], 1.0, n2[:], ALU.add, ALU.mult, [srow2, n2], [srow2])
        tmps2 = norm_tmps(k, st, "o2")
        hb2 = [k.sb(st, "o_hb%d" % i, [128, D], BF16) for i in range(2)]
        def prep(tt):
            m, mt, x = mx[tt % 2], mT[tt % 2], xt[tt % 2]
            k.ld(m[:], SC["MIX"][tt * 128:(tt + 1) * 128, :], m, r=[RS["MIX"]])
            k.ld(x[:], I["x"][tt * 128:(tt + 1) * 128, :], x)
            transpose_to(k, m, NK, mt, 0)
        prep(0)
        for tt in range(16):
            m, mt, x, o = mx[tt % 2], mT[tt % 2], xt[tt % 2], x1[tt % 2]
            if tt + 1 < 16:
                prep(tt + 1)
            for nb in range(4):
                pb = k.bank()
                for kc in range(NK):
                    k.mm(pb[:], mt[:, kc, :], wo[:, kc, nb * 512:(nb + 1) * 512], kc == 0, kc == NK - 1, [mt, wo], [pb])
                sl = slice(nb * 512, (nb + 1) * 512)
                k.tt(o[:, sl], pb[:], g1row[:, sl], ALU.mult, [pb, g1row], [o])
                k.tt(o[:, sl], o[:, sl], x[:, sl], ALU.add, [o, x], [o])
            k.sto(SC["X1"][tt * 128:(tt + 1) * 128, :], o[:], o, [RS["X1"]])
            h_ = hb2[tt % 2]
            norm_rows(k, tmps2, o, 128, srow2, shrow2, h_)
            k.sto(SC["H2"][tt * 128:(tt + 1) * 128, :], h_[:], h_, [RS["H2"]])


def stage_ffn(k, env):
    I, SC, RS, OUT = env["I"], env["SC"], env["RS"], env["OUT"]
    RO = Res("OUT")
    NT = T // 128
    sto_ = ExitStack()
    sto_.__enter__()
    wdb0 = k.sb(sto_, "f_wdb0", [128, NFC, 512], BF16)
    stg = [k.sb(sto_, "f_stg%d" % i, [128, 4, 512]) for i in range(2)]
    with ExitStack() as st:
        h2T = k.sb(st, "h2T", [128, NK, T], BF16)
        with ExitStack() as s0:
            hb = [k.sb(s0, "f_h%d" % i, [128, D], BF16) for i in range(2)]
            for tt in range(NT):
                h = hb[tt % 2]
                k.ld(h[:], SC["H2"][tt * 128:(tt + 1) * 128, :], h, r=[RS["H2"]])
                transpose_to(k, h, NK, h2T, tt * 128)
        k.S.barrier()
        with ExitStack() as sa:
            wgs = [k.sb(sa, "f_wgs%d" % i, [128, NK, 128]) for i in range(2)]
            wus = [k.sb(sa, "f_wus%d" % i, [128, NK, 128]) for i in range(2)]
            wgb = [k.sb(sa, "f_wgb%d" % i, [128, NK, 128], BF16) for i in range(2)]
            wub = [k.sb(sa, "f_wub%d" % i, [128, NK, 128], BF16) for i in range(2)]
            sil = [k.sb(sa, "f_sil%d" % i, [128, 512]) for i in range(2)]
            hc = [k.sb(sa, "f_hc%d" % i, [128, T], BF16) for i in range(2)]
            it = 0
            for fc in range(NFC):
                a, b, ab, bb, hcc = wgs[fc % 2], wus[fc % 2], wgb[fc % 2], wub[fc % 2], hc[fc % 2]
                k.ld(a[:], I["wg"][:, fc * 128:(fc + 1) * 128].rearrange("(kc p) n -> p kc n", p=128), a)
                k.ld(b[:], I["wu"][:, fc * 128:(fc + 1) * 128].rearrange("(kc p) n -> p kc n", p=128), b)
                k.cp(ab[:], a[:], [a], [ab], eng="scalar")
                k.cp(bb[:], b[:], [b], [bb], eng="vector")
                for tb in range(T // 512):
                    sl_ = sil[it % 2]
                    it += 1
                    pg = k.bank()
                    pu = k.bank()
                    ts_ = slice(tb * 512, (tb + 1) * 512)
                    for kc in range(NK):
                        k.mm(pg[:], ab[:, kc, :], h2T[:, kc, ts_], kc == 0, kc == NK - 1, [ab, h2T], [pg])
                    for kc in range(NK):
                        k.mm(pu[:], bb[:, kc, :], h2T[:, kc, ts_], kc == 0, kc == NK - 1, [bb, h2T], [pu])
                    k.act(sl_[:], pg[:], AF.Silu, [pg], [sl_])
                    k.tt(hcc[:, ts_], sl_[:], pu[:], ALU.mult, [sl_, pu], [hcc])
                k.sto(SC["HTS"][:, :, fc, :].rearrange("tt p t -> p tt t"), hcc[:].rearrange("p (tt t) -> p tt t", t=128), hcc, [RS["HTS"]])
                if fc == 16:
                    load_cast_weight(k, I["wd"], NFC, wdb0, stg, n0=0, ncols=512, c0=0)
    k.S.barrier()
    with ExitStack() as st:
        wdb = [wdb0, k.sb(st, "f_wdb1", [128, NFC, 512], BF16)]
        hts = [k.sb(st, "f_hts%d" % i, [128, NFC, 128], BF16) for i in range(2)]
        g2row = k.sb(st, "g2row", [128, D])
        k.ld(g2row[:], SC["MOD"][0:1, 5 * D:6 * D].broadcast_to([128, D]), g2row, r=[RS["MOD"]])
        x1p = [k.sb(st, "f_x1p%d" % i, [128, 512]) for i in range(2)]
        op_ = [k.sb(st, "f_op%d" % i, [128, 512]) for i in range(2)]
        finrow = k.sb(st, "finrow", [128, D])
        k.ld(finrow[:], I["rows1"][2:3, :].broadcast_to([128, D]), finrow)
        tmps = norm_tmps(k, st, "f1")
        xs = [k.sb(st, "f_x2%d" % i, [128, D]) for i in range(2)]
        os_ = [k.sb(st, "f_o%d" % i, [128, D]) for i in range(2)]
        it = 0
        for nb in range(4):
            wb_ = wdb[nb % 2]
            if nb + 1 < 4:
                load_cast_weight(k, I["wd"], NFC, wdb[(nb + 1) % 2], stg, n0=0, ncols=512, c0=(nb + 1) * 512)
            for tt in range(NT):
                h_, xp, o_ = hts[it % 2], x1p[it % 2], op_[it % 2]
                it += 1
                k.ld(h_[:], SC["HTS"][tt], h_, r=[RS["HTS"]])
                k.ld(xp[:], SC["X1"][tt * 128:(tt + 1) * 128, nb * 512:(nb + 1) * 512], xp, r=[RS["X1"]])
                pb = k.bank()
                for fc in range(NFC):
                    k.mm(pb[:], h_[:, fc, :], wb_[:, fc, :], fc == 0, fc == NFC - 1, [h_, wb_], [pb])
                k.tt(o_[:], pb[:], g2row[:, nb * 512:(nb + 1) * 512], ALU.mult, [pb, g2row], [o_])
                if nb < 3:
                    k.tt(o_[:], o_[:], xp[:], ALU.add, [o_, xp], [o_])
                    k.sto(SC["X2"][tt * 128:(tt + 1) * 128, nb * 512:(nb + 1) * 512], o_[:], o_, [RS["X2"]], tracked=True)
                else:
                    x, o = xs[tt % 2], os_[tt % 2]
                    k.ld(x[:, 0:1536], SC["X2"][tt * 128:(tt + 1) * 128, 0:1536], x, r=[RS["X2"]])
                    k.tt(x[:, 1536:2048], o_[:], xp[:], ALU.add, [o_, xp], [x])
                    rms_rows(k, tmps, x, D, finrow[:], o[:], o)
                    k.sto(OUT[tt * 128:(tt + 1) * 128, :], o[:], o, [RO])
    sto_.__exit__(None, None, None)


def kernel(**inputs):
    shared = prep_shared(inputs)
    x = np.asarray(inputs["x"], np.float32)
    c = np.asarray(inputs["c"], np.float32)
    ctx = np.asarray(inputs["ctx"], np.float32)
    c_ctx = np.asarray(inputs["c_ctx"], np.float32)
    ncores = 8
    nc = build_program()
    in_maps = []
    for b in range(ncores):
        m = dict(shared)
        m["x"] = np.ascontiguousarray(x[b])
        m["ctx"] = np.ascontiguousarray(ctx[b])
        cv = np.stack([c[b].reshape(NK, 128).T, c_ctx.reshape(NK, 128).T], axis=-1)
        m["cvec"] = np.ascontiguousarray(cv.astype(np.float32))
        in_maps.append(m)
    res = run_bass_kernel_spmd(nc, in_maps, core_ids=list(range(ncores)))
    out = np.stack([np.asarray(r["out"], np.float32) for r in res.results], axis=0)
    return out
```
